# Optimizing a Trainium2 kernel written in Bass

```python
import math
import jax, jax.numpy as jnp
from jax import lax
import numpy as np

D_MODEL = 1024
BATCH = 2
SEQ = 8192
DEPTH = 2

GRID_W = 64
ROPE_BASE = 10000.0
EPS = 1e-6
Q_BLOCK = 128

N_GROUPS = 4
GROUP_W = D_MODEL // N_GROUPS
MIX_W = N_GROUPS * GROUP_W

A_HEADS = 4
A_NOPE = 64
A_ROPE = 32
A_VDIM = GROUP_W // A_HEADS
A_Q_LORA = 192
A_KV_LORA = 128
A_COLS = A_Q_LORA + A_KV_LORA + A_ROPE

B_HEADS = 4
B_KV_HEADS = 2
B_HDIM = GROUP_W // B_HEADS
B_COLS = (B_HEADS + 2 * B_KV_HEADS) * B_HDIM

C_HEADS = 4
C_HDIM = GROUP_W // C_HEADS
C_NGROUPS = 2
C_STATE = 64
C_CHUNK = 128
C_XBC = GROUP_W + 2 * C_NGROUPS * C_STATE
C_COLS = GROUP_W + C_XBC + 2 * C_HEADS

D_HEADS = 4
D_HDIM = GROUP_W // D_HEADS
D_CHUNK = 64
D_QKV = 3 * GROUP_W
D_COLS = D_QKV + GROUP_W + 4 * D_HEADS

IN_COLS = A_COLS + B_COLS + C_COLS + D_COLS
CONV_W = 3

D_FF = 2816
FFN_CONV_W = 3

kernel_name = "bidir_hybrid_parallel_heads_mla_gqa_ssd_deltanet"


def split_last(t, sizes):
    return jnp.split(t, [int(s) for s in np.cumsum(sizes)[:-1]], axis=-1)


def rms_norm(x, w, eps=EPS):
    x32 = x.astype(jnp.float32)
    y = x32 * lax.rsqrt(jnp.mean(x32 * x32, axis=-1, keepdims=True) + eps)
    return (y * w.astype(jnp.float32)).astype(x.dtype)


def l2_normalize(x, eps=1e-6):
    x32 = x.astype(jnp.float32)
    return (x32 * lax.rsqrt(jnp.sum(x32 * x32, axis=-1, keepdims=True) + eps)).astype(x.dtype)


def dwconv_centred(x, w, b=None):
    K = w.shape[0]
    L = x.shape[1]
    pad = K // 2
    xp = jnp.pad(x, ((0, 0), (pad, pad), (0, 0)))
    y = sum(xp[:, k:k + L] * w[k] for k in range(K))
    return y if b is None else y + b


def axial_rope_tables(seq_len, rot_dim):
    rows = seq_len // GRID_W
    row = jnp.repeat(jnp.arange(rows), GRID_W).astype(jnp.float32)
    col = jnp.tile(jnp.arange(GRID_W), rows).astype(jnp.float32)
    sec = rot_dim // 2
    inv_freq = ROPE_BASE ** (-jnp.arange(0, sec, 2, dtype=jnp.float32) / sec)
    ang_r = row[:, None] * inv_freq
    ang_c = col[:, None] * inv_freq
    ang = jnp.concatenate([ang_r, ang_r, ang_c, ang_c], axis=-1)
    return jnp.cos(ang), jnp.sin(ang)


def apply_axial_rope(x, cos, sin):
    r = x.shape[-1]
    xs = x.reshape(x.shape[:-1] + (2, 2, r // 4))
    rot = jnp.stack([-xs[..., 1, :], xs[..., 0, :]], axis=-2).reshape(x.shape)
    return (x * cos + rot * sin).astype(x.dtype)


def mla_attention(q_nope, q_rope, k_nope, k_rope, v):
    Bsz, L, H, dn = q_nope.shape
    dr = q_rope.shape[-1]
    nb = L // Q_BLOCK
    scale = (dn + dr) ** -0.5
    qn_b = q_nope.reshape(Bsz, nb, Q_BLOCK, H, dn).swapaxes(0, 1)
    qr_b = q_rope.reshape(Bsz, nb, Q_BLOCK, H, dr).swapaxes(0, 1)

    def block(qs):
        qn, qr = qs
        s = jnp.einsum('bqhd,bkhd->bhqk', qn, k_nope) + jnp.einsum('bqhr,bkr->bhqk', qr, k_rope)
        p = jax.nn.softmax(s.astype(jnp.float32) * scale, axis=-1).astype(v.dtype)
        return jnp.einsum('bhqk,bkhd->bqhd', p, v)

    o = lax.map(block, (qn_b, qr_b))
    return o.swapaxes(0, 1).reshape(Bsz, L, H * v.shape[-1])


def gqa_attention(q, k, v):
    Bsz, L, Hq, hd = q.shape
    Hkv = k.shape[2]
    rep = Hq // Hkv
    nb = L // Q_BLOCK
    qb = q.reshape(Bsz, nb, Q_BLOCK, Hkv, rep, hd).swapaxes(0, 1)
    scale = hd ** -0.5

    def block(qi):
        s = jnp.einsum('bqgrd,bkgd->bgrqk', qi, k)
        p = jax.nn.softmax(s.astype(jnp.float32) * scale, axis=-1).astype(v.dtype)
        return jnp.einsum('bgrqk,bkgd->bqgrd', p, v)

    o = lax.map(block, qb)
    return o.swapaxes(0, 1).reshape(Bsz, L, Hq * hd)


def ssd_scan(x, dt, A, Bm, Cm):
    Bsz, L, H, P = x.shape
    N = Bm.shape[-1]
    Q = C_CHUNK
    nc = L // Q
    xc = (x * dt[..., None]).reshape(Bsz, nc, Q, H, P)
    Bc = Bm.reshape(Bsz, nc, Q, H, N)
    Cc = Cm.reshape(Bsz, nc, Q, H, N)
    acum = jnp.cumsum((dt * A).reshape(Bsz, nc, Q, H).transpose(0, 1, 3, 2), axis=-1)
    idx = jnp.arange(Q)
    causal = idx[:, None] >= idx[None, :]
    seg = acum[..., :, None] - acum[..., None, :]
    decay = jnp.exp(jnp.where(causal, seg, -jnp.inf))
    scores = jnp.einsum('bcihn,bcjhn->bchij', Cc, Bc) * decay
    y_diag = jnp.einsum('bchij,bcjhp->bcihp', scores, xc)
    decay_to_end = jnp.exp(acum[..., -1:] - acum)
    states = jnp.einsum('bchj,bcjhn,bcjhp->bchpn', decay_to_end, Bc, xc)
    chunk_decay = jnp.exp(acum[..., -1])

    def step(S, inp):
        st, dec = inp
        return S * dec[..., None, None] + st, S

    S0 = jnp.zeros((Bsz, H, P, N), x.dtype)
    _, S_in = lax.scan(step, S0, (states.swapaxes(0, 1), chunk_decay.swapaxes(0, 1)))
    S_in = S_in.swapaxes(0, 1)
    y_off = jnp.einsum('bcihn,bchpn,bchi->bcihp', Cc, S_in, jnp.exp(acum))
    return (y_diag + y_off).reshape(Bsz, L, H, P)


def gated_delta_rule(q, k, v, g, beta):
    Bsz, L, H, dk = q.shape
    dv = v.shape[-1]
    Q = D_CHUNK
    nc = L // Q

    def chunks(t):
        return t.reshape(Bsz, nc, Q, H, -1).transpose(0, 1, 3, 2, 4)

    qc = chunks(q) * dk ** -0.5
    kc = chunks(k)
    vc = chunks(v)
    bc = beta.reshape(Bsz, nc, Q, H).transpose(0, 1, 3, 2)
    G = jnp.cumsum(g.reshape(Bsz, nc, Q, H).transpose(0, 1, 3, 2), axis=-1)
    idx = jnp.arange(Q)
    lower_strict = idx[:, None] > idx[None, :]
    lower_incl = idx[:, None] >= idx[None, :]
    seg = G[..., :, None] - G[..., None, :]
    decay = jnp.exp(jnp.where(lower_incl, seg, -jnp.inf))
    kb = kc * bc[..., None]
    Lmat = jnp.where(lower_strict, jnp.einsum('bchid,bchjd->bchij', kb, kc) * decay, 0.0)
    eye = jnp.eye(Q, dtype=Lmat.dtype)
    T = lax.linalg.triangular_solve(Lmat + eye, jnp.broadcast_to(eye, Lmat.shape),
                                    left_side=True, lower=True, unit_diagonal=True)
    u_base = jnp.einsum('bchij,bchjd->bchid', T, vc * bc[..., None])
    w = jnp.einsum('bchij,bchjd->bchid', T, kb * jnp.exp(G)[..., None])
    qk = jnp.einsum('bchid,bchjd->bchij', qc, kc) * decay
    q_dec = qc * jnp.exp(G)[..., None]
    k_dec = kc * jnp.exp(G[..., -1:] - G)[..., None]
    g_end = jnp.exp(G[..., -1])

    def step(S, inp):
        u_c, w_c, qk_c, qd_c, kd_c, ge_c = inp
        v_new = u_c - jnp.einsum('bhid,bhdv->bhiv', w_c, S)
        o = jnp.einsum('bhid,bhdv->bhiv', qd_c, S) + jnp.einsum('bhij,bhjv->bhiv', qk_c, v_new)
        S = S * ge_c[..., None, None] + jnp.einsum('bhjd,bhjv->bhdv', kd_c, v_new)
        return S, o

    xs = tuple(t.swapaxes(0, 1) for t in (u_base, w, qk, q_dec, k_dec, g_end))
    S0 = jnp.zeros((Bsz, H, dk, dv), q.dtype)
    _, o = lax.scan(step, S0, xs)
    return o.transpose(1, 0, 3, 2, 4).reshape(Bsz, L, H, dv)


def mla_mixer(p, q_norm, w_uq, kv_norm, w_ukv, out_norm, cos, sin):
    Bsz, L, _ = p.shape
    cq, ckv, kr = split_last(p, [A_Q_LORA, A_KV_LORA, A_ROPE])
    q = (rms_norm(cq, q_norm) @ w_uq).reshape(Bsz, L, A_HEADS, A_NOPE + A_ROPE)
    kv = (rms_norm(ckv, kv_norm) @ w_ukv).reshape(Bsz, L, A_HEADS, A_NOPE + A_VDIM)
    q_nope, q_rope = q[..., :A_NOPE], q[..., A_NOPE:]
    k_nope, v = kv[..., :A_NOPE], kv[..., A_NOPE:]
    q_rope = apply_axial_rope(q_rope, cos[:, None, :], sin[:, None, :])
    k_rope = apply_axial_rope(kr, cos, sin)
    o = mla_attention(q_nope, q_rope, k_nope, k_rope, v)
    return rms_norm(o, out_norm)


def gqa_mixer(p, q_norm, k_norm, out_norm, cos, sin):
    Bsz, L, _ = p.shape
    q, k, v = split_last(p, [B_HEADS * B_HDIM, B_KV_HEADS * B_HDIM, B_KV_HEADS * B_HDIM])
    q = rms_norm(q.reshape(Bsz, L, B_HEADS, B_HDIM), q_norm)
    k = rms_norm(k.reshape(Bsz, L, B_KV_HEADS, B_HDIM), k_norm)
    v = v.reshape(Bsz, L, B_KV_HEADS, B_HDIM)
    q = apply_axial_rope(q, cos[:, None, :], sin[:, None, :])
    k = apply_axial_rope(k, cos[:, None, :], sin[:, None, :])
    return rms_norm(gqa_attention(q, k, v), out_norm)


def mamba2_mixer(p, conv_w, conv_b, a_log, dt_bias, d_skip, out_norm):
    Bsz, L, _ = p.shape
    f32 = jnp.float32
    z, xbc, dt_raw = split_last(p, [GROUP_W, C_XBC, 2 * C_HEADS])
    xbc = jax.nn.silu(dwconv_centred(xbc, conv_w, conv_b))
    xs, Bm, Cm = split_last(xbc, [GROUP_W, C_NGROUPS * C_STATE, C_NGROUPS * C_STATE])
    rep = C_HEADS // C_NGROUPS
    xs = xs.reshape(Bsz, L, C_HEADS, C_HDIM).astype(f32)
    Bm = jnp.repeat(Bm.reshape(Bsz, L, C_NGROUPS, C_STATE), rep, axis=2).astype(f32)
    Cm = jnp.repeat(Cm.reshape(Bsz, L, C_NGROUPS, C_STATE), rep, axis=2).astype(f32)
    dt = jax.nn.softplus((dt_raw.reshape(Bsz, L, 2, C_HEADS) + dt_bias).astype(f32))
    A = -jnp.exp(a_log.astype(f32))
    fl = lambda t: jnp.flip(t, axis=1)
    y_f = ssd_scan(xs, dt[:, :, 0], A[0], Bm, Cm)
    y_b = fl(ssd_scan(fl(xs), fl(dt[:, :, 1]), A[1], fl(Bm), fl(Cm)))
    y = y_f + y_b + xs * d_skip.astype(f32)[:, None]
    y = y.reshape(Bsz, L, GROUP_W).astype(p.dtype)
    return rms_norm(y * jax.nn.silu(z), out_norm)


def deltanet_mixer(p, conv_w, a_log, dt_bias, out_norm):
    Bsz, L, _ = p.shape
    f32 = jnp.float32
    qkv, z, ab = split_last(p, [D_QKV, GROUP_W, 4 * D_HEADS])
    qkv = jax.nn.silu(dwconv_centred(qkv, conv_w))
    q, k, v = split_last(qkv, [GROUP_W, GROUP_W, GROUP_W])
    q = l2_normalize(q.reshape(Bsz, L, D_HEADS, D_HDIM)).astype(f32)
    k = l2_normalize(k.reshape(Bsz, L, D_HEADS, D_HDIM)).astype(f32)
    v = v.reshape(Bsz, L, D_HEADS, D_HDIM).astype(f32)
    ab = ab.reshape(Bsz, L, 4, D_HEADS).astype(f32)
    beta = jax.nn.sigmoid(ab[:, :, 0:2])
    g = -jnp.exp(a_log.astype(f32)) * jax.nn.softplus(ab[:, :, 2:4] + dt_bias.astype(f32))
    fl = lambda t: jnp.flip(t, axis=1)
    o_f = gated_delta_rule(q, k, v, g[:, :, 0], beta[:, :, 0])
    o_b = fl(gated_delta_rule(fl(q), fl(k), fl(v), fl(g[:, :, 1]), fl(beta[:, :, 1])))
    o = rms_norm((o_f + o_b).astype(p.dtype), out_norm)
    o = o * jax.nn.silu(z.reshape(Bsz, L, D_HEADS, D_HDIM))
    return o.reshape(Bsz, L, GROUP_W)


def conv_ffn(h, w_in, conv_w, conv_b, w_out):
    gu = dwconv_centred(h @ w_in, conv_w, conv_b)
    gate, up = split_last(gu, [D_FF, D_FF])
    return (jax.nn.silu(gate) * up) @ w_out


def setup_inputs(seed: int = 0) -> dict:
    key = jax.random.key(seed)
    ks = iter(jax.random.split(key, 40))
    nrm = lambda shape, scale: jax.random.normal(next(ks), shape, jnp.float32) * scale
    gain = lambda shape: 1.0 + nrm(shape, 0.05)
    log_a = lambda h: jnp.log(jax.random.uniform(next(ks), (DEPTH, 2, h), jnp.float32, 1.0, 16.0))

    def dt_bias(h):
        dt = jnp.exp(jax.random.uniform(next(ks), (DEPTH, 2, h), jnp.float32, math.log(1e-3), math.log(1e-1)))
        return dt + jnp.log(-jnp.expm1(-dt))

    return {
        "x": nrm((BATCH, SEQ, D_MODEL), 1.0),
        "pre_mix_norm": gain((DEPTH, D_MODEL)),
        "w_in": nrm((DEPTH, D_MODEL, IN_COLS), D_MODEL ** -0.5),
        "a_q_norm": gain((DEPTH, A_Q_LORA)),
        "a_w_uq": nrm((DEPTH, A_Q_LORA, A_HEADS * (A_NOPE + A_ROPE)), A_Q_LORA ** -0.5),
        "a_kv_norm": gain((DEPTH, A_KV_LORA)),
        "a_w_ukv": nrm((DEPTH, A_KV_LORA, A_HEADS * (A_NOPE + A_VDIM)), A_KV_LORA ** -0.5),
        "a_out_norm": gain((DEPTH, GROUP_W)),
        "b_q_norm": gain((DEPTH, B_HDIM)),
        "b_k_norm": gain((DEPTH, B_HDIM)),
        "b_out_norm": gain((DEPTH, GROUP_W)),
        "c_conv_w": nrm((DEPTH, CONV_W, C_XBC), CONV_W ** -0.5),
        "c_conv_b": nrm((DEPTH, C_XBC), 0.02),
        "c_a_log": log_a(C_HEADS),
        "c_dt_bias": dt_bias(C_HEADS),
        "c_d_skip": 1.0 + nrm((DEPTH, C_HEADS), 0.1),
        "c_out_norm": gain((DEPTH, GROUP_W)),
        "d_conv_w": nrm((DEPTH, CONV_W, D_QKV), CONV_W ** -0.5),
        "d_a_log": log_a(D_HEADS),
        "d_dt_bias": dt_bias(D_HEADS),
        "d_out_norm": gain((DEPTH, D_HDIM)),
        "w_out": nrm((DEPTH, MIX_W, D_MODEL), MIX_W ** -0.5),
        "post_mix_norm": gain((DEPTH, D_MODEL)),
        "pre_ffn_norm": gain((DEPTH, D_MODEL)),
        "f_w_in": nrm((DEPTH, D_MODEL, 2 * D_FF), D_MODEL ** -0.5),
        "f_conv_w": nrm((DEPTH, FFN_CONV_W, 2 * D_FF), FFN_CONV_W ** -0.5),
        "f_conv_b": nrm((DEPTH, 2 * D_FF), 0.02),
        "f_w_out": nrm((DEPTH, D_FF, D_MODEL), D_FF ** -0.5),
        "post_ffn_norm": gain((DEPTH, D_MODEL)),
    }


def reference(x, pre_mix_norm, w_in, a_q_norm, a_w_uq, a_kv_norm, a_w_ukv, a_out_norm,
              b_q_norm, b_k_norm, b_out_norm, c_conv_w, c_conv_b, c_a_log, c_dt_bias, c_d_skip,
              c_out_norm, d_conv_w, d_a_log, d_dt_bias, d_out_norm, w_out, post_mix_norm,
              pre_ffn_norm, f_w_in, f_conv_w, f_conv_b, f_w_out, post_ffn_norm):
    L = x.shape[1]
    cos_a, sin_a = axial_rope_tables(L, A_ROPE)
    cos_b, sin_b = axial_rope_tables(L, B_HDIM)
    for l in range(DEPTH):
        h = rms_norm(x, pre_mix_norm[l])
        p = h @ w_in[l]
        pa, pb, pc, pd = split_last(p, [A_COLS, B_COLS, C_COLS, D_COLS])
        o_a = mla_mixer(pa, a_q_norm[l], a_w_uq[l], a_kv_norm[l], a_w_ukv[l], a_out_norm[l], cos_a, sin_a)
        o_b = gqa_mixer(pb, b_q_norm[l], b_k_norm[l], b_out_norm[l], cos_b, sin_b)
        o_c = mamba2_mixer(pc, c_conv_w[l], c_conv_b[l], c_a_log[l], c_dt_bias[l], c_d_skip[l], c_out_norm[l])
        o_d = deltanet_mixer(pd, d_conv_w[l], d_a_log[l], d_dt_bias[l], d_out_norm[l])
        o = jnp.concatenate([o_a, o_b, o_c, o_d], axis=-1)
        x = x + rms_norm(o @ w_out[l], post_mix_norm[l])
        h = rms_norm(x, pre_ffn_norm[l])
        x = x + rms_norm(conv_ffn(h, f_w_in[l], f_conv_w[l], f_conv_b[l], f_w_out[l]), post_ffn_norm[l])
    return x
```

```python
import contextlib
import numpy as np
import ml_dtypes
import concourse.bass as bass
import concourse.mybir as mybir
from concourse.bass_utils import run_bass_kernel_spmd

F32 = mybir.dt.float32
BF16 = mybir.dt.bfloat16
ALU = mybir.AluOpType
AF = mybir.ActivationFunctionType

D_MODEL = 1024
SEQ = 8192
DEPTH = 2
EPS = 1e-6
D_FF = 2816
NCORES = 8
NTOK = 2048

ENGS = ("pe", "act", "dve", "pool", "sp")
N_DMA_SEMS = 12


class Op:
    __slots__ = ("eng", "fn", "deps", "signal", "semkey", "semval", "is_dma", "dma_slot")

    def __init__(self, eng, fn, is_dma=False):
        self.eng = eng
        self.fn = fn
        self.deps = []
        self.signal = False
        self.semkey = None
        self.semval = None
        self.is_dma = is_dma
        self.dma_slot = None


class Sched:
    def __init__(self, nc):
        self.nc = nc
        self.ops = {e: [] for e in ENGS}
        self.writers = {}
        self.readers = {}
        self.dma_count = {e: 0 for e in ENGS}
        self.dma_last = {}
        self.last_op = {}

    def add(self, eng, fn, reads=(), writes=(), is_dma=False):
        op = Op(eng, fn, is_dma)
        if is_dma:
            n = self.dma_count[eng]
            self.dma_count[eng] = n + 1
            op.dma_slot = n % N_DMA_SEMS
            prev = self.dma_last.get((eng, op.dma_slot))
            if prev is not None:
                op.deps.append(prev)
            self.dma_last[(eng, op.dma_slot)] = op
        deps = op.deps
        for k in reads:
            deps.extend(self.writers.get(k, {}).values())
        for k in writes:
            deps.extend(self.writers.get(k, {}).values())
            deps.extend(self.readers.get(k, {}).values())
        tk = (eng, op.dma_slot) if is_dma else eng
        for k in reads:
            self.readers.setdefault(k, {})[tk] = op
        for k in writes:
            self.writers.setdefault(k, {})[tk] = op
            self.readers[k] = {}
        if eng == "pe" and not is_dma:
            op.deps = [d for d in deps if not (d.eng == "pe" and not d.is_dma)]
        self.ops[eng].append(op)
        self.last_op[tk] = op
        return op

    def barrier(self):
        lasts = list(self.last_op.values())
        for e in ENGS:
            if e == "sp" or self.ops[e]:
                op = Op(e, None)
                op.deps = [d for d in lasts]
                self.ops[e].append(op)
                self.last_op[e] = op
        self.writers = {}
        self.readers = {}

    def finalize(self, final_waits=()):
        nc = self.nc
        for e in ENGS:
            for op in self.ops[e]:
                for d in op.deps:
                    d.signal = True
        for op in final_waits:
            op.signal = True
        with contextlib.ExitStack() as st:
            esem = {e: st.enter_context(nc.semaphore(f"s_{e}")) for e in ENGS}
            dsem = {}
            for e in ENGS:
                for i in range(min(N_DMA_SEMS, self.dma_count[e])):
                    dsem[(e, i)] = st.enter_context(nc.semaphore(f"d_{e}{i}"))
            for e in ENGS:
                c = 0
                dc = {}
                for op in self.ops[e]:
                    if op.is_dma:
                        k = (e, op.dma_slot)
                        dc[k] = dc.get(k, 0) + 16
                        op.semkey = ("d", k)
                        op.semval = dc[k]
                    elif op.signal and op.fn is not None:
                        c += 1
                        op.semkey = ("e", e)
                        op.semval = c
                    elif op.signal:
                        c += 1
                        op.semkey = ("e", e)
                        op.semval = c
            block = st.enter_context(nc.Block())

            def semof(key):
                return esem[key[1]] if key[0] == "e" else dsem[key[1]]

            def run(e, eng):
                waited = {}
                for op in self.ops[e]:
                    need = {}
                    for d in op.deps:
                        if d.semkey is None:
                            continue
                        if waited.get(d.semkey, 0) >= d.semval:
                            continue
                        if need.get(d.semkey, 0) < d.semval:
                            need[d.semkey] = d.semval
                    for k, v in need.items():
                        eng.wait_ge(semof(k), v)
                        waited[k] = v
                    if op.fn is None:
                        if op.signal:
                            eng.sem_inc(semof(op.semkey), 1)
                        continue
                    ins = op.fn(eng)
                    if op.is_dma:
                        ins.then_inc(semof(op.semkey), 16)
                    elif op.signal:
                        ins.then_inc(semof(op.semkey), 1)
                if e == "sp":
                    for op in final_waits:
                        if waited.get(op.semkey, 0) < op.semval:
                            eng.wait_ge(semof(op.semkey), op.semval)
                            waited[op.semkey] = op.semval

            @block.sync
            def _(eng):
                run("sp", eng)

            @block.scalar
            def _(eng):
                run("act", eng)

            @block.vector
            def _(eng):
                run("dve", eng)

            @block.gpsimd
            def _(eng):
                run("pool", eng)

            @block.tensor
            def _(eng):
                run("pe", eng)


class Ctx:
    def __init__(self):
        self.nc = bass.Bass("TRN2", target_bir_lowering=False)
        self.S = Sched(self.nc)
        self.st = contextlib.ExitStack()
        self.finals = []
        self.ps_banks = None
        self._n = 0
        self.io = {}
        self.prefix = ""
        self.phase_st = None

    def begin_phase(self, prefix, io):
        self.prefix = prefix
        self.io = dict(io)
        self.phase_st = contextlib.ExitStack()

    def end_phase(self):
        self.S.barrier()
        self.phase_st.close()
        self.phase_st = None
        self.io = {}

    def din(self, name, shape, dt):
        if name in self.io:
            ap = self.io[name]
            assert list(ap.shape) == list(shape), (name, ap.shape, shape)
            return ap
        return self.nc.dram_tensor(self.prefix + name, list(shape), dt, kind="ExternalInput").ap()

    def dout(self, name, shape, dt):
        if name in self.io:
            ap = self.io[name]
            assert list(ap.shape) == list(shape), (name, ap.shape, shape)
            return ap
        return self.nc.dram_tensor(self.prefix + name, list(shape), dt, kind="ExternalOutput").ap()

    def scratch(self, name, shape, dt):
        return self.nc.dram_tensor(name, list(shape), dt, kind="Internal").ap()

    def sb(self, name, shape, dt, st=None):
        return (st or self.phase_st or self.st).enter_context(self.nc.sbuf_tensor(self.prefix + name, list(shape), dt))

    def psum_banks(self):
        if self.ps_banks is None:
            self.ps_banks = [self.st.enter_context(self.nc.psum_tensor(f"psb{i}", [128, 512], F32)) for i in range(8)]
        return self.ps_banks

    def dma(self, q, out, in_, r=(), w=(), final=False, slow=False):
        if slow:
            op = self.S.add(q, lambda e: e.dma_start(out=out, in_=in_, allow_slow_non_contiguous=True), r, w, is_dma=True)
        else:
            op = self.S.add(q, lambda e: e.dma_start(out=out, in_=in_), r, w, is_dma=True)
        if final:
            self.finals.append(op)
        return op

    def mm(self, out, lhsT, rhs, start, stop, r=(), w=()):
        return self.S.add("pe", lambda e: e.matmul(out, lhsT=lhsT, rhs=rhs, start=start, stop=stop), r, w)

    def tr(self, out, in_, ident, r=(), w=()):
        return self.S.add("pe", lambda e: e.transpose(out, in_, ident), r, w)

    def act(self, out, in_, func, r=(), w=(), scale=1.0, bias=0.0):
        return self.S.add("act", lambda e: e.activation(out=out, in_=in_, func=func, bias=bias, scale=scale), r, w)

    def ts(self, eng, out, in0, s1, s2, op0, op1, r=(), w=()):
        if s2 is None:
            return self.S.add(eng, lambda e: e.tensor_single_scalar(out=out, in_=in0, scalar=s1, op=op0), r, w)
        return self.S.add(eng, lambda e: e.tensor_scalar(out=out, in0=in0, scalar1=s1, scalar2=s2, op0=op0, op1=op1), r, w)

    def stt(self, eng, out, in0, scalar, in1, op0, op1, r=(), w=()):
        return self.S.add(eng, lambda e: e.scalar_tensor_tensor(out=out, in0=in0, scalar=scalar, in1=in1, op0=op0, op1=op1), r, w)

    def tt(self, eng, out, in0, in1, op, r=(), w=()):
        return self.S.add(eng, lambda e: e.tensor_tensor(out=out, in0=in0, in1=in1, op=op), r, w)

    def cp(self, eng, out, in_, r=(), w=()):
        if eng == "act":
            return self.S.add(eng, lambda e: e.copy(out=out, in_=in_), r, w)
        return self.S.add(eng, lambda e: e.tensor_copy(out=out, in_=in_), r, w)

    def memset(self, eng, ap, val, w=()):
        return self.S.add(eng, lambda e: e.memset(ap, val), (), w)

    def recip(self, out, in_, r=(), w=()):
        return self.S.add("dve", lambda e: e.reciprocal(out=out, in_=in_), r, w)

    def finish(self):
        self.S.finalize(final_waits=self.finals)
        self.st.close()
        return self.nc

    def uid(self, p):
        self._n += 1
        return f"{p}{self._n}"


def rstd_from_ss(c, out_sb, ss_ps, inv_d, eps, r, w, rows=slice(0, 128)):
    c.act(out_sb[rows], ss_ps[rows], AF.Ln, r=r, w=w, scale=inv_d, bias=eps)
    c.act(out_sb[rows], out_sb[rows], AF.Exp, r=w, w=w, scale=-0.5)


def pk(v):
    v = np.asarray(v, np.float32)
    return np.ascontiguousarray(v.reshape(-1, 128).T)


def bf(a):
    return np.asarray(a).astype(ml_dtypes.bfloat16)


def run(nc, in_maps):
    res = run_bass_kernel_spmd(nc, in_maps, core_ids=list(range(len(in_maps))))
    return res.results


def emit_norm_block(c, tag, src, nch, n, wt, ones_bf, ssps, sq, rstd, dst, inv_d, slot, pre_keys, eng_sq=("act", "pool")):
    ksq, krs = f"{tag}sq{slot}", f"{tag}rstd{slot}"
    for k in range(nch):
        e = eng_sq[k % len(eng_sq)]
        if e == "act":
            c.act(sq[:, k, :n], src[:, k, :n], AF.Square, r=pre_keys, w=[ksq])
        else:
            c.tt(e, sq[:, k, :n], src[:, k, :n], src[:, k, :n], ALU.mult, r=pre_keys, w=[ksq])
    for k in range(nch):
        c.mm(ssps[:, :n], ones_bf[:, :], sq[:, k, :n], k == 0, k == nch - 1, r=[ksq, "const"], w=[f"{tag}ssps{slot}"])
    rstd_from_ss(c, rstd[:, :n], ssps[:, :n], inv_d, EPS, r=[f"{tag}ssps{slot}"], w=[krs])
    return krs


def emit_P(c, ncols=NTOK):
    xt = c.din("xt", [D_MODEL, ncols], F32).rearrange("(k p) t -> p k t", p=128)
    w = c.din("w", [128, 8], F32)
    xnt = c.dout("xnt", [D_MODEL, ncols], BF16).rearrange("(k p) t -> p k t", p=128)
    ps = c.psum_banks()
    wt = c.sb("wt", [128, 8], F32)
    ones = c.sb("ones", [128, 128], BF16)
    c.memset("pool", ones[:, :], 1.0, w=["const"])
    c.dma("sp", wt[:, :], w, w=["const"])
    NB = 2
    xs = [c.sb(f"xs{i}", [128, 8, 512], F32) for i in range(NB)]
    sq = [c.sb(f"sq{i}", [128, 8, 512], BF16) for i in range(NB)]
    rs = [c.sb(f"rs{i}", [128, 512], F32) for i in range(NB)]
    xo = [c.sb(f"xo{i}", [128, 8, 512], BF16) for i in range(NB)]
    for b in range(ncols // 512):
        s = b % NB
        cols = slice(b * 512, (b + 1) * 512)
        c.dma("sp", xs[s][:, :, :], xt[:, :, cols], w=[f"xs{s}"])
        krs = emit_norm_block(c, "p", xs[s], 8, 512, wt, ones, ps[s], sq[s], rs[s], None, 1.0 / D_MODEL, s, [f"xs{s}"], eng_sq=("act",))
        for k in range(8):
            c.stt("dve", xo[s][:, k, :], xs[s][:, k, :], wt[:, k:k + 1], rs[s][:, :], ALU.mult, ALU.mult,
                  r=[f"xs{s}", krs, "const"], w=[f"xo{s}"])
        c.dma("sp", xnt[:, :, cols], xo[s][:, :, :], r=[f"xo{s}"], final=True)


def emit_Ta(c, ncols=NTOK):
    ot = c.din("ot", [D_MODEL, ncols], BF16).rearrange("(k p) t -> p k t", p=128)
    xt = c.din("xt", [D_MODEL, ncols], F32).rearrange("(k p) t -> p k t", p=128)
    wout = c.din("wout", [D_MODEL, D_MODEL], F32).rearrange("(k p) n -> p k n", p=128)
    onw_d = c.din("onw", [128, 8], F32)
    postw_d = c.din("postw", [128, 8], F32)
    prew_d = c.din("prew", [128, 8], F32)
    xmt = c.dout("xmt", [D_MODEL, ncols], F32).rearrange("(k p) t -> p k t", p=128)
    ht = c.dout("ht", [D_MODEL, ncols], BF16).rearrange("(k p) t -> p k t", p=128)
    ps = c.psum_banks()
    onw = c.sb("onw_s", [128, 8], F32)
    postw = c.sb("postw_s", [128, 8], F32)
    prew = c.sb("prew_s", [128, 8], F32)
    ones = c.sb("ones", [128, 128], BF16)
    bd2 = c.sb("bd2", [128, 128], BF16)
    top = c.sb("top", [128, 128], BF16)
    c.memset("pool", ones[:, :], 1.0, w=["const"])
    c.memset("pool", bd2[:, :], 0.0, w=["const"])
    c.memset("pool", bd2[0:64, 0:64], 1.0, w=["const"])
    c.memset("pool", bd2[64:128, 64:128], 1.0, w=["const"])
    c.memset("pool", top[:, :], 0.0, w=["const"])
    c.memset("pool", top[0:64, :], 1.0, w=["const"])
    for t, d in ((onw, onw_d), (postw, postw_d), (prew, prew_d)):
        c.dma("sp", t[:, :], d, w=["const"])
    wo = c.sb("wo", [128, 8, D_MODEL], BF16)
    stg = [c.sb(f"stg{i}", [128, 2, D_MODEL], F32) for i in range(2)]
    for i in range(4):
        s = i % 2
        c.dma("pool", stg[s][:, :, :], wout[:, 2 * i:2 * i + 2, :], w=[f"stg{s}"])
        c.cp("pool" if i % 2 else "act", wo[:, 2 * i:2 * i + 2, :], stg[s][:, :, :], r=[f"stg{s}"], w=["wo"])
    B = []
    for p in range(2):
        B.append(dict(
            ots=c.sb(f"ots{p}", [128, 8, 512], BF16), xs=c.sb(f"xs{p}", [128, 8, 512], F32), sq=c.sb(f"sq{p}", [128, 8, 512], BF16),
            on=c.sb(f"on{p}", [128, 8, 512], BF16), ys=c.sb(f"ys{p}", [128, 8, 512], F32), hs=c.sb(f"hs{p}", [128, 8, 512], BF16),
            rsAB=c.sb(f"rsAB{p}", [128, 512], F32), rsC=c.sb(f"rsC{p}", [128, 512], F32), rs2=c.sb(f"rs2{p}", [128, 512], F32),
            rs3=c.sb(f"rs3{p}", [128, 512], F32)))
    for b in range(ncols // 512):
        p = b % 2
        T = B[p]
        ots, xs, sq, on, ys, hs = T["ots"], T["xs"], T["sq"], T["on"], T["ys"], T["hs"]
        rsAB, rsC, rs2, rs3 = T["rsAB"], T["rsC"], T["rs2"], T["rs3"]
        K_ = lambda n: f"{n}{p}"
        pA, pB, pO = ps[4 * p], ps[4 * p + 1], (ps[4 * p + 2], ps[4 * p + 3])
        kA, kB, kO = f"bk{p}a", f"bk{p}b", (f"bk{p}c", f"bk{p}d")
        cols = slice(b * 512, (b + 1) * 512)
        c.dma("sp", ots[:, :, :], ot[:, :, cols], w=[K_("ots")])
        c.dma("sp", xs[:, :, :], xt[:, :, cols], w=[K_("xs")])
        for k in range(8):
            c.act(sq[:, k, :], ots[:, k, :], AF.Square, r=[K_("ots")], w=[K_("sq")])
        for i, k in enumerate((0, 2, 4, 6)):
            c.mm(pA[:, :], bd2[:, :], sq[:, k, :], i == 0, i == 3, r=[K_("sq"), "const"], w=[kA])
        for i, k in enumerate((1, 3, 5, 7)):
            c.mm(pB[:, :], top[:, :], sq[:, k, :], i == 0, i == 3, r=[K_("sq"), "const"], w=[kB])
        rstd_from_ss(c, rsAB, pA, 1.0 / 256, EPS, r=[kA], w=[kA, K_("rsAB")])
        rstd_from_ss(c, rsC, pB, 1.0 / 256, EPS, r=[kB], w=[kB, K_("rsC")])
        for k in range(8):
            if k % 2 == 0:
                c.stt("dve", on[:, k, :], ots[:, k, :], onw[:, k:k + 1], rsAB[:, :], ALU.mult, ALU.mult,
                      r=[K_("ots"), K_("rsAB"), "const"], w=[K_("on")])
            else:
                c.stt("dve", on[0:64, k, :], ots[0:64, k, :], onw[0:64, k:k + 1], rsC[0:64, :], ALU.mult, ALU.mult,
                      r=[K_("ots"), K_("rsC"), "const"], w=[K_("on")])
                c.cp("dve", on[64:128, k, :], ots[64:128, k, :], r=[K_("ots")], w=[K_("on")])
        for m in range(8):
            pb, kb = pO[m % 2], kO[m % 2]
            for k in range(8):
                c.mm(pb[:, :], wo[:, k, m * 128:(m + 1) * 128], on[:, k, :], k == 0, k == 7, r=[K_("on"), "wo"], w=[kb])
            c.cp("act", ys[:, m, :], pb[:, :], r=[kb], w=[kb, K_("ys")])
            c.act(sq[:, m, :], pb[:, :], AF.Square, r=[kb], w=[kb, K_("sq")])
        for k in range(8):
            c.mm(pA[:, :], ones[:, :], sq[:, k, :], k == 0, k == 7, r=[K_("sq"), "const"], w=[kA])
        rstd_from_ss(c, rs2, pA, 1.0 / D_MODEL, EPS, r=[kA], w=[kA, K_("rs2")])
        for k in range(8):
            c.stt("dve", ys[:, k, :], ys[:, k, :], postw[:, k:k + 1], rs2[:, :], ALU.mult, ALU.mult,
                  r=[K_("ys"), K_("rs2"), "const"], w=[K_("ys")])
            c.tt("dve", xs[:, k, :], xs[:, k, :], ys[:, k, :], ALU.add, r=[K_("ys"), K_("xs")], w=[K_("xs")])
        c.dma("sp", xmt[:, :, cols], xs[:, :, :], r=[K_("xs")], final=True)
        for k in range(8):
            c.act(sq[:, k, :], xs[:, k, :], AF.Square, r=[K_("xs")], w=[K_("sq")])
        for k in range(8):
            c.mm(pB[:, :], ones[:, :], sq[:, k, :], k == 0, k == 7, r=[K_("sq"), "const"], w=[kB])
        rstd_from_ss(c, rs3, pB, 1.0 / D_MODEL, EPS, r=[kB], w=[kB, K_("rs3")])
        for k in range(8):
            c.stt("dve", hs[:, k, :], xs[:, k, :], prew[:, k:k + 1], rs3[:, :], ALU.mult, ALU.mult,
                  r=[K_("xs"), K_("rs3"), "const"], w=[K_("hs")])
        c.dma("sp", ht[:, :, cols], hs[:, :, :], r=[K_("hs")], final=True)


def emit_Tb(c, ncols=NTOK, TB=256):
    hth = c.din("hth", [D_MODEL, ncols + 2], BF16).rearrange("(k p) t -> p k t", p=128)
    xmt = c.din("xmt", [D_MODEL, ncols], F32).rearrange("(k p) t -> p k t", p=128)
    w1 = c.din("w1", [D_MODEL, 2 * D_FF], F32).rearrange("(k p) n -> p k n", p=128)
    w2 = c.din("w2", [D_FF, D_MODEL], F32).rearrange("(k p) n -> p k n", p=128)
    fcw_d = c.din("fcw", [128, 44, 3], F32)
    fcb_d = c.din("fcb", [128, 44], F32)
    postw_d = c.din("postw", [128, 8], F32)
    nextw_d = c.din("nextw", [128, 8], F32)
    xo = c.dout("xo", [D_MODEL, ncols], F32).rearrange("(k p) t -> p k t", p=128)
    xnt = c.dout("xnt", [D_MODEL, ncols], BF16).rearrange("(k p) t -> p k t", p=128)
    ps = c.psum_banks()
    fcw = c.sb("fcw_s", [128, 44, 3], F32)
    fcb = c.sb("fcb_s", [128, 44], F32)
    postw = c.sb("postw_s", [128, 8], F32)
    nextw = c.sb("nextw_s", [128, 8], F32)
    ones = c.sb("ones", [128, 128], BF16)
    c.memset("pool", ones[:, :], 1.0, w=["const"])
    for t, d in ((fcw, fcw_d), (fcb, fcb_d), (postw, postw_d), (nextw, nextw_d)):
        c.dma("sp", t[:], d, w=["const"])
    w1b = c.sb("w1b", [128, 8, 2 * D_FF], BF16)
    w2b = c.sb("w2b", [128, 22, D_MODEL], BF16)
    stg = [c.sb(f"stg{i}", [128, 1408], F32) for i in range(2)]
    n = 0
    for k in range(8):
        for c0 in range(0, 2 * D_FF, 1408):
            c1 = c0 + 1408
            s = n % 2
            c.dma("sp" if n % 2 else "pool", stg[s][:, :], w1[:, k, c0:c1], w=[f"stg{s}"])
            c.cp(("pool", "act", "dve")[n % 3], w1b[:, k, c0:c1], stg[s][:, :], r=[f"stg{s}"], w=[f"w1b{k}"])
            n += 1
    for j in range(22):
        s = n % 2
        c.dma("sp" if n % 2 else "pool", stg[s][:, :D_MODEL], w2[:, j, :], w=[f"stg{s}"])
        c.cp(("pool", "act", "dve")[n % 3], w2b[:, j, :], stg[s][:, :D_MODEL], r=[f"stg{s}"], w=["w2b"])
        n += 1
    w1keys = [f"w1b{k}" for k in range(8)]
    hb = [c.sb(f"hb{i}", [128, 8, TB + 2], BF16) for i in range(2)]
    xm = [c.sb(f"xm{i}", [128, 8, TB], F32) for i in range(2)]
    acts = c.sb("acts", [128, 22, TB], BF16)
    ag = [c.sb(f"ag{i}", [128, TB], F32) for i in range(2)]
    au = [c.sb(f"au{i}", [128, TB], F32) for i in range(2)]
    sg = [c.sb(f"sg{i}", [128, TB], F32) for i in range(2)]
    ys = c.sb("ys", [128, 8, TB], F32)
    sq = c.sb("sq", [128, 8, TB], BF16)
    rs2 = c.sb("rs2", [128, TB], F32)
    rs3 = c.sb("rs3", [128, TB], F32)
    xn = c.sb("xn", [128, 8, TB], BF16)
    nblk = ncols // TB
    pending_tail = []
    for b in range(nblk):
        s = b % 2
        c.dma("sp", hb[s][:, :, :], hth[:, :, b * TB:b * TB + TB + 2], w=[f"hb{s}"])
        c.dma("sp", xm[s][:, :, :], xmt[:, :, b * TB:(b + 1) * TB], w=[f"xm{s}"])
        for j in range(22):
            q = j % 2
            if j == 4 and pending_tail:
                pending_tail.pop(0)()
            for which, col0, acc, pb, pk_ in (("g", j * 128, ag[q], ps[2 * q], f"psg{q}"),
                                             ("u", D_FF + j * 128, au[q], ps[2 * q + 1], f"psu{q}")):
                jj = j if which == "g" else 22 + j
                for k in range(8):
                    c.mm(pb[:, :TB + 2], w1b[:, k, col0:col0 + 128], hb[s][:, k, :], k == 0, k == 7,
                         r=[f"hb{s}", f"w1b{k}"], w=[pk_])
                ak = f"a{which}{q}"
                c.act(acc[:, :], pb[:, 1:TB + 1], AF.Identity, r=[pk_, "const"], w=[ak],
                      scale=fcw[:, jj, 1:2], bias=fcb[:, jj:jj + 1])
                c.stt("dve", acc[:, :], pb[:, 0:TB], fcw[:, jj, 0:1], acc[:, :], ALU.mult, ALU.add,
                      r=[pk_, ak, "const"], w=[ak])
                c.stt("dve", acc[:, :], pb[:, 2:TB + 2], fcw[:, jj, 2:3], acc[:, :], ALU.mult, ALU.add,
                      r=[pk_, ak, "const"], w=[ak])
            c.act(sg[q][:, :], ag[q][:, :], AF.Silu, r=[f"ag{q}"], w=[f"sg{q}"])
            c.tt("dve", acts[:, j, :], sg[q][:, :], au[q][:, :], ALU.mult, r=[f"sg{q}", f"au{q}"], w=["acts"])
        for m in range(8):
            pb = ps[4 + m % 2]
            for j in range(22):
                c.mm(pb[:, :TB], w2b[:, j, m * 128:(m + 1) * 128], acts[:, j, :], j == 0, j == 21,
                     r=["acts", "w2b"], w=[f"pso{m % 2}"])
            c.cp("act", ys[:, m, :], pb[:, :TB], r=[f"pso{m % 2}"], w=["ys"])
            c.act(sq[:, m, :], pb[:, :TB], AF.Square, r=[f"pso{m % 2}"], w=["sq"])
        def tail(b=b, s=s):
            for k in range(8):
                c.mm(ps[6][:, :TB], ones[:, :], sq[:, k, :], k == 0, k == 7, r=["sq", "const"], w=["ps6"])
            rstd_from_ss(c, rs2, ps[6][:, :TB], 1.0 / D_MODEL, EPS, r=["ps6"], w=["rs2"])
            for k in range(8):
                c.stt("dve", ys[:, k, :], ys[:, k, :], postw[:, k:k + 1], rs2[:, :], ALU.mult, ALU.mult,
                      r=["ys", "rs2", "const"], w=["ys"])
                c.tt("dve", ys[:, k, :], ys[:, k, :], xm[s][:, k, :], ALU.add, r=["ys", f"xm{s}"], w=["ys"])
            c.dma("sp", xo[:, :, b * TB:(b + 1) * TB], ys[:, :, :], r=["ys"], final=True)
            for k in range(8):
                c.act(sq[:, k, :], ys[:, k, :], AF.Square, r=["ys"], w=["sq"])
            for k in range(8):
                c.mm(ps[7][:, :TB], ones[:, :], sq[:, k, :], k == 0, k == 7, r=["sq", "const"], w=["ps7"])
            rstd_from_ss(c, rs3, ps[7][:, :TB], 1.0 / D_MODEL, EPS, r=["ps7"], w=["rs3"])
            for k in range(8):
                c.stt("dve", xn[:, k, :], ys[:, k, :], nextw[:, k:k + 1], rs3[:, :], ALU.mult, ALU.mult,
                      r=["ys", "rs3", "const"], w=["xn"])
            c.dma("sp", xnt[:, :, b * TB:(b + 1) * TB], xn[:, :, :], r=["xn"], final=True)
        pending_tail.append(tail)
    while pending_tail:
        pending_tail.pop(0)()


def attention_core(c, QT, KT, VA, dk, scale, OB, ps, pT, r64, osb, onesrow):
    NQ, NK = SEQ // 512, SEQ // 128
    its = [(qb, kc) for qb in range(NQ) for kc in range(NK)]

    def s_mm(i):
        qb, kc = its[i]
        c.mm(ps[i % 3][:, :], KT[0:dk, kc * 128:(kc + 1) * 128], QT[0:dk, qb * 512:(qb + 1) * 512], True, True,
             r=["QT", "KT"], w=[f"sT{i % 3}"])

    pending = []

    def fin1(qb):
        ob = ps[4 + qb % 2]
        c.cp("act", r64[64:65, :], ob[64:65, :], r=[f"oacc{qb % 2}"], w=["r64"])
        c.recip(r64[64:65, :], r64[64:65, :], r=["r64"], w=["r64"])
        c.cp("act", osb[:, :], ob[0:64, :], r=[f"oacc{qb % 2}"], w=["osb"])

    def fin2(qb):
        c.mm(ps[6][0:64, :], onesrow[64:65, 0:64], r64[64:65, :], True, True, r=["r64", "const"], w=["bc"])
        c.tt("dve", OB[:, qb * 512:(qb + 1) * 512], osb[:, :], ps[6][0:64, :], ALU.mult, r=["osb", "bc"], w=["OB"])

    s_mm(0)
    s_mm(1)
    for i, (qb, kc) in enumerate(its):
        c.act(pT[i % 3][:, :], ps[i % 3][:, :], AF.Exp, r=[f"sT{i % 3}"], w=[f"pT{i % 3}"], scale=scale)
        if i + 2 < len(its):
            s_mm(i + 2)
        c.mm(ps[4 + qb % 2][0:65, :], VA[:, kc, :], pT[i % 3][:, :], kc == 0, kc == NK - 1,
             r=[f"pT{i % 3}", "VA"], w=[f"oacc{qb % 2}"])
        if FILLER:
            c.mm(ps[7][:, 0:FILLER], KT[0:dk, 0:128], QT[0:dk, 0:FILLER], True, True, r=["QT", "KT"], w=["filler"])
        for p in [p for p in pending if p[0] == i]:
            p[1]()
        pending = [p for p in pending if p[0] != i]
        if kc == NK - 1:
            fin1(qb)
            if i + 6 < len(its):
                pending.append((i + 6, lambda qb=qb: fin2(qb)))
            else:
                fin2(qb)


def load_cast(c, dst, src_ap, shape, tag, q="sp", eng="pool", wkey="wts"):
    stg = c.sb(c.uid(tag + "_stg"), shape, F32)
    c.dma(q, stg[:], src_ap, w=[tag + "stg"])
    c.cp(eng, dst, stg[:], r=[tag + "stg"], w=[wkey])


def emit_MB(c):
    xnp = c.din("xnp", [D_MODEL, SEQ + 2], BF16).rearrange("(k p) t -> p k t", p=128)
    wq = c.din("wq", [D_MODEL, 64], F32).rearrange("(k p) n -> p k n", p=128)
    wk = c.din("wk", [D_MODEL, 64], F32).rearrange("(k p) n -> p k n", p=128)
    wv = c.din("wv", [D_MODEL, 64], F32).rearrange("(k p) n -> p k n", p=128)
    qnw_d = c.din("qnw", [64, 1], F32)
    knw_d = c.din("knw", [64, 1], F32)
    cosT = c.din("cosT", [64, SEQ], F32)
    sinT = c.din("sinT", [64, SEQ], F32)
    rmT_d = c.din("rmT", [64, 64], F32)
    ob_d = c.dout("ob", [64, SEQ], BF16)
    ps = c.psum_banks()
    ones = c.sb("ones", [128, 128], BF16)
    onesrow = c.sb("onesrow", [128, 64], F32)
    c.memset("pool", ones[:, :], 1.0, w=["const"])
    c.memset("pool", onesrow[:, :], 1.0, w=["const"])
    qnw = c.sb("qnw_s", [64, 1], F32)
    knw = c.sb("knw_s", [64, 1], F32)
    rmT = c.sb("rmT_s", [64, 64], F32)
    for t, d in ((qnw, qnw_d), (knw, knw_d), (rmT, rmT_d)):
        c.dma("sp", t[:], d, w=["const"])
    wqb = c.sb("wqb", [128, 8, 64], BF16)
    wkb = c.sb("wkb", [128, 8, 64], BF16)
    wvb = c.sb("wvb", [128, 8, 64], BF16)
    for t, d, n in ((wqb, wq, "wq"), (wkb, wk, "wk"), (wvb, wv, "wv")):
        load_cast(c, t[:], d, [128, 8, 64], n)
    QT = c.sb("QT", [128, SEQ], BF16)
    KT = c.sb("KT", [128, SEQ], BF16)
    c.memset("pool", QT[64:128, :], 0.0, w=["QT"])
    c.memset("pool", KT[64:128, :], 0.0, w=["KT"])
    VA = c.sb("VA", [128, SEQ // 128, 65], BF16)
    OB = c.sb("OB", [64, SEQ], BF16)
    c.memset("pool", VA[:, :, 64:65], 1.0, w=["VA"])
    xnb = [c.sb(f"xnb{i}", [128, 8, 514], BF16) for i in range(2)]
    cs = [c.sb(f"cs{i}", [64, 512], F32) for i in range(2)]
    sn = [c.sb(f"sn{i}", [64, 512], F32) for i in range(2)]
    W = []
    for i, (tag, wb, nw, dst) in enumerate((("q", wqb, qnw, QT), ("k", wkb, knw, KT))):
        W.append(dict(tag=tag, wb=wb, nw=nw, dst=dst, dk=dst is QT and "QT" or "KT", pq=ps[3 * i], pss=ps[3 * i + 1], prot=ps[3 * i + 2],
                      sqb=c.sb("sqb" + tag, [64, 512], BF16), rs=c.sb("rs" + tag, [64, 512], F32), qn=c.sb("qn" + tag, [64, 512], F32),
                      t1=c.sb("t1" + tag, [64, 512], F32), t2=c.sb("t2" + tag, [64, 512], F32)))
    for tb in range(SEQ // 512):
        s = tb % 2
        cols = slice(tb * 512, (tb + 1) * 512)
        c.dma("sp", xnb[s][:, :, :], xnp[:, :, tb * 512:tb * 512 + 514], w=[f"xnb{s}"])
        c.dma("pool", cs[s][:, :], cosT[:, cols], w=[f"cs{s}"])
        c.dma("pool", sn[s][:, :], sinT[:, cols], w=[f"sn{s}"])

        def st_proj(w):
            for k in range(8):
                c.mm(w["pq"][0:64, :], w["wb"][:, k, :], xnb[s][:, k, 1:513], k == 0, k == 7, r=[f"xnb{s}", "wts"], w=["pq" + w["tag"]])

        def st_sq(w):
            c.act(w["sqb"][:, :], w["pq"][0:64, :], AF.Square, r=["pq" + w["tag"]], w=["sqb" + w["tag"]])

        def st_ss(w):
            c.mm(w["pss"][0:64, :], ones[0:64, 0:64], w["sqb"][:, :], True, True, r=["sqb" + w["tag"], "const"], w=["pss" + w["tag"]])

        def st_rs(w):
            rstd_from_ss(c, w["rs"], w["pss"], 1.0 / 64, EPS, r=["pss" + w["tag"]], w=["rs" + w["tag"]], rows=slice(0, 64))

        def st_qn(w):
            c.stt("dve", w["qn"][:, :], w["pq"][0:64, :], w["nw"][:, 0:1], w["rs"][:, :], ALU.mult, ALU.mult,
                  r=["pq" + w["tag"], "rs" + w["tag"], "const"], w=["qn" + w["tag"]])

        def st_rot(w):
            c.mm(w["prot"][0:64, :], rmT[:, :], w["qn"][:, :], True, True, r=["qn" + w["tag"], "const"], w=["prot" + w["tag"]])

        def st_mul(w):
            c.tt("dve", w["t1"][:, :], w["qn"][:, :], cs[s][:, :], ALU.mult, r=["qn" + w["tag"], f"cs{s}"], w=["t1" + w["tag"]])
            c.tt("dve", w["t2"][:, :], w["prot"][0:64, :], sn[s][:, :], ALU.mult, r=["prot" + w["tag"], f"sn{s}"], w=["t2" + w["tag"]])

        def st_add(w):
            c.tt("dve", w["dst"][0:64, cols], w["t1"][:, :], w["t2"][:, :], ALU.add, r=["t1" + w["tag"], "t2" + w["tag"]], w=[w["dk"]])

        for stp in (st_proj, st_sq, st_ss, st_rs, st_qn, st_rot, st_mul, st_add):
            for w in W:
                stp(w)
        for ci in range(4):
            pb = ps[6 + ci % 2]
            for k in range(8):
                c.mm(pb[:, 0:64], xnb[s][:, k, 1 + ci * 128:1 + (ci + 1) * 128], wvb[:, k, :], k == 0, k == 7,
                     r=[f"xnb{s}", "wts"], w=[f"pv{ci % 2}"])
            c.cp("act", VA[:, tb * 4 + ci, 0:64], pb[:, 0:64], r=[f"pv{ci % 2}"], w=["VA"])
    c.S.barrier()
    pT = [c.sb(f"pT{i}", [128, 512], BF16) for i in range(3)]
    r64 = c.sb("r64", [128, 512], F32)
    osb = c.sb("osb", [64, 512], F32)
    attention_core(c, QT, KT, VA, MB_DK, 64 ** -0.5, OB, ps, pT, r64, osb, onesrow)
    for i in range(4):
        c.dma("sp", ob_d[:, i * 2048:(i + 1) * 2048], OB[:, i * 2048:(i + 1) * 2048], r=["OB"], final=True)


def emit_MA(c):
    xnp = c.din("xnp", [D_MODEL, SEQ + 2], BF16).rearrange("(k p) t -> p k t", p=128)
    wcq = c.din("wcq", [D_MODEL, 192], F32).rearrange("(k p) n -> p k n", p=128)
    wckv = c.din("wckv", [D_MODEL, 128], F32).rearrange("(k p) n -> p k n", p=128)
    wkr = c.din("wkr", [D_MODEL, 32], F32).rearrange("(k p) n -> p k n", p=128)
    wuq0_d = c.din("wuq0", [128, 96], F32)
    wuq1_d = c.din("wuq1", [64, 96], F32)
    wuk_d = c.din("wuk", [128, 64], F32)
    wuv_d = c.din("wuv", [128, 64], F32)
    qnw_d = c.din("qnw", [128, 2], F32)
    kvnw_d = c.din("kvnw", [128, 1], F32)
    cosT = c.din("cosT", [96, SEQ], F32)
    sinT = c.din("sinT", [96, SEQ], F32)
    rmT_d = c.din("rmT", [96, 96], F32)
    ob_d = c.dout("ob", [64, SEQ], BF16)
    ps = c.psum_banks()
    ones = c.sb("ones", [128, 128], BF16)
    onesrow = c.sb("onesrow", [128, 64], F32)
    c.memset("pool", ones[:, :], 1.0, w=["const"])
    c.memset("pool", onesrow[:, :], 1.0, w=["const"])
    qnw = c.sb("qnw_s", [128, 2], F32)
    kvnw = c.sb("kvnw_s", [128, 1], F32)
    rmT = c.sb("rmT_s", [96, 96], F32)
    for t, d in ((qnw, qnw_d), (kvnw, kvnw_d), (rmT, rmT_d)):
        c.dma("sp", t[:], d, w=["const"])
    wcqb = c.sb("wcqb", [128, 8, 192], BF16)
    wckvb = c.sb("wckvb", [128, 8, 128], BF16)
    wkrp = c.sb("wkrp", [128, 8, 96], BF16)
    wkp = c.sb("wkp", [128, 96], BF16)
    wuq0 = c.sb("wuq0b", [128, 96], BF16)
    wuq1 = c.sb("wuq1b", [64, 96], BF16)
    wuv = c.sb("wuvb", [128, 64], BF16)
    c.memset("pool", wkrp[:, :, :], 0.0, w=["wts"])
    c.memset("pool", wkp[:, :], 0.0, w=["wts"])
    load_cast(c, wcqb[:], wcq, [128, 8, 192], "wcq")
    load_cast(c, wckvb[:], wckv, [128, 8, 128], "wckv")
    load_cast(c, wkrp[:, :, 64:96], wkr, [128, 8, 32], "wkr")
    load_cast(c, wkp[:, 0:64], wuk_d, [128, 64], "wuk")
    load_cast(c, wuq0[:], wuq0_d, [128, 96], "wuq0")
    load_cast(c, wuq1[:], wuq1_d, [64, 96], "wuq1")
    load_cast(c, wuv[:], wuv_d, [128, 64], "wuv")
    QT = c.sb("QT", [128, SEQ], BF16)
    KT = c.sb("KT", [128, SEQ], BF16)
    c.memset("pool", QT[64:128, :], 0.0, w=["QT"])
    c.memset("pool", KT[64:128, :], 0.0, w=["KT"])
    VA = c.sb("VA", [128, SEQ // 128, 65], BF16)
    OB = c.sb("OB", [64, SEQ], BF16)
    c.memset("pool", VA[:, :, 64:65], 1.0, w=["VA"])
    xnb = [c.sb(f"xnb{i}", [128, 8, 514], BF16) for i in range(2)]
    cs = [c.sb(f"cs{i}", [96, 512], F32) for i in range(2)]
    sn = [c.sb(f"sn{i}", [96, 512], F32) for i in range(2)]
    sq0 = c.sb("sq0", [128, 512], BF16)
    sq1 = c.sb("sq1", [64, 512], BF16)
    rs = c.sb("rs", [128, 512], F32)
    rs2 = c.sb("rs2", [128, 512], F32)
    cqn0 = c.sb("cqn0", [128, 512], BF16)
    cqn1 = c.sb("cqn1", [64, 512], BF16)
    ckvn = c.sb("ckvn", [128, 512], BF16)
    qs = c.sb("qs", [96, 512], F32)
    t1 = c.sb("t1", [96, 512], F32)
    t2 = c.sb("t2", [96, 512], F32)
    for tb in range(SEQ // 512):
        s = tb % 2
        cols = slice(tb * 512, (tb + 1) * 512)
        c.dma("sp", xnb[s][:, :, :], xnp[:, :, tb * 512:tb * 512 + 514], w=[f"xnb{s}"])
        c.dma("pool", cs[s][:, :], cosT[:, cols], w=[f"cs{s}"])
        c.dma("pool", sn[s][:, :], sinT[:, cols], w=[f"sn{s}"])
        xk = [f"xnb{s}", "wts"]
        for k in range(8):
            c.mm(ps[0][:, :], wcqb[:, k, 0:128], xnb[s][:, k, 1:513], k == 0, k == 7, r=xk, w=["pcq0"])
        for k in range(8):
            c.mm(ps[1][0:64, :], wcqb[:, k, 128:192], xnb[s][:, k, 1:513], k == 0, k == 7, r=xk, w=["pcq1"])
        for k in range(8):
            c.mm(ps[2][:, :], wckvb[:, k, :], xnb[s][:, k, 1:513], k == 0, k == 7, r=xk, w=["pckv"])
        c.act(sq0[:, :], ps[0][:, :], AF.Square, r=["pcq0"], w=["sq0"])
        c.act(sq1[:, :], ps[1][0:64, :], AF.Square, r=["pcq1"], w=["sq1"])
        c.mm(ps[3][:, :], ones[:, :], sq0[:, :], True, False, r=["sq0", "const"], w=["pss"])
        c.mm(ps[3][:, :], ones[0:64, :], sq1[:, :], False, True, r=["sq1", "const"], w=["pss"])
        rstd_from_ss(c, rs, ps[3], 1.0 / 192, EPS, r=["pss"], w=["rs"])
        c.stt("dve", cqn0[:, :], ps[0][:, :], qnw[:, 0:1], rs[:, :], ALU.mult, ALU.mult, r=["pcq0", "rs", "const"], w=["cqn0"])
        c.stt("dve", cqn1[:, :], ps[1][0:64, :], qnw[0:64, 1:2], rs[0:64, :], ALU.mult, ALU.mult, r=["pcq1", "rs", "const"], w=["cqn1"])
        c.act(sq0[:, :], ps[2][:, :], AF.Square, r=["pckv"], w=["sq0"])
        c.mm(ps[3][:, :], ones[:, :], sq0[:, :], True, True, r=["sq0", "const"], w=["pss"])
        rstd_from_ss(c, rs2, ps[3], 1.0 / 128, EPS, r=["pss"], w=["rs2"])
        c.stt("dve", ckvn[:, :], ps[2][:, :], kvnw[:, 0:1], rs2[:, :], ALU.mult, ALU.mult, r=["pckv", "rs2", "const"], w=["ckvn"])
        for which, dst, dk_ in (("q", QT, "QT"), ("k", KT, "KT")):
            if which == "q":
                c.mm(ps[4][0:96, :], wuq0[:, :], cqn0[:, :], True, False, r=["cqn0", "wts"], w=["pqk"])
                c.mm(ps[4][0:96, :], wuq1[:, :], cqn1[:, :], False, True, r=["cqn1", "wts"], w=["pqk"])
            else:
                c.mm(ps[4][0:96, :], wkp[:, :], ckvn[:, :], True, False, r=["ckvn", "wts"], w=["pqk"])
                for k in range(8):
                    c.mm(ps[4][0:96, :], wkrp[:, k, :], xnb[s][:, k, 1:513], False, k == 7, r=xk, w=["pqk"])
            c.cp("act", qs[:, :], ps[4][0:96, :], r=["pqk"], w=["qs"])
            c.mm(ps[5][0:96, :], rmT[:, :], qs[:, :], True, True, r=["qs", "const"], w=["prot"])
            c.tt("dve", t1[:, :], qs[:, :], cs[s][:, :], ALU.mult, r=["qs", f"cs{s}"], w=["t1"])
            c.tt("dve", t2[:, :], ps[5][0:96, :], sn[s][:, :], ALU.mult, r=["prot", f"sn{s}"], w=["t2"])
            c.tt("dve", dst[0:96, cols], t1[:, :], t2[:, :], ALU.add, r=["t1", "t2"], w=[dk_])
        for ci in range(4):
            pb = ps[6 + ci % 2]
            c.mm(pb[:, 0:64], ckvn[:, ci * 128:(ci + 1) * 128], wuv[:, :], True, True, r=["ckvn", "wts"], w=[f"pv{ci % 2}"])
            c.cp("act", VA[:, tb * 4 + ci, 0:64], pb[:, 0:64], r=[f"pv{ci % 2}"], w=["VA"])
    c.S.barrier()
    pT = [c.sb(f"pT{i}", [128, 512], BF16) for i in range(3)]
    r64 = c.sb("r64", [128, 512], F32)
    osb = c.sb("osb", [64, 512], F32)
    attention_core(c, QT, KT, VA, 128, 96 ** -0.5, OB, ps, pT, r64, osb, onesrow)
    for i in range(4):
        c.dma("sp", ob_d[:, i * 2048:(i + 1) * 2048], OB[:, i * 2048:(i + 1) * 2048], r=["OB"], final=True)


OFF_A, OFF_B, OFF_C, OFF_D = 0, 352, 864, 1640


def rope_tables(rot_dim):
    rows = SEQ // 64
    row = np.repeat(np.arange(rows), 64).astype(np.float32)
    col = np.tile(np.arange(64), rows).astype(np.float32)
    sec = rot_dim // 2
    inv_freq = (np.float32(10000.0) ** (-np.arange(0, sec, 2, dtype=np.float32) / np.float32(sec))).astype(np.float32)
    ang_r = row[:, None] * inv_freq
    ang_c = col[:, None] * inv_freq
    ang = np.concatenate([ang_r, ang_r, ang_c, ang_c], -1).astype(np.float32)
    return np.cos(ang).astype(np.float32), np.sin(ang).astype(np.float32)


def rot_matrix(r):
    Rm = np.zeros((r, r), np.float32)
    q = r // 4
    for a in range(2):
        for e in range(q):
            Rm[a * 2 * q + e, a * 2 * q + q + e] = -1.0
            Rm[a * 2 * q + q + e, a * 2 * q + e] = 1.0
    return Rm


_CONST = {}


def consts():
    if not _CONST:
        cb, sb_ = rope_tables(64)
        ca, sa = rope_tables(32)
        _CONST["cosB"] = np.ascontiguousarray(cb.T)
        _CONST["sinB"] = np.ascontiguousarray(sb_.T)
        cA = np.ones((96, SEQ), np.float32)
        sA = np.zeros((96, SEQ), np.float32)
        cA[64:96] = ca.T
        sA[64:96] = sa.T
        _CONST["cosA"], _CONST["sinA"] = cA, sA
        _CONST["rmTB"] = np.ascontiguousarray(rot_matrix(64).T)
        rA = np.zeros((96, 96), np.float32)
        rA[64:96, 64:96] = rot_matrix(32).T
        _CONST["rmTA"] = rA
    return _CONST


def pad_seq(xnt):
    out = np.zeros((xnt.shape[0], SEQ + 2), xnt.dtype)
    out[:, 1:-1] = xnt
    return out


def maps_MB(inp, l, xnp_b):
    K_ = consts()
    w = inp["w_in"][l]
    maps = []
    for cidx in range(NCORES):
        b, h = cidx // 4, cidx % 4
        g = h // 2
        maps.append({
            "xnp": xnp_b[b],
            "wq": np.ascontiguousarray(w[:, OFF_B + h * 64:OFF_B + (h + 1) * 64]),
            "wk": np.ascontiguousarray(w[:, OFF_B + 256 + g * 64:OFF_B + 256 + (g + 1) * 64]),
            "wv": np.ascontiguousarray(w[:, OFF_B + 384 + g * 64:OFF_B + 384 + (g + 1) * 64]),
            "qnw": np.ascontiguousarray(inp["b_q_norm"][l].reshape(64, 1)),
            "knw": np.ascontiguousarray(inp["b_k_norm"][l].reshape(64, 1)),
            "cosT": K_["cosB"], "sinT": K_["sinB"], "rmT": K_["rmTB"],
        })
    return maps


def maps_MA(inp, l, xnp_b):
    K_ = consts()
    w = inp["w_in"][l]
    wuq = inp["a_w_uq"][l]
    wukv = inp["a_w_ukv"][l]
    qn = inp["a_q_norm"][l]
    qnw = np.zeros((128, 2), np.float32)
    qnw[:, 0] = qn[0:128]
    qnw[0:64, 1] = qn[128:192]
    maps = []
    for cidx in range(NCORES):
        b, h = cidx // 4, cidx % 4
        wq_h = wuq[:, h * 96:(h + 1) * 96]
        maps.append({
            "xnp": xnp_b[b],
            "wcq": np.ascontiguousarray(w[:, OFF_A:OFF_A + 192]),
            "wckv": np.ascontiguousarray(w[:, OFF_A + 192:OFF_A + 320]),
            "wkr": np.ascontiguousarray(w[:, OFF_A + 320:OFF_A + 352]),
            "wuq0": np.ascontiguousarray(wq_h[0:128]), "wuq1": np.ascontiguousarray(wq_h[128:192]),
            "wuk": np.ascontiguousarray(wukv[:, h * 128:h * 128 + 64]),
            "wuv": np.ascontiguousarray(wukv[:, h * 128 + 64:h * 128 + 128]),
            "qnw": qnw, "kvnw": np.ascontiguousarray(inp["a_kv_norm"][l].reshape(128, 1)),
            "cosT": K_["cosA"], "sinT": K_["sinA"], "rmT": K_["rmTA"],
        })
    return maps


NEG = -30000.0
FILLER = 0
MB_DK = 128


def chunk_masks():
    i = np.arange(128)
    m = {}
    m["Lf"] = (i[:, None] > i[None, :]).astype(np.float32)
    m["Rf"] = (i[:, None] <= i[None, :]).astype(np.float32)
    m["Lb"] = (i[:, None] < i[None, :]).astype(np.float32)
    m["Rb"] = (i[:, None] >= i[None, :]).astype(np.float32)
    m["nmf"] = np.where(i[:, None] <= i[None, :], 0.0, NEG).astype(np.float32)
    m["nmb"] = np.where(i[:, None] >= i[None, :], 0.0, NEG).astype(np.float32)
    m["nmfs"] = np.where(i[:, None] < i[None, :], 0.0, NEG).astype(np.float32)
    m["nmbs"] = np.where(i[:, None] > i[None, :], 0.0, NEG).astype(np.float32)
    m["ident"] = np.eye(128, dtype=np.float32)
    return m


def emit_conv_silu(c, ps_main, ps_halo, pre, acc, taps, dst, tagk, rk, wk):
    c.cp("act", pre[:, 1:513], ps_main[:, 0:512], r=[rk[0]], w=[tagk + "pre"])
    c.cp("dve", pre[:, 0:514:513], ps_halo[:, 0:2], r=[rk[1]], w=[tagk + "pre"])
    c.act(acc[:, :], pre[:, 1:513], AF.Identity, r=[tagk + "pre", "const"], w=[tagk + "acc"], scale=taps[:, 1:2], bias=taps[:, 3:4])
    c.stt("dve", acc[:, :], pre[:, 0:512], taps[:, 0:1], acc[:, :], ALU.mult, ALU.add, r=[tagk + "pre", tagk + "acc", "const"], w=[tagk + "acc"])
    c.stt("dve", acc[:, :], pre[:, 2:514], taps[:, 2:3], acc[:, :], ALU.mult, ALU.add, r=[tagk + "pre", tagk + "acc", "const"], w=[tagk + "acc"])
    c.act(dst, acc[:, :], AF.Silu, r=[tagk + "acc"], w=wk)


def emit_MC(c):
    NCH = SEQ // 128
    xnp = c.din("xnp", [D_MODEL, SEQ + 2], BF16).rearrange("(k p) t -> p k t", p=128)
    w1 = c.din("w1", [D_MODEL, 128], F32).rearrange("(k p) n -> p k n", p=128)
    w2 = c.din("w2", [D_MODEL, 128], F32).rearrange("(k p) n -> p k n", p=128)
    wdt = c.din("wdt", [D_MODEL, 2], F32).rearrange("(k p) n -> p k n", p=128)
    taps1_d = c.din("taps1", [128, 4], F32)
    taps2_d = c.din("taps2", [128, 4], F32)
    scal_d = c.din("scal", [128, 6], F32)
    mk_d = {k: c.din("m_" + k, [128, 128], F32) for k in ("Lf", "Rf", "Lb", "Rb", "nmf", "nmb", "ident")}
    ob_d = c.dout("ob", [64, SEQ], BF16)
    ps = c.psum_banks()
    mk = {k: c.sb("mk_" + k, [128, 128], F32) for k in mk_d}
    for k in mk_d:
        c.dma("sp", mk[k][:, :], mk_d[k], w=["const"])
    taps1 = c.sb("taps1_s", [128, 4], F32)
    taps2 = c.sb("taps2_s", [128, 4], F32)
    scal = c.sb("scal_s", [128, 6], F32)
    for t, d in ((taps1, taps1_d), (taps2, taps2_d), (scal, scal_d)):
        c.dma("sp", t[:], d, w=["const"])
    ones32 = c.sb("ones32", [128, 128], F32)
    c.memset("pool", ones32[:, :], 1.0, w=["const"])
    w1b = c.sb("w1b", [128, 8, 128], BF16)
    w2b = c.sb("w2b", [128, 8, 128], BF16)
    wdtb = c.sb("wdtb", [128, 8, 2], BF16)
    load_cast(c, w1b[:], w1, [128, 8, 128], "w1")
    load_cast(c, w2b[:], w2, [128, 8, 128], "w2")
    load_cast(c, wdtb[:], wdt, [128, 8, 2], "wdt")
    F1 = c.sb("F1", [128, SEQ], F32)
    F2 = c.sb("F2", [128, SEQ], F32)
    Y = c.sb("Y", [64, SEQ], F32)
    C0 = c.sb("C0", [128, SEQ], BF16)
    F1b = c.sb("F1b", [128, SEQ], BF16)
    c.memset("pool", C0[0:64, :], 0.0, w=["C0"])
    OB = c.sb("OB", [64, SEQ], BF16)
    RAW = c.sb("RAW", [128, NCH, 2], F32)
    DT = c.sb("DT", [128, NCH, 2], F32)
    AA = c.sb("AA", [128, NCH, 2], F32)
    expa = c.sb("expa", [128, 2], F32)
    xnb = [c.sb(f"xnb{i}", [128, 8, 514], BF16) for i in range(2)]
    pre = c.sb("pre", [128, 514], F32)
    acc = c.sb("acc", [128, 512], F32)
    for tb in range(SEQ // 512):
        s = tb % 2
        cols = slice(tb * 512, (tb + 1) * 512)
        c.dma("sp", xnb[s][:, :, :], xnp[:, :, tb * 512:tb * 512 + 514], w=[f"xnb{s}"])
        xk = [f"xnb{s}", "wts"]
        for ti, (wb, taps, F) in enumerate(((w1b, taps1, F1), (w2b, taps2, F2))):
            for k in range(8):
                c.mm(ps[ti][:, :], wb[:, k, :], xnb[s][:, k, 1:513], k == 0, k == 7, r=xk, w=[f"pm{ti}"])
            for k in range(8):
                c.mm(ps[2 + ti][:, 0:2], wb[:, k, :], xnb[s][:, k, 0:514:513], k == 0, k == 7, r=xk, w=[f"ph{ti}"])
            emit_conv_silu(c, ps[ti], ps[2 + ti], pre, acc, taps, F[:, cols], "c", (f"pm{ti}", f"ph{ti}"), [f"F{ti + 1}"])
            if ti == 1:
                c.cp("pool", C0[64:128, cols], F2[64:128, cols], r=["F2"], w=["C0"])
            else:
                c.cp("pool", F1b[:, cols], F1[:, cols], r=["F1"], w=["F1b"])
        for ci in range(4):
            pb = ps[4 + ci % 2]
            for k in range(8):
                c.mm(pb[:, 0:2], xnb[s][:, k, 1 + ci * 128:1 + (ci + 1) * 128], wdtb[:, k, :], k == 0, k == 7, r=xk, w=[f"pdt{ci % 2}"])
            c.cp("act", RAW[:, tb * 4 + ci, :], pb[:, 0:2], r=[f"pdt{ci % 2}"], w=["RAW"])
    for d in range(2):
        c.act(DT[:, :, d], RAW[:, :, d], AF.Exp, r=["RAW", "const"], w=["DT"], bias=scal[:, d:d + 1])
        c.act(DT[:, :, d], DT[:, :, d], AF.Ln, r=["DT"], w=["DT"], bias=1.0)
    c.act(expa[:, :], scal[:, 2:4], AF.Exp, r=["const"], w=["expa"])
    for d in range(2):
        c.ts("dve", AA[:, :, d], DT[:, :, d], expa[:, d:d + 1], -1.0, ALU.mult, ALU.mult, r=["DT", "expa"], w=["AA"])
    c.S.barrier()
    T = {}
    for d in range(2):
        for n in ("lhsA", "abc", "E", "eacb"):
            T[(n, d)] = c.sb(f"{n}{d}", [128, 128], F32)
        for n in ("MT", "BD", "CD"):
            T[(n, d)] = c.sb(f"{n}{d}", [128, 128], BF16)
        T[("XC", d)] = c.sb(f"XC{d}", [128, 64], BF16)
        T[("STb", d)] = c.sb(f"STb{d}", [128, 64], BF16)
        c.memset("pool", T[("STb", d)][:, :], 0.0, w=[f"STb{d}"])
        T[("sm", d)] = c.sb(f"sm{d}", [128, 2], F32)
        T[("ST", d)] = c.sb(f"ST{d}", [128, 64], F32)
        c.memset("pool", T[("BD", d)][:, :], 0.0, w=[f"BD{d}"])
        c.memset("pool", T[("CD", d)][:, :], 0.0, w=[f"CD{d}"])
        c.memset("pool", T[("ST", d)][:, :], 0.0, w=[f"ST{d}"])
    tmpy = c.sb("tmpy", [64, 128], F32)
    done = set()
    order = []
    for i in range(NCH):
        order.append((0, i))
        order.append((1, NCH - 1 - i))
    for d, ch in order:
        cc = slice(ch * 128, (ch + 1) * 128)
        mL, mR, nm = (mk["Lf"], mk["Rf"], mk["nmf"]) if d == 0 else (mk["Lb"], mk["Rb"], mk["nmb"])
        a_col = AA[:, ch, d:d + 1]
        dt_col = DT[:, ch, d:d + 1]
        t = lambda n: T[(n, d)]
        k_ = lambda n: f"{n}{d}"
        bA, bB, bC, bD = ps[4 * d], ps[4 * d + 1], ps[4 * d + 2], ps[4 * d + 3]
        kA, kB, kC, kD = (f"B{4 * d + i}" for i in range(4))
        sm = t("sm")
        c.ts("dve", t("lhsA")[:, :], mL[:, :], a_col, None, ALU.mult, None, r=["AA", "const"], w=[k_("lhsA")])
        c.ts("dve", t("abc")[:, :], ones32[:, :], a_col, None, ALU.mult, None, r=["AA", "const"], w=[k_("abc")])
        c.mm(bA[:, 0:128], t("lhsA")[:, :], mR[:, :], True, True, r=[k_("lhsA"), "const"], w=[kA, k_("pseg")])
        c.mm(bA[:, 128:256], t("abc")[:, :], mR[:, :], True, True, r=[k_("abc"), "const"], w=[kA, k_("pacb")])
        c.mm(bB[:, 0:1], mL[:, :], a_col, True, True, r=["AA", "const"], w=[kB, k_("psm0")])
        c.mm(bB[:, 1:2], ones32[:, :], a_col, True, True, r=["AA", "const"], w=[kB, k_("psm1")])
        c.mm(bB[:, 128:256], F1b[:, cc], C0[:, cc], True, True, r=["F1b", "C0"], w=[kB, k_("psc")])
        c.tr(bC[:, 0:128], F1[:, cc], mk["ident"][:, :], r=["F1", "const"], w=[kC, k_("ptr")])
        c.act(sm[:, :], bB[:, 0:2], AF.Exp, r=[k_("psm0"), k_("psm1")], w=[kB, k_("sm")])
        c.tt("dve", t("E")[:, :], bA[:, 0:128], nm[:, :], ALU.add, r=[k_("pseg"), "const"], w=[kA, k_("E")])
        c.act(t("eacb")[64:128, :], bA[64:128, 128:256], AF.Exp, r=[k_("pacb")], w=[kA, k_("eacb")])
        c.act(t("E")[:, :], t("E")[:, :], AF.Exp, r=[k_("E")], w=[k_("E")])
        c.tt("dve", t("MT")[:, :], bB[:, 128:256], t("E")[:, :], ALU.mult, r=[k_("psc"), k_("E")], w=[kB, k_("MT")])
        c.ts("dve", t("XC")[:, :], bC[:, 0:64], dt_col, None, ALU.mult, None, r=[k_("ptr"), "DT"], w=[kC, k_("XC")])
        c.ts("dve", t("BD")[:, 64:128], bC[:, 64:128], sm[:, 0:1], None, ALU.mult, None, r=[k_("ptr"), k_("sm")], w=[kC, k_("BD")])
        c.tt("dve", t("CD")[64:128, :], F2[64:128, cc], t("eacb")[64:128, :], ALU.mult, r=["F2", k_("eacb")], w=[k_("CD")])
        c.mm(bD[0:64, 0:128], t("XC")[:, :], t("MT")[:, :], True, False, r=[k_("XC"), k_("MT")], w=[kD, k_("py")])
        c.mm(bD[0:64, 0:128], t("STb")[:, :], t("CD")[:, :], False, True, r=[k_("STb"), k_("CD")], w=[kD, k_("py")])
        c.mm(bD[:, 128:192], t("BD")[:, :], t("XC")[:, :], True, True, r=[k_("BD"), k_("XC")], w=[kD, k_("pst")])
        c.stt("dve", t("ST")[64:128, :], t("ST")[64:128, :], sm[64:128, 1:2], bD[64:128, 128:192], ALU.mult, ALU.add,
              r=[k_("ST"), k_("sm"), k_("pst")], w=[kD, k_("ST")])
        c.cp("act", t("STb")[64:128, :], t("ST")[64:128, :], r=[k_("ST")], w=[k_("STb")])
        if ch not in done:
            done.add(ch)
            c.cp("act", Y[:, cc], bD[0:64, 0:128], r=[k_("py")], w=[kD, "Y"])
        else:
            c.tt("dve", tmpy[:, :], Y[:, cc], bD[0:64, 0:128], ALU.add, r=["Y", k_("py")], w=[kD, "tmpy"])
            c.stt("dve", tmpy[:, :], F1[0:64, cc], scal[0:64, 4:5], tmpy[:, :], ALU.mult, ALU.add, r=["F1", "tmpy", "const"], w=["tmpy"])
            c.tt("pool", OB[:, cc], tmpy[:, :], F2[0:64, cc], ALU.mult, r=["tmpy", "F2"], w=["OB"])
    for i in range(4):
        c.dma("sp", ob_d[:, i * 2048:(i + 1) * 2048], OB[:, i * 2048:(i + 1) * 2048], r=["OB"], final=True)


def maps_MC(inp, l, xnp_b):
    mk = chunk_masks()
    w = inp["w_in"][l]
    cw = inp["c_conv_w"][l]
    cb = inp["c_conv_b"][l]
    maps = []
    for cidx in range(NCORES):
        b, h = cidx // 4, cidx % 4
        g = h // 2
        xs_c = np.arange(h * 64, (h + 1) * 64)
        B_c = 256 + np.arange(g * 64, (g + 1) * 64)
        C_c = 384 + np.arange(g * 64, (g + 1) * 64)
        ch1 = np.concatenate([xs_c, B_c])
        taps1 = np.concatenate([cw[:, ch1].T, cb[ch1][:, None]], 1).astype(np.float32)
        taps2 = np.zeros((128, 4), np.float32)
        taps2[0:64, 1] = 1.0
        taps2[64:128, 0:3] = cw[:, C_c].T
        taps2[64:128, 3] = cb[C_c]
        scal = np.zeros((128, 6), np.float32)
        scal[:, 0] = inp["c_dt_bias"][l][0, h]
        scal[:, 1] = inp["c_dt_bias"][l][1, h]
        scal[:, 2] = inp["c_a_log"][l][0, h]
        scal[:, 3] = inp["c_a_log"][l][1, h]
        scal[:, 4] = inp["c_d_skip"][l][h]
        m = {
            "xnp": xnp_b[b],
            "w1": np.ascontiguousarray(w[:, OFF_C + 256 + ch1]),
            "w2": np.ascontiguousarray(np.concatenate([w[:, OFF_C + h * 64:OFF_C + (h + 1) * 64], w[:, OFF_C + 256 + C_c]], 1)),
            "wdt": np.ascontiguousarray(w[:, [OFF_C + 768 + h, OFF_C + 772 + h]]),
            "taps1": np.ascontiguousarray(taps1), "taps2": taps2, "scal": scal,
        }
        for k in ("Lf", "Rf", "Lb", "Rb", "nmf", "nmb", "ident"):
            m["m_" + k] = mk[k]
        maps.append(m)
    return maps


def emit_MD(c):
    NCH = SEQ // 128
    xnp = c.din("xnp", [D_MODEL, SEQ + 2], BF16).rearrange("(k p) t -> p k t", p=128)
    wqv = c.din("wqv", [D_MODEL, 128], F32).rearrange("(k p) n -> p k n", p=128)
    wkz = c.din("wkz", [D_MODEL, 128], F32).rearrange("(k p) n -> p k n", p=128)
    wab = c.din("wab", [D_MODEL, 4], F32).rearrange("(k p) n -> p k n", p=128)
    tapsqv_d = c.din("tapsqv", [128, 4], F32)
    tapskz_d = c.din("tapskz", [128, 4], F32)
    scal_d = c.din("scal", [128, 6], F32)
    normw_d = c.din("normw", [128, 64], F32)
    mnames = ("Lf", "Rf", "Lb", "Rb", "nmf", "nmb", "ident")
    mk_d = {k: c.din("m_" + k, [128, 128], F32) for k in mnames}
    ob_d = c.dout("ob", [64, SEQ], BF16)
    ps = c.psum_banks()
    mk = {k: c.sb("mk_" + k, [128, 128], F32) for k in mk_d}
    for k in mk_d:
        c.dma("sp", mk[k][:, :], mk_d[k], w=["const"])
    tapsqv = c.sb("tapsqv_s", [128, 4], F32)
    tapskz = c.sb("tapskz_s", [128, 4], F32)
    scal = c.sb("scal_s", [128, 6], F32)
    normw = c.sb("normw_s", [128, 64], F32)
    for t, d in ((tapsqv, tapsqv_d), (tapskz, tapskz_d), (scal, scal_d), (normw, normw_d)):
        c.dma("sp", t[:], d, w=["const"])
    ones32 = c.sb("ones32", [128, 128], F32)
    c.memset("pool", ones32[:, :], 1.0, w=["const"])
    wqvb = c.sb("wqvb", [128, 8, 128], BF16)
    wkzb = c.sb("wkzb", [128, 8, 128], BF16)
    wabb = c.sb("wabb", [128, 8, 4], BF16)
    load_cast(c, wqvb[:], wqv, [128, 8, 128], "wqv")
    load_cast(c, wkzb[:], wkz, [128, 8, 128], "wkz")
    load_cast(c, wabb[:], wab, [128, 8, 4], "wab")
    FQV = c.sb("FQV", [128, SEQ], F32)
    FKZ = c.sb("FKZ", [128, SEQ], F32)
    OACC = c.sb("OACC", [128, NCH, 64], F32)
    ZS = c.sb("ZS", [128, NCH, 64], F32)
    OB = c.sb("OB", [64, SEQ], BF16)
    RAW = c.sb("RAW", [128, NCH, 4], F32)
    BETA = c.sb("BETA", [128, NCH, 2], F32)
    GG = c.sb("GG", [128, NCH, 2], F32)
    expa = c.sb("expa", [128, 2], F32)
    st1 = contextlib.ExitStack()
    xnb = [c.sb(f"xnb{i}", [128, 8, 514], BF16, st=st1) for i in range(2)]
    pre = c.sb("pre", [128, 514], F32, st=st1)
    acc = c.sb("acc", [128, 512], F32, st=st1)
    sq = c.sb("sq", [64, 512], F32, st=st1)
    rs = c.sb("rs", [64, 512], F32, st=st1)
    for tb in range(SEQ // 512):
        s = tb % 2
        cols = slice(tb * 512, (tb + 1) * 512)
        c.dma("sp", xnb[s][:, :, :], xnp[:, :, tb * 512:tb * 512 + 514], w=[f"xnb{s}"])
        xk = [f"xnb{s}", "wts"]
        for ti, (wb, taps, F, qscale) in enumerate(((wqvb, tapsqv, FQV, 0.125), (wkzb, tapskz, FKZ, 1.0))):
            for k in range(8):
                c.mm(ps[ti][:, :], wb[:, k, :], xnb[s][:, k, 1:513], k == 0, k == 7, r=xk, w=[f"pm{ti}"])
            for k in range(8):
                c.mm(ps[2 + ti][:, 0:2], wb[:, k, :], xnb[s][:, k, 0:514:513], k == 0, k == 7, r=xk, w=[f"ph{ti}"])
            fk = f"F{ti}"
            emit_conv_silu(c, ps[ti], ps[2 + ti], pre, acc, taps, F[:, cols], "d", (f"pm{ti}", f"ph{ti}"), [fk])
            c.tt("dve", sq[:, :], F[0:64, cols], F[0:64, cols], ALU.mult, r=[fk], w=["sq"])
            c.mm(ps[6][0:64, :], ones32[0:64, 0:64], sq[:, :], True, True, r=["sq", "const"], w=["pss"])
            c.act(rs[:, :], ps[6][0:64, :], AF.Ln, r=["pss"], w=["rs"], bias=1e-6)
            c.act(rs[:, :], rs[:, :], AF.Exp, r=["rs"], w=["rs"], scale=-0.5)
            c.stt("dve", F[0:64, cols], F[0:64, cols], qscale, rs[:, :], ALU.mult, ALU.mult, r=[fk, "rs"], w=[fk])
        for ci in range(4):
            pb = ps[4 + ci % 2]
            for k in range(8):
                c.mm(pb[:, 0:4], xnb[s][:, k, 1 + ci * 128:1 + (ci + 1) * 128], wabb[:, k, :], k == 0, k == 7, r=xk, w=[f"pab{ci % 2}"])
            c.cp("act", RAW[:, tb * 4 + ci, :], pb[:, 0:4], r=[f"pab{ci % 2}"], w=["RAW"])
    c.act(BETA[:, :, :], RAW[:, :, 0:2], AF.Exp, r=["RAW"], w=["BETA"], scale=-1.0)
    c.ts("dve", BETA[:, :, :], BETA[:, :, :], 1.0, None, ALU.add, None, r=["BETA"], w=["BETA"])
    c.recip(BETA[:, :, :], BETA[:, :, :], r=["BETA"], w=["BETA"])
    for d in range(2):
        c.act(GG[:, :, d], RAW[:, :, 2 + d], AF.Exp, r=["RAW", "const"], w=["GG"], bias=scal[:, d:d + 1])
        c.act(GG[:, :, d], GG[:, :, d], AF.Ln, r=["GG"], w=["GG"], bias=1.0)
    c.act(expa[:, :], scal[:, 2:4], AF.Exp, r=["const"], w=["expa"])
    for d in range(2):
        c.ts("dve", GG[:, :, d], GG[:, :, d], expa[:, d:d + 1], -1.0, ALU.mult, ALU.mult, r=["GG", "expa"], w=["GG"])
    c.S.barrier()
    st1.close()
    G = 8
    slots = []
    for g in range(G):
        t = {n: c.sb(f"{n}_{g}", [128, 128], F32) for n in ("E", "Es", "NT", "NN", "PTk", "Pk", "X0", "X1", "PT")}
        t["WT"] = c.sb(f"WT_{g}", [64, 128], F32)
        for n in ("VN", "KD", "OT"):
            t[n] = c.sb(f"{n}_{g}", [128, 64], F32)
        t["sm"] = c.sb(f"sm_{g}", [128, 4], F32)
        slots.append(t)
    Sst = [c.sb(f"S{d}", [64, 64], F32) for d in range(2)]
    for d in range(2):
        c.memset("pool", Sst[d][:, :], 0.0, w=[f"S{d}"])
    osum = c.sb("osum", [128, 64], F32)
    osq = c.sb("osq", [128, 64], F32)
    oss = c.sb("oss", [128, 1], F32)
    og = c.sb("og", [128, 64], F32)
    ident = mk["ident"]
    done = set()
    stepno = [0]

    def region(g, s):
        bank = (g // 4) * 4 + (s % 4)
        reg = g % 4
        return ps[bank][:, reg * 128:(reg + 1) * 128], f"B{bank}", f"r{bank}_{reg}"

    def STEP(pe_fn, cons_fn):
        s = stepno[0]
        stepno[0] += 1
        for g in range(G):
            R_, bk, rk = region(g, s)
            pe_fn(g, R_, [bk, rk])
        for g in range(G):
            R_, bk, rk = region(g, s)
            cons_fn(g, R_, rk, bk)

    for grp in range(NCH // 4):
        cds = [(0, 4 * grp + j) for j in range(4)] + [(1, NCH - 1 - 4 * grp - j) for j in range(4)]
        info = []
        for g, (d, ch) in enumerate(cds):
            mL, mR, nm, ms = (mk["Lf"], mk["Rf"], mk["nmf"], mk["Lb"]) if d == 0 else (mk["Lb"], mk["Rb"], mk["nmb"], mk["Lf"])
            info.append(dict(d=d, ch=ch, cc=slice(ch * 128, (ch + 1) * 128), mL=mL, mR=mR, nm=nm, ms=ms,
                             g_col=GG[:, ch, d:d + 1], b_col=BETA[:, ch, d:d + 1], t=slots[g],
                             k=lambda n, g=g: f"{n}_{g}"))
        for I in info:
            c.ts("dve", I["t"]["E"][:, :], I["mL"][:, :], I["g_col"], None, ALU.mult, None, r=["GG", "const"], w=[I["k"]("E")])

        def pe(g, R_, w):
            I = info[g]
            c.mm(R_, I["t"]["E"][:, :], I["mR"][:, :], True, True, r=[I["k"]("E"), "const"], w=w)

        def cons(g, R_, rk, bk):
            I = info[g]
            c.tt("dve", I["t"]["E"][:, :], R_, I["nm"][:, :], ALU.add, r=[rk, "const"], w=[bk, I["k"]("E")])
            c.act(I["t"]["E"][:, :], I["t"]["E"][:, :], AF.Exp, r=[I["k"]("E")], w=[I["k"]("E")])
            c.tt("dve", I["t"]["Es"][:, :], I["t"]["E"][:, :], I["ms"][:, :], ALU.mult, r=[I["k"]("E"), "const"], w=[I["k"]("Es")])
        STEP(pe, cons)

        def pe(g, R_, w):
            I = info[g]
            c.mm(R_[:, 0:1], I["mR"][:, :], I["g_col"], True, True, r=["GG", "const"], w=w)
            c.mm(R_[:, 1:2], I["mL"][:, :], I["g_col"], True, True, r=["GG", "const"], w=w)
            c.mm(R_[:, 2:3], ones32[:, :], I["g_col"], True, True, r=["GG", "const"], w=w)

        def cons(g, R_, rk, bk):
            I = info[g]
            c.act(I["t"]["sm"][:, 0:3], R_[:, 0:3], AF.Exp, r=[rk], w=[bk, I["k"]("sm")])
        STEP(pe, cons)

        def pe(g, R_, w):
            I = info[g]
            c.mm(R_, FKZ[0:64, I["cc"]], FKZ[0:64, I["cc"]], True, True, r=["F1"], w=w)

        def cons(g, R_, rk, bk):
            I = info[g]
            c.stt("dve", I["t"]["NT"][:, :], R_, I["b_col"], I["t"]["Es"][:, :], ALU.mult, ALU.mult,
                  r=[rk, "BETA", I["k"]("Es")], w=[bk, I["k"]("NT")])
        STEP(pe, cons)

        def pe(g, R_, w):
            I = info[g]
            c.mm(R_, FKZ[0:64, I["cc"]], FQV[0:64, I["cc"]], True, True, r=["F0", "F1"], w=w)

        def cons(g, R_, rk, bk):
            I = info[g]
            c.tt("dve", I["t"]["PT"][:, :], R_, I["t"]["E"][:, :], ALU.mult, r=[rk, I["k"]("E")], w=[bk, I["k"]("PT")])
        STEP(pe, cons)

        def pe(g, R_, w):
            I = info[g]
            c.tr(R_, I["t"]["NT"][:, :], ident[:, :], r=[I["k"]("NT"), "const"], w=w)

        def cons(g, R_, rk, bk):
            I = info[g]
            c.cp("act", I["t"]["NN"][:, :], R_, r=[rk], w=[bk, I["k"]("NN")])
        STEP(pe, cons)

        def pe(g, R_, w):
            I = info[g]
            c.tr(R_, FQV[:, I["cc"]], ident[:, :], r=["F0", "const"], w=w)

        def cons(g, R_, rk, bk):
            I = info[g]
            c.cp("act", I["t"]["X0"][:, 0:64], R_[:, 64:128], r=[rk], w=[bk, I["k"]("X0")])
        STEP(pe, cons)

        def pe(g, R_, w):
            I = info[g]
            c.tr(R_, FKZ[:, I["cc"]], ident[:, :], r=["F1", "const"], w=w)

        def cons(g, R_, rk, bk):
            I = info[g]
            sm = I["t"]["sm"]
            c.ts("dve", I["t"]["X0"][:, 64:128], R_[:, 0:64], sm[:, 0:1], None, ALU.mult, None, r=[rk, I["k"]("sm")], w=[bk, I["k"]("X0")])
            c.ts("dve", I["t"]["KD"][:, :], R_[:, 0:64], sm[:, 1:2], None, ALU.mult, None, r=[rk, I["k"]("sm")], w=[bk, I["k"]("KD")])
            if I["ch"] not in done:
                c.cp("act", ZS[:, I["ch"], :], R_[:, 64:128], r=[rk], w=[bk, "ZS"])
        STEP(pe, cons)

        def pe(g, R_, w):
            I = info[g]
            c.mm(R_, I["t"]["NT"][:, :], I["t"]["X0"][:, :], True, True, r=[I["k"]("NT"), I["k"]("X0")], w=w)

        def cons(g, R_, rk, bk):
            I = info[g]
            c.tt("dve", I["t"]["X1"][:, :], I["t"]["X0"][:, :], R_, ALU.subtract, r=[I["k"]("X0"), rk], w=[bk, I["k"]("X1")])
        STEP(pe, cons)

        cur, oth = "X1", "X0"
        prevT, prevN = "NT", "NN"
        for lv in range(6):
            newT, newN = ("PTk", "Pk") if lv % 2 == 0 else ("NT", "NN")

            def pe(g, R_, w, prevT=prevT, prevN=prevN):
                I = info[g]
                c.mm(R_, I["t"][prevN][:, :], I["t"][prevT][:, :], True, True, r=[I["k"](prevT), I["k"](prevN)], w=w)

            def cons(g, R_, rk, bk, newT=newT):
                I = info[g]
                c.cp("act", I["t"][newT][:, :], R_, r=[rk], w=[bk, I["k"](newT)])
            STEP(pe, cons)
            if lv < 5:
                def pe(g, R_, w, prevT=prevT, prevN=prevN):
                    I = info[g]
                    c.mm(R_, I["t"][prevT][:, :], I["t"][prevN][:, :], True, True, r=[I["k"](prevT), I["k"](prevN)], w=w)

                def cons(g, R_, rk, bk, newN=newN):
                    I = info[g]
                    c.cp("dve", I["t"][newN][:, :], R_, r=[rk], w=[bk, I["k"](newN)])
                STEP(pe, cons)

            def pe(g, R_, w, newT=newT, cur=cur):
                I = info[g]
                c.mm(R_, I["t"][newT][:, :], I["t"][cur][:, :], True, True, r=[I["k"](newT), I["k"](cur)], w=w)

            def cons(g, R_, rk, bk, cur=cur, oth=oth):
                I = info[g]
                c.tt("dve", I["t"][oth][:, :], I["t"][cur][:, :], R_, ALU.add, r=[I["k"](cur), rk], w=[bk, I["k"](oth)])
            STEP(pe, cons)
            cur, oth = oth, cur
            prevT, prevN = newT, newN
        UWn = oth
        for I in info:
            c.ts("dve", I["t"][UWn][:, :], I["t"][cur][:, :], I["b_col"], None, ALU.mult, None, r=[I["k"](cur), "BETA"], w=[I["k"](UWn)])

        def pe(g, R_, w):
            I = info[g]
            c.tr(R_[0:64, :], I["t"][UWn][:, 64:128], ident[:, :], r=[I["k"](UWn), "const"], w=w)

        def cons(g, R_, rk, bk):
            I = info[g]
            c.cp("act", I["t"]["WT"][:, :], R_[0:64, :], r=[rk], w=[bk, I["k"]("WT")])
        STEP(pe, cons)

        for j in range(4):
            for d in range(2):
                g = d * 4 + j
                I = info[g]
                t, k_ = I["t"], I["k"]
                ch, cc = I["ch"], I["cc"]
                bA, bB, bC = ps[4 * d], ps[4 * d + 1], ps[4 * d + 2]
                kA, kB, kC = f"B{4 * d}", f"B{4 * d + 1}", f"B{4 * d + 2}"
                S_ = Sst[d]
                sm = t["sm"]
                c.mm(bA[:, 0:64], t["WT"][:, :], S_[:, :], True, True, r=[k_("WT"), f"S{d}"], w=[kA, f"pa{d}"])
                c.mm(bA[:, 128:192], FQV[0:64, cc], S_[:, :], True, True, r=["F0", f"S{d}"], w=[kA, f"po{d}"])
                c.tt("dve", t["VN"][:, :], t[UWn][:, 0:64], bA[:, 0:64], ALU.subtract, r=[k_(UWn), f"pa{d}"], w=[kA, k_("VN")])
                c.act(t["OT"][:, :], bA[:, 128:192], AF.Identity, r=[f"po{d}", k_("sm")], w=[kA, k_("OT")], scale=sm[:, 0:1])
                c.mm(bB[:, 0:64], t["PT"][:, :], t["VN"][:, :], True, True, r=[k_("PT"), k_("VN")], w=[kB, f"po2{d}"])
                c.mm(bB[0:64, 128:192], t["KD"][:, :], t["VN"][:, :], True, True, r=[k_("KD"), k_("VN")], w=[kB, f"pS{d}"])
                c.stt("dve", S_[:, :], S_[:, :], sm[0:64, 2:3], bB[0:64, 128:192], ALU.mult, ALU.add,
                      r=[f"S{d}", k_("sm"), f"pS{d}"], w=[kB, f"S{d}"])
                if ch not in done:
                    done.add(ch)
                    c.tt("dve", OACC[:, ch, :], t["OT"][:, :], bB[:, 0:64], ALU.add, r=[k_("OT"), f"po2{d}"], w=[kB, "OACC"])
                else:
                    c.tt("dve", osum[:, :], t["OT"][:, :], bB[:, 0:64], ALU.add, r=[k_("OT"), f"po2{d}"], w=[kB, "osum"])
                    c.tt("dve", osum[:, :], osum[:, :], OACC[:, ch, :], ALU.add, r=["osum", "OACC"], w=["osum"])
                    c.tt("pool", osq[:, :], osum[:, :], osum[:, :], ALU.mult, r=["osum"], w=["osq"])
                    c.S.add("dve", lambda e: e.reduce_sum(out=oss[:, 0:1], in_=osq[:, :], axis=mybir.AxisListType.X), ["osq"], ["oss"])
                    c.act(oss[:, :], oss[:, :], AF.Ln, r=["oss"], w=["oss"], scale=1.0 / 64, bias=EPS)
                    c.act(oss[:, :], oss[:, :], AF.Exp, r=["oss"], w=["oss"], scale=-0.5)
                    c.stt("dve", og[:, :], osum[:, :], oss[:, 0:1], normw[:, :], ALU.mult, ALU.mult, r=["osum", "oss", "const"], w=["og"])
                    c.tt("pool", og[:, :], og[:, :], ZS[:, ch, :], ALU.mult, r=["og", "ZS"], w=["og"])
                    c.tr(bC[0:64, 0:128], og[:, :], ident[:, :], r=["og", "const"], w=[kC, "ptrO"])
                    c.cp("act", OB[:, cc], bC[0:64, 0:128], r=["ptrO"], w=[kC, "OB"])
    for i in range(4):
        c.dma("sp", ob_d[:, i * 2048:(i + 1) * 2048], OB[:, i * 2048:(i + 1) * 2048], r=["OB"], final=True)


def maps_MD(inp, l, xnp_b):
    mk = chunk_masks()
    w = inp["w_in"][l]
    cw = inp["d_conv_w"][l]
    maps = []
    for cidx in range(NCORES):
        b, h = cidx // 4, cidx % 4
        hc = np.arange(h * 64, (h + 1) * 64)
        tqv = np.zeros((128, 4), np.float32)
        tqv[0:64, 0:3] = cw[:, hc].T
        tqv[64:128, 0:3] = cw[:, 512 + hc].T
        tkz = np.zeros((128, 4), np.float32)
        tkz[0:64, 0:3] = cw[:, 256 + hc].T
        tkz[64:128, 1] = 1.0
        scal = np.zeros((128, 6), np.float32)
        scal[:, 0] = inp["d_dt_bias"][l][0, h]
        scal[:, 1] = inp["d_dt_bias"][l][1, h]
        scal[:, 2] = inp["d_a_log"][l][0, h]
        scal[:, 3] = inp["d_a_log"][l][1, h]
        m = {
            "xnp": xnp_b[b],
            "wqv": np.ascontiguousarray(np.concatenate([w[:, OFF_D + hc], w[:, OFF_D + 512 + hc]], 1)),
            "wkz": np.ascontiguousarray(np.concatenate([w[:, OFF_D + 256 + hc], w[:, OFF_D + 768 + hc]], 1)),
            "wab": np.ascontiguousarray(w[:, [OFF_D + 1024 + h, OFF_D + 1028 + h, OFF_D + 1032 + h, OFF_D + 1036 + h]]),
            "tapsqv": tqv, "tapskz": tkz, "scal": scal,
            "normw": np.ascontiguousarray(np.broadcast_to(inp["d_out_norm"][l][None, :], (128, 64))).astype(np.float32),
        }
        for k in ("Lf", "Rf", "Lb", "Rb", "nmf", "nmb", "ident"):
            m["m_" + k] = mk[k]
        maps.append(m)
    return maps


def _standalone(emit, **kw):
    c = Ctx()
    emit(c, **kw)
    return c.finish()


def build_P(**kw):
    return _standalone(emit_P, **kw)


def build_Ta(**kw):
    return _standalone(emit_Ta, **kw)


def build_Tb(**kw):
    return _standalone(emit_Tb, **kw)


def build_MA():
    return _standalone(emit_MA)


def build_MB():
    return _standalone(emit_MB)


def build_MC():
    return _standalone(emit_MC)


def build_MD():
    return _standalone(emit_MD)


_NC_CACHE = {}


def _nc(name):
    if name not in _NC_CACHE:
        _NC_CACHE[name] = {"P": build_P, "Ta": build_Ta, "Tb": build_Tb, "MA": build_MA, "MB": build_MB,
                           "MC": build_MC, "MD": build_MD}[name]()
    return _NC_CACHE[name]


_PERM = np.array([m * 256 + h * 64 + d for h in range(4) for m in range(4) for d in range(64)])


def kernel_unfused(**inputs):
    inp = {k: np.asarray(v) for k, v in inputs.items()}
    x = inp["x"].astype(np.float32)
    NB = x.shape[0]
    NQ = SEQ // NTOK
    XT = [np.ascontiguousarray(x[b].T) for b in range(NB)]
    cq = lambda a, q: np.ascontiguousarray(a[:, q * NTOK:(q + 1) * NTOK])
    cores = [(c_ // NQ, c_ % NQ) for c_ in range(NCORES)]
    res = run(_nc("P"), [{"xt": cq(XT[b], q), "w": pk(inp["pre_mix_norm"][0])} for b, q in cores])
    XNT = [np.concatenate([res[b * NQ + q]["xnt"] for q in range(NQ)], axis=1) for b in range(NB)]
    fcw_all = inp["f_conv_w"]
    for l in range(DEPTH):
        xnp_b = [pad_seq(XNT[b]) for b in range(NB)]
        outs = []
        for name in ("MA", "MB", "MC", "MD"):
            mfn = {"MA": maps_MA, "MB": maps_MB, "MC": maps_MC, "MD": maps_MD}[name]
            outs.append(run(_nc(name), mfn(inp, l, xnp_b)))
        OT = [np.concatenate([outs[m][b * 4 + h]["ob"] for h in range(4) for m in range(4)], axis=0) for b in range(NB)]
        onw_full = np.concatenate([inp["a_out_norm"][l], inp["b_out_norm"][l], inp["c_out_norm"][l], np.ones(256, np.float32)])
        wout_p = np.ascontiguousarray(inp["w_out"][l][_PERM])
        res = run(_nc("Ta"), [{"ot": cq(OT[b], q), "xt": cq(XT[b], q), "wout": wout_p, "onw": pk(onw_full[_PERM]),
                               "postw": pk(inp["post_mix_norm"][l]), "prew": pk(inp["pre_ffn_norm"][l])} for b, q in cores])
        XMT = [np.concatenate([res[b * NQ + q]["xmt"] for q in range(NQ)], axis=1) for b in range(NB)]
        HTP = [pad_seq(np.concatenate([res[b * NQ + q]["ht"] for q in range(NQ)], axis=1)) for b in range(NB)]
        fcw_l = np.ascontiguousarray(fcw_all[l].T.reshape(44, 128, 3).transpose(1, 0, 2))
        fcb_l = np.ascontiguousarray(inp["f_conv_b"][l].reshape(44, 128).T)
        nextw = inp["pre_mix_norm"][min(l + 1, DEPTH - 1)]
        res = run(_nc("Tb"), [{"hth": np.ascontiguousarray(HTP[b][:, q * NTOK:q * NTOK + NTOK + 2]), "xmt": cq(XMT[b], q),
                               "w1": inp["f_w_in"][l], "w2": inp["f_w_out"][l], "fcw": fcw_l, "fcb": fcb_l,
                               "postw": pk(inp["post_ffn_norm"][l]), "nextw": pk(nextw)} for b, q in cores])
        XT = [np.concatenate([res[b * NQ + q]["xo"] for q in range(NQ)], axis=1) for b in range(NB)]
        XNT = [np.concatenate([res[b * NQ + q]["xnt"] for q in range(NQ)], axis=1) for b in range(NB)]
    return np.stack([np.ascontiguousarray(XT[b].T) for b in range(NB)]).astype(np.float32)


_MIX = (("MA", None), ("MB", None), ("MC", None), ("MD", None))


def build_fused():
    emits = {"MA": emit_MA, "MB": emit_MB, "MC": emit_MC, "MD": emit_MD}
    c = Ctx()
    c.psum_banks()
    x0 = c.din("xt0", [D_MODEL, SEQ], F32)
    xout = c.dout("xout", [D_MODEL, SEQ], F32)
    XNP = c.scratch("XNP", [D_MODEL, SEQ + 2], BF16)
    HTP = c.scratch("HTP", [D_MODEL, SEQ + 2], BF16)
    OT = c.scratch("OT", [D_MODEL, SEQ], BF16)
    XMT = c.scratch("XMT", [D_MODEL, SEQ], F32)
    X1 = c.scratch("X1", [D_MODEL, SEQ], F32)
    XND = c.scratch("XND", [D_MODEL, SEQ], BF16)
    c.begin_phase("Z_", {})
    z = c.sb("z", [128, 8, 1], BF16)
    c.memset("pool", z[:, :, :], 0.0, w=["z"])
    for T in (XNP, HTP):
        for col in (0, SEQ + 1):
            c.dma("sp", T.rearrange("(k p) t -> p k t", p=128)[:, :, col:col + 1], z[:, :, :], r=["z"], slow=True)
    c.end_phase()
    c.begin_phase("P_", {"xt": x0, "xnt": XNP[:, 1:SEQ + 1]})
    emit_P(c, ncols=SEQ)
    c.end_phase()
    xin = x0
    for l in range(DEPTH):
        for m, nm in enumerate(("MA", "MB", "MC", "MD")):
            for h in range(4):
                c.begin_phase(f"L{l}{nm}{h}_", {"xnp": XNP, "ob": OT[h * 256 + m * 64:h * 256 + (m + 1) * 64, :]})
                emits[nm](c)
                c.end_phase()
        c.begin_phase(f"L{l}Ta_", {"ot": OT, "xt": xin, "xmt": XMT, "ht": HTP[:, 1:SEQ + 1]})
        emit_Ta(c, ncols=SEQ)
        c.end_phase()
        xo = xout if l == DEPTH - 1 else X1
        c.begin_phase(f"L{l}Tb_", {"hth": HTP, "xmt": XMT, "xo": xo, "xnt": XNP[:, 1:SEQ + 1] if l < DEPTH - 1 else XND})
        emit_Tb(c, ncols=SEQ)
        c.end_phase()
        xin = xo
    return c.finish()


def fused_maps(inp):
    x = inp["x"].astype(np.float32)
    mfns = {"MA": maps_MA, "MB": maps_MB, "MC": maps_MC, "MD": maps_MD}
    maps = []
    per_layer = []
    for l in range(DEPTH):
        per_layer.append({nm: fn(inp, l, [None, None]) for nm, fn in mfns.items()})
    for b in range(x.shape[0]):
        m = {"xt0": np.ascontiguousarray(x[b].T), "P_w": pk(inp["pre_mix_norm"][0])}
        for l in range(DEPTH):
            for nm in mfns:
                for h in range(4):
                    for k, v in per_layer[l][nm][b * 4 + h].items():
                        if k != "xnp":
                            m[f"L{l}{nm}{h}_{k}"] = v
            onw_full = np.concatenate([inp["a_out_norm"][l], inp["b_out_norm"][l], inp["c_out_norm"][l], np.ones(256, np.float32)])
            m[f"L{l}Ta_wout"] = np.ascontiguousarray(inp["w_out"][l][_PERM])
            m[f"L{l}Ta_onw"] = pk(onw_full[_PERM])
            m[f"L{l}Ta_postw"] = pk(inp["post_mix_norm"][l])
            m[f"L{l}Ta_prew"] = pk(inp["pre_ffn_norm"][l])
            m[f"L{l}Tb_w1"] = inp["f_w_in"][l]
            m[f"L{l}Tb_w2"] = inp["f_w_out"][l]
            m[f"L{l}Tb_fcw"] = np.ascontiguousarray(inp["f_conv_w"][l].T.reshape(44, 128, 3).transpose(1, 0, 2))
            m[f"L{l}Tb_fcb"] = np.ascontiguousarray(inp["f_conv_b"][l].reshape(44, 128).T)
            m[f"L{l}Tb_postw"] = pk(inp["post_ffn_norm"][l])
            m[f"L{l}Tb_nextw"] = pk(inp["pre_mix_norm"][min(l + 1, DEPTH - 1)])
        maps.append(m)
    return maps


def kernel(**inputs):
    inp = {k: np.asarray(v) for k, v in inputs.items()}
    if "F" not in _NC_CACHE:
        _NC_CACHE["F"] = build_fused()
    res = run(_NC_CACHE["F"], fused_maps(inp))
    return np.stack([np.ascontiguousarray(res[b]["xout"].T) for b in range(len(res))]).astype(np.float32)
```

```python
import contextlib
import numpy as np
import ml_dtypes
import concourse.bass as bass
import concourse.mybir as mybir
from concourse.bass_utils import run_bass_kernel_spmd

F32 = mybir.dt.float32
BF16 = mybir.dt.bfloat16
ALU = mybir.AluOpType
AF = mybir.ActivationFunctionType

D_MODEL = 1024
SEQ = 8192
DEPTH = 2
EPS = 1e-6
D_FF = 2816
NCORES = 8
NTOK = 2048

ENGS = ("pe", "act", "dve", "pool", "sp")
N_DMA_SEMS = 12


class Op:
    __slots__ = ("eng", "fn", "deps", "signal", "semkey", "semval", "is_dma", "dma_slot")

    def __init__(self, eng, fn, is_dma=False):
        self.eng = eng
        self.fn = fn
        self.deps = []
        self.signal = False
        self.semkey = None
        self.semval = None
        self.is_dma = is_dma
        self.dma_slot = None


class Sched:
    def __init__(self, nc):
        self.nc = nc
        self.ops = {e: [] for e in ENGS}
        self.writers = {}
        self.readers = {}
        self.dma_count = {e: 0 for e in ENGS}
        self.dma_last = {}
        self.last_op = {}

    def add(self, eng, fn, reads=(), writes=(), is_dma=False):
        op = Op(eng, fn, is_dma)
        if is_dma:
            n = self.dma_count[eng]
            self.dma_count[eng] = n + 1
            op.dma_slot = n % N_DMA_SEMS
            prev = self.dma_last.get((eng, op.dma_slot))
            if prev is not None:
                op.deps.append(prev)
            self.dma_last[(eng, op.dma_slot)] = op
        deps = op.deps
        for k in reads:
            deps.extend(self.writers.get(k, {}).values())
        for k in writes:
            deps.extend(self.writers.get(k, {}).values())
            deps.extend(self.readers.get(k, {}).values())
        tk = (eng, op.dma_slot) if is_dma else eng
        for k in reads:
            self.readers.setdefault(k, {})[tk] = op
        for k in writes:
            self.writers.setdefault(k, {})[tk] = op
            self.readers[k] = {}
        if eng == "pe" and not is_dma:
            op.deps = [d for d in deps if not (d.eng == "pe" and not d.is_dma)]
        self.ops[eng].append(op)
        self.last_op[tk] = op
        return op

    def barrier(self):
        lasts = list(self.last_op.values())
        for e in ENGS:
            if e == "sp" or self.ops[e]:
                op = Op(e, None)
                op.deps = [d for d in lasts]
                self.ops[e].append(op)
                self.last_op[e] = op
        self.writers = {}
        self.readers = {}

    def finalize(self, final_waits=()):
        nc = self.nc
        for e in ENGS:
            for op in self.ops[e]:
                for d in op.deps:
                    d.signal = True
        for op in final_waits:
            op.signal = True
        with contextlib.ExitStack() as st:
            esem = {e: st.enter_context(nc.semaphore(f"s_{e}")) for e in ENGS}
            dsem = {}
            for e in ENGS:
                for i in range(min(N_DMA_SEMS, self.dma_count[e])):
                    dsem[(e, i)] = st.enter_context(nc.semaphore(f"d_{e}{i}"))
            for e in ENGS:
                c = 0
                dc = {}
                for op in self.ops[e]:
                    if op.is_dma:
                        k = (e, op.dma_slot)
                        dc[k] = dc.get(k, 0) + 16
                        op.semkey = ("d", k)
                        op.semval = dc[k]
                    elif op.signal and op.fn is not None:
                        c += 1
                        op.semkey = ("e", e)
                        op.semval = c
                    elif op.signal:
                        c += 1
                        op.semkey = ("e", e)
                        op.semval = c
            block = st.enter_context(nc.Block())

            def semof(key):
                return esem[key[1]] if key[0] == "e" else dsem[key[1]]

            def run(e, eng):
                waited = {}
                for op in self.ops[e]:
                    need = {}
                    for d in op.deps:
                        if d.semkey is None:
                            continue
                        if waited.get(d.semkey, 0) >= d.semval:
                            continue
                        if need.get(d.semkey, 0) < d.semval:
                            need[d.semkey] = d.semval
                    for k, v in need.items():
                        eng.wait_ge(semof(k), v)
                        waited[k] = v
                    if op.fn is None:
                        if op.signal:
                            eng.sem_inc(semof(op.semkey), 1)
                        continue
                    ins = op.fn(eng)
                    if op.is_dma:
                        ins.then_inc(semof(op.semkey), 16)
                    elif op.signal:
                        ins.then_inc(semof(op.semkey), 1)
                if e == "sp":
                    for op in final_waits:
                        if waited.get(op.semkey, 0) < op.semval:
                            eng.wait_ge(semof(op.semkey), op.semval)
                            waited[op.semkey] = op.semval

            @block.sync
            def _(eng):
                run("sp", eng)

            @block.scalar
            def _(eng):
                run("act", eng)

            @block.vector
            def _(eng):
                run("dve", eng)

            @block.gpsimd
            def _(eng):
                run("pool", eng)

            @block.tensor
            def _(eng):
                run("pe", eng)


class Ctx:
    def __init__(self):
        self.nc = bass.Bass("TRN2", target_bir_lowering=False)
        self.S = Sched(self.nc)
        self.st = contextlib.ExitStack()
        self.finals = []
        self.ps_banks = None
        self._n = 0
        self.io = {}
        self.prefix = ""
        self.phase_st = None

    def begin_phase(self, prefix, io):
        self.prefix = prefix
        self.io = dict(io)
        self.phase_st = contextlib.ExitStack()

    def end_phase(self):
        self.S.barrier()
        self.phase_st.close()
        self.phase_st = None
        self.io = {}

    def din(self, name, shape, dt):
        if name in self.io:
            ap = self.io[name]
            assert list(ap.shape) == list(shape), (name, ap.shape, shape)
            return ap
        return self.nc.dram_tensor(self.prefix + name, list(shape), dt, kind="ExternalInput").ap()

    def dout(self, name, shape, dt):
        if name in self.io:
            ap = self.io[name]
            assert list(ap.shape) == list(shape), (name, ap.shape, shape)
            return ap
        return self.nc.dram_tensor(self.prefix + name, list(shape), dt, kind="ExternalOutput").ap()

    def scratch(self, name, shape, dt):
        return self.nc.dram_tensor(name, list(shape), dt, kind="Internal").ap()

    def sb(self, name, shape, dt, st=None):
        return (st or self.phase_st or self.st).enter_context(self.nc.sbuf_tensor(self.prefix + name, list(shape), dt))

    def psum_banks(self):
        if self.ps_banks is None:
            self.ps_banks = [self.st.enter_context(self.nc.psum_tensor(f"psb{i}", [128, 512], F32)) for i in range(8)]
        return self.ps_banks

    def dma(self, q, out, in_, r=(), w=(), final=False, slow=False):
        if slow:
            op = self.S.add(q, lambda e: e.dma_start(out=out, in_=in_, allow_slow_non_contiguous=True), r, w, is_dma=True)
        else:
            op = self.S.add(q, lambda e: e.dma_start(out=out, in_=in_), r, w, is_dma=True)
        if final:
            self.finals.append(op)
        return op

    def mm(self, out, lhsT, rhs, start, stop, r=(), w=()):
        return self.S.add("pe", lambda e: e.matmul(out, lhsT=lhsT, rhs=rhs, start=start, stop=stop), r, w)

    def tr(self, out, in_, ident, r=(), w=()):
        return self.S.add("pe", lambda e: e.transpose(out, in_, ident), r, w)

    def act(self, out, in_, func, r=(), w=(), scale=1.0, bias=0.0):
        return self.S.add("act", lambda e: e.activation(out=out, in_=in_, func=func, bias=bias, scale=scale), r, w)

    def ts(self, eng, out, in0, s1, s2, op0, op1, r=(), w=()):
        if s2 is None:
            return self.S.add(eng, lambda e: e.tensor_single_scalar(out=out, in_=in0, scalar=s1, op=op0), r, w)
        return self.S.add(eng, lambda e: e.tensor_scalar(out=out, in0=in0, scalar1=s1, scalar2=s2, op0=op0, op1=op1), r, w)

    def stt(self, eng, out, in0, scalar, in1, op0, op1, r=(), w=()):
        return self.S.add(eng, lambda e: e.scalar_tensor_tensor(out=out, in0=in0, scalar=scalar, in1=in1, op0=op0, op1=op1), r, w)

    def tt(self, eng, out, in0, in1, op, r=(), w=()):
        return self.S.add(eng, lambda e: e.tensor_tensor(out=out, in0=in0, in1=in1, op=op), r, w)

    def cp(self, eng, out, in_, r=(), w=()):
        if eng == "act":
            return self.S.add(eng, lambda e: e.copy(out=out, in_=in_), r, w)
        return self.S.add(eng, lambda e: e.tensor_copy(out=out, in_=in_), r, w)

    def memset(self, eng, ap, val, w=()):
        return self.S.add(eng, lambda e: e.memset(ap, val), (), w)

    def recip(self, out, in_, r=(), w=()):
        return self.S.add("dve", lambda e: e.reciprocal(out=out, in_=in_), r, w)

    def finish(self):
        self.S.finalize(final_waits=self.finals)
        self.st.close()
        return self.nc

    def uid(self, p):
        self._n += 1
        return f"{p}{self._n}"


def rstd_from_ss(c, out_sb, ss_ps, inv_d, eps, r, w, rows=slice(0, 128)):
    c.act(out_sb[rows], ss_ps[rows], AF.Ln, r=r, w=w, scale=inv_d, bias=eps)
    c.act(out_sb[rows], out_sb[rows], AF.Exp, r=w, w=w, scale=-0.5)


def pk(v):
    v = np.asarray(v, np.float32)
    return np.ascontiguousarray(v.reshape(-1, 128).T)


def bf(a):
    return np.asarray(a).astype(ml_dtypes.bfloat16)


def run(nc, in_maps):
    res = run_bass_kernel_spmd(nc, in_maps, core_ids=list(range(len(in_maps))))
    return res.results


def emit_norm_block(c, tag, src, nch, n, wt, ones_bf, ssps, sq, rstd, dst, inv_d, slot, pre_keys, eng_sq=("act", "pool")):
    ksq, krs = f"{tag}sq{slot}", f"{tag}rstd{slot}"
    for k in range(nch):
        e = eng_sq[k % len(eng_sq)]
        if e == "act":
            c.act(sq[:, k, :n], src[:, k, :n], AF.Square, r=pre_keys, w=[ksq])
        else:
            c.tt(e, sq[:, k, :n], src[:, k, :n], src[:, k, :n], ALU.mult, r=pre_keys, w=[ksq])
    for k in range(nch):
        c.mm(ssps[:, :n], ones_bf[:, :], sq[:, k, :n], k == 0, k == nch - 1, r=[ksq, "const"], w=[f"{tag}ssps{slot}"])
    rstd_from_ss(c, rstd[:, :n], ssps[:, :n], inv_d, EPS, r=[f"{tag}ssps{slot}"], w=[krs])
    return krs


def emit_P(c, ncols=NTOK):
    xt = c.din("xt", [D_MODEL, ncols], F32).rearrange("(k p) t -> p k t", p=128)
    w = c.din("w", [128, 8], F32)
    xnt = c.dout("xnt", [D_MODEL, ncols], BF16).rearrange("(k p) t -> p k t", p=128)
    ps = c.psum_banks()
    wt = c.sb("wt", [128, 8], F32)
    ones = c.sb("ones", [128, 128], BF16)
    c.memset("pool", ones[:, :], 1.0, w=["const"])
    c.dma("sp", wt[:, :], w, w=["const"])
    NB = 2
    xs = [c.sb(f"xs{i}", [128, 8, 512], F32) for i in range(NB)]
    sq = [c.sb(f"sq{i}", [128, 8, 512], BF16) for i in range(NB)]
    rs = [c.sb(f"rs{i}", [128, 512], F32) for i in range(NB)]
    xo = [c.sb(f"xo{i}", [128, 8, 512], BF16) for i in range(NB)]
    def load(b):
        c.dma("sp", xs[b % NB][:, :, :], xt[:, :, b * 512:(b + 1) * 512], w=[f"xs{b % NB}"])

    load(0)
    for b in range(ncols // 512):
        s = b % NB
        cols = slice(b * 512, (b + 1) * 512)
        if b + 1 < ncols // 512:
            load(b + 1)
        krs = emit_norm_block(c, "p", xs[s], 8, 512, wt, ones, ps[s], sq[s], rs[s], None, 1.0 / D_MODEL, s, [f"xs{s}"], eng_sq=("act",))
        for k in range(8):
            c.stt("dve", xo[s][:, k, :], xs[s][:, k, :], wt[:, k:k + 1], rs[s][:, :], ALU.mult, ALU.mult,
                  r=[f"xs{s}", krs, "const"], w=[f"xo{s}"])
        c.dma("sp", xnt[:, :, cols], xo[s][:, :, :], r=[f"xo{s}"], final=True)


def emit_Ta(c, ncols=NTOK):
    ot = c.din("ot", [D_MODEL, ncols], BF16).rearrange("(k p) t -> p k t", p=128)
    xt = c.din("xt", [D_MODEL, ncols], F32).rearrange("(k p) t -> p k t", p=128)
    wout = c.din("wout", [D_MODEL, D_MODEL], F32).rearrange("(k p) n -> p k n", p=128)
    onw_d = c.din("onw", [128, 8], F32)
    postw_d = c.din("postw", [128, 8], F32)
    prew_d = c.din("prew", [128, 8], F32)
    xmt = c.dout("xmt", [D_MODEL, ncols], F32).rearrange("(k p) t -> p k t", p=128)
    ht = c.dout("ht", [D_MODEL, ncols], BF16).rearrange("(k p) t -> p k t", p=128)
    ps = c.psum_banks()
    onw = c.sb("onw_s", [128, 8], F32)
    postw = c.sb("postw_s", [128, 8], F32)
    prew = c.sb("prew_s", [128, 8], F32)
    ones = c.sb("ones", [128, 128], BF16)
    bd2 = c.sb("bd2", [128, 128], BF16)
    top = c.sb("top", [128, 128], BF16)
    c.memset("pool", ones[:, :], 1.0, w=["const"])
    c.memset("pool", bd2[:, :], 0.0, w=["const"])
    c.memset("pool", bd2[0:64, 0:64], 1.0, w=["const"])
    c.memset("pool", bd2[64:128, 64:128], 1.0, w=["const"])
    c.memset("pool", top[:, :], 0.0, w=["const"])
    c.memset("pool", top[0:64, :], 1.0, w=["const"])
    for t, d in ((onw, onw_d), (postw, postw_d), (prew, prew_d)):
        c.dma("sp", t[:, :], d, w=["const"])
    wo = c.sb("wo", [128, 8, D_MODEL], BF16)
    stg = [c.sb(f"stg{i}", [128, 2, D_MODEL], F32) for i in range(2)]
    for i in range(4):
        s = i % 2
        c.dma("pool", stg[s][:, :, :], wout[:, 2 * i:2 * i + 2, :], w=[f"stg{s}"])
        c.cp("pool" if i % 2 else "act", wo[:, 2 * i:2 * i + 2, :], stg[s][:, :, :], r=[f"stg{s}"], w=["wo"])
    B = []
    for p in range(2):
        B.append(dict(
            ots=c.sb(f"ots{p}", [128, 8, 512], BF16), xs=c.sb(f"xs{p}", [128, 8, 512], F32), sq=c.sb(f"sq{p}", [128, 8, 512], BF16),
            on=c.sb(f"on{p}", [128, 8, 512], BF16), ys=c.sb(f"ys{p}", [128, 8, 512], F32), hs=c.sb(f"hs{p}", [128, 8, 512], BF16),
            rsAB=c.sb(f"rsAB{p}", [128, 512], F32), rsC=c.sb(f"rsC{p}", [128, 512], F32), rs2=c.sb(f"rs2{p}", [128, 512], F32),
            rs3=c.sb(f"rs3{p}", [128, 512], F32)))
    def load_ta(b):
        q = b % 2
        c.dma("sp", B[q]["ots"][:, :, :], ot[:, :, b * 512:(b + 1) * 512], w=[f"ots{q}"])
        c.dma("sp", B[q]["xs"][:, :, :], xt[:, :, b * 512:(b + 1) * 512], w=[f"xs{q}"])

    for b in range(ncols // 512):
        p = b % 2
        T = B[p]
        ots, xs, sq, on, ys, hs = T["ots"], T["xs"], T["sq"], T["on"], T["ys"], T["hs"]
        rsAB, rsC, rs2, rs3 = T["rsAB"], T["rsC"], T["rs2"], T["rs3"]
        K_ = lambda n: f"{n}{p}"
        pA, pB, pO = ps[4 * p], ps[4 * p + 1], (ps[4 * p + 2], ps[4 * p + 3])
        kA, kB, kO = f"bk{p}a", f"bk{p}b", (f"bk{p}c", f"bk{p}d")
        cols = slice(b * 512, (b + 1) * 512)
        if b == 0:
            load_ta(0)
        if b + 1 < ncols // 512:
            load_ta(b + 1)
        for k in range(8):
            c.act(sq[:, k, :], ots[:, k, :], AF.Square, r=[K_("ots")], w=[K_("sq")])
        for i, k in enumerate((0, 2, 4, 6)):
            c.mm(pA[:, :], bd2[:, :], sq[:, k, :], i == 0, i == 3, r=[K_("sq"), "const"], w=[kA])
        for i, k in enumerate((1, 3, 5, 7)):
            c.mm(pB[:, :], top[:, :], sq[:, k, :], i == 0, i == 3, r=[K_("sq"), "const"], w=[kB])
        rstd_from_ss(c, rsAB, pA, 1.0 / 256, EPS, r=[kA], w=[kA, K_("rsAB")])
        rstd_from_ss(c, rsC, pB, 1.0 / 256, EPS, r=[kB], w=[kB, K_("rsC")])
        for k in range(8):
            if k % 2 == 0:
                c.stt("dve", on[:, k, :], ots[:, k, :], onw[:, k:k + 1], rsAB[:, :], ALU.mult, ALU.mult,
                      r=[K_("ots"), K_("rsAB"), "const"], w=[K_("on")])
            else:
                c.stt("dve", on[0:64, k, :], ots[0:64, k, :], onw[0:64, k:k + 1], rsC[0:64, :], ALU.mult, ALU.mult,
                      r=[K_("ots"), K_("rsC"), "const"], w=[K_("on")])
                c.cp("dve", on[64:128, k, :], ots[64:128, k, :], r=[K_("ots")], w=[K_("on")])
        for m in range(8):
            pb, kb = pO[m % 2], kO[m % 2]
            for k in range(8):
                c.mm(pb[:, :], wo[:, k, m * 128:(m + 1) * 128], on[:, k, :], k == 0, k == 7, r=[K_("on"), "wo"], w=[kb])
            c.cp("act", ys[:, m, :], pb[:, :], r=[kb], w=[kb, K_("ys")])
            c.act(sq[:, m, :], pb[:, :], AF.Square, r=[kb], w=[kb, K_("sq")])
        for k in range(8):
            c.mm(pA[:, :], ones[:, :], sq[:, k, :], k == 0, k == 7, r=[K_("sq"), "const"], w=[kA])
        rstd_from_ss(c, rs2, pA, 1.0 / D_MODEL, EPS, r=[kA], w=[kA, K_("rs2")])
        for k in range(8):
            c.stt("dve", ys[:, k, :], ys[:, k, :], postw[:, k:k + 1], rs2[:, :], ALU.mult, ALU.mult,
                  r=[K_("ys"), K_("rs2"), "const"], w=[K_("ys")])
            c.tt("dve", xs[:, k, :], xs[:, k, :], ys[:, k, :], ALU.add, r=[K_("ys"), K_("xs")], w=[K_("xs")])
        c.dma("sp", xmt[:, :, cols], xs[:, :, :], r=[K_("xs")], final=True)
        for k in range(8):
            c.act(sq[:, k, :], xs[:, k, :], AF.Square, r=[K_("xs")], w=[K_("sq")])
        for k in range(8):
            c.mm(pB[:, :], ones[:, :], sq[:, k, :], k == 0, k == 7, r=[K_("sq"), "const"], w=[kB])
        rstd_from_ss(c, rs3, pB, 1.0 / D_MODEL, EPS, r=[kB], w=[kB, K_("rs3")])
        for k in range(8):
            c.stt("dve", hs[:, k, :], xs[:, k, :], prew[:, k:k + 1], rs3[:, :], ALU.mult, ALU.mult,
                  r=[K_("xs"), K_("rs3"), "const"], w=[K_("hs")])
        c.dma("sp", ht[:, :, cols], hs[:, :, :], r=[K_("hs")], final=True)


def emit_Tb(c, ncols=NTOK, TB=256):
    hth = c.din("hth", [D_MODEL, ncols + 2], BF16).rearrange("(k p) t -> p k t", p=128)
    xmt = c.din("xmt", [D_MODEL, ncols], F32).rearrange("(k p) t -> p k t", p=128)
    w1 = c.din("w1", [D_MODEL, 2 * D_FF], F32).rearrange("(k p) n -> p k n", p=128)
    w2 = c.din("w2", [D_FF, D_MODEL], F32).rearrange("(k p) n -> p k n", p=128)
    fcw_d = c.din("fcw", [128, 44, 3], F32)
    fcb_d = c.din("fcb", [128, 44], F32)
    postw_d = c.din("postw", [128, 8], F32)
    nextw_d = c.din("nextw", [128, 8], F32)
    xo = c.dout("xo", [D_MODEL, ncols], F32).rearrange("(k p) t -> p k t", p=128)
    xnt = c.dout("xnt", [D_MODEL, ncols], BF16).rearrange("(k p) t -> p k t", p=128)
    ps = c.psum_banks()
    fcw = c.sb("fcw_s", [128, 44, 3], F32)
    fcb = c.sb("fcb_s", [128, 44], F32)
    postw = c.sb("postw_s", [128, 8], F32)
    nextw = c.sb("nextw_s", [128, 8], F32)
    ones = c.sb("ones", [128, 128], BF16)
    c.memset("pool", ones[:, :], 1.0, w=["const"])
    for t, d in ((fcw, fcw_d), (fcb, fcb_d), (postw, postw_d), (nextw, nextw_d)):
        c.dma("sp", t[:], d, w=["const"])
    w1b = c.sb("w1b", [128, 8, 2 * D_FF], BF16)
    w2b = c.sb("w2b", [128, 22, D_MODEL], BF16)
    stg = [c.sb(f"stg{i}", [128, 1408], F32) for i in range(2)]
    n = 0
    for k in range(8):
        for c0 in range(0, 2 * D_FF, 1408):
            c1 = c0 + 1408
            s = n % 2
            c.dma("sp" if n % 2 else "pool", stg[s][:, :], w1[:, k, c0:c1], w=[f"stg{s}"])
            c.cp(("pool", "act", "dve")[n % 3], w1b[:, k, c0:c1], stg[s][:, :], r=[f"stg{s}"], w=[f"w1b{k}"])
            n += 1
    for j in range(22):
        s = n % 2
        c.dma("sp" if n % 2 else "pool", stg[s][:, :D_MODEL], w2[:, j, :], w=[f"stg{s}"])
        c.cp(("pool", "act", "dve")[n % 3], w2b[:, j, :], stg[s][:, :D_MODEL], r=[f"stg{s}"], w=["w2b"])
        n += 1
    w1keys = [f"w1b{k}" for k in range(8)]
    hb = [c.sb(f"hb{i}", [128, 8, TB + 2], BF16) for i in range(2)]
    xm = [c.sb(f"xm{i}", [128, 8, TB], F32) for i in range(2)]
    acts = c.sb("acts", [128, 22, TB], BF16)
    ag = [c.sb(f"ag{i}", [128, TB], F32) for i in range(2)]
    au = [c.sb(f"au{i}", [128, TB], F32) for i in range(2)]
    sg = [c.sb(f"sg{i}", [128, TB], F32) for i in range(2)]
    ys = c.sb("ys", [128, 8, TB], F32)
    sq = c.sb("sq", [128, 8, TB], BF16)
    rs2 = c.sb("rs2", [128, TB], F32)
    rs3 = c.sb("rs3", [128, TB], F32)
    xn = c.sb("xn", [128, 8, TB], BF16)
    nblk = ncols // TB
    pending_tail = []
    for b in range(nblk):
        s = b % 2
        c.dma("sp", hb[s][:, :, :], hth[:, :, b * TB:b * TB + TB + 2], w=[f"hb{s}"])
        c.dma("sp", xm[s][:, :, :], xmt[:, :, b * TB:(b + 1) * TB], w=[f"xm{s}"])
        for j in range(22):
            q = j % 2
            if j == 4 and pending_tail:
                pending_tail.pop(0)()
            for which, col0, acc, pb, pk_ in (("g", j * 128, ag[q], ps[2 * q], f"psg{q}"),
                                             ("u", D_FF + j * 128, au[q], ps[2 * q + 1], f"psu{q}")):
                jj = j if which == "g" else 22 + j
                for k in range(8):
                    c.mm(pb[:, :TB + 2], w1b[:, k, col0:col0 + 128], hb[s][:, k, :], k == 0, k == 7,
                         r=[f"hb{s}", f"w1b{k}"], w=[pk_])
                ak = f"a{which}{q}"
                c.act(acc[:, :], pb[:, 1:TB + 1], AF.Identity, r=[pk_, "const"], w=[ak],
                      scale=fcw[:, jj, 1:2], bias=fcb[:, jj:jj + 1])
                c.stt("dve", acc[:, :], pb[:, 0:TB], fcw[:, jj, 0:1], acc[:, :], ALU.mult, ALU.add,
                      r=[pk_, ak, "const"], w=[ak])
                c.stt("dve", acc[:, :], pb[:, 2:TB + 2], fcw[:, jj, 2:3], acc[:, :], ALU.mult, ALU.add,
                      r=[pk_, ak, "const"], w=[ak])
            c.act(sg[q][:, :], ag[q][:, :], AF.Silu, r=[f"ag{q}"], w=[f"sg{q}"])
            c.tt("dve", acts[:, j, :], sg[q][:, :], au[q][:, :], ALU.mult, r=[f"sg{q}", f"au{q}"], w=["acts"])
        for m in range(8):
            pb = ps[4 + m % 2]
            for j in range(22):
                c.mm(pb[:, :TB], w2b[:, j, m * 128:(m + 1) * 128], acts[:, j, :], j == 0, j == 21,
                     r=["acts", "w2b"], w=[f"pso{m % 2}"])
            c.cp("act", ys[:, m, :], pb[:, :TB], r=[f"pso{m % 2}"], w=["ys"])
            c.act(sq[:, m, :], pb[:, :TB], AF.Square, r=[f"pso{m % 2}"], w=["sq"])
        def tail(b=b, s=s):
            for k in range(8):
                c.mm(ps[6][:, :TB], ones[:, :], sq[:, k, :], k == 0, k == 7, r=["sq", "const"], w=["ps6"])
            rstd_from_ss(c, rs2, ps[6][:, :TB], 1.0 / D_MODEL, EPS, r=["ps6"], w=["rs2"])
            for k in range(8):
                c.stt("dve", ys[:, k, :], ys[:, k, :], postw[:, k:k + 1], rs2[:, :], ALU.mult, ALU.mult,
                      r=["ys", "rs2", "const"], w=["ys"])
                c.tt("dve", ys[:, k, :], ys[:, k, :], xm[s][:, k, :], ALU.add, r=["ys", f"xm{s}"], w=["ys"])
            c.dma("sp", xo[:, :, b * TB:(b + 1) * TB], ys[:, :, :], r=["ys"], final=True)
            for k in range(8):
                c.act(sq[:, k, :], ys[:, k, :], AF.Square, r=["ys"], w=["sq"])
            for k in range(8):
                c.mm(ps[7][:, :TB], ones[:, :], sq[:, k, :], k == 0, k == 7, r=["sq", "const"], w=["ps7"])
            rstd_from_ss(c, rs3, ps[7][:, :TB], 1.0 / D_MODEL, EPS, r=["ps7"], w=["rs3"])
            for k in range(8):
                c.stt("dve", xn[:, k, :], ys[:, k, :], nextw[:, k:k + 1], rs3[:, :], ALU.mult, ALU.mult,
                      r=["ys", "rs3", "const"], w=["xn"])
            c.dma("sp", xnt[:, :, b * TB:(b + 1) * TB], xn[:, :, :], r=["xn"], final=True)
        pending_tail.append(tail)
    while pending_tail:
        pending_tail.pop(0)()


def attention_core(c, QT, KT, VA, dk, scale, OB, ps, pT, r64, osb, onesrow):
    NQ, NK = SEQ // 512, SEQ // 128
    its = [(qb, kc) for qb in range(NQ) for kc in range(NK)]

    def s_mm(i):
        qb, kc = its[i]
        c.mm(ps[i % 3][:, :], KT[0:dk, kc * 128:(kc + 1) * 128], QT[0:dk, qb * 512:(qb + 1) * 512], True, True,
             r=["QT", "KT"], w=[f"sT{i % 3}"])

    pending = []

    def fin1(qb):
        ob = ps[4 + qb % 2]
        c.cp("act", r64[64:65, :], ob[64:65, :], r=[f"oacc{qb % 2}"], w=["r64"])
        c.recip(r64[64:65, :], r64[64:65, :], r=["r64"], w=["r64"])
        c.cp("act", osb[:, :], ob[0:64, :], r=[f"oacc{qb % 2}"], w=["osb"])

    def fin2(qb):
        c.mm(ps[6][0:64, :], onesrow[64:65, 0:64], r64[64:65, :], True, True, r=["r64", "const"], w=["bc"])
        c.tt("dve", OB[:, qb * 512:(qb + 1) * 512], osb[:, :], ps[6][0:64, :], ALU.mult, r=["osb", "bc"], w=["OB"])

    s_mm(0)
    s_mm(1)
    for i, (qb, kc) in enumerate(its):
        c.act(pT[i % 3][:, :], ps[i % 3][:, :], AF.Exp, r=[f"sT{i % 3}"], w=[f"pT{i % 3}"], scale=scale)
        if i + 2 < len(its):
            s_mm(i + 2)
        c.mm(ps[4 + qb % 2][0:65, :], VA[:, kc, :], pT[i % 3][:, :], kc == 0, kc == NK - 1,
             r=[f"pT{i % 3}", "VA"], w=[f"oacc{qb % 2}"])
        if FILLER:
            c.mm(ps[7][:, 0:FILLER], KT[0:dk, 0:128], QT[0:dk, 0:FILLER], True, True, r=["QT", "KT"], w=["filler"])
        for p in [p for p in pending if p[0] == i]:
            p[1]()
        pending = [p for p in pending if p[0] != i]
        if kc == NK - 1:
            fin1(qb)
            if i + 6 < len(its):
                pending.append((i + 6, lambda qb=qb: fin2(qb)))
            else:
                fin2(qb)


def load_cast(c, dst, src_ap, shape, tag, q="sp", eng="pool", wkey="wts"):
    stg = c.sb(c.uid(tag + "_stg"), shape, F32)
    c.dma(q, stg[:], src_ap, w=[tag + "stg"])
    c.cp(eng, dst, stg[:], r=[tag + "stg"], w=[wkey])


def emit_MB(c):
    xnp = c.din("xnp", [D_MODEL, SEQ + 2], BF16).rearrange("(k p) t -> p k t", p=128)
    wq = c.din("wq", [D_MODEL, 64], F32).rearrange("(k p) n -> p k n", p=128)
    wk = c.din("wk", [D_MODEL, 64], F32).rearrange("(k p) n -> p k n", p=128)
    wv = c.din("wv", [D_MODEL, 64], F32).rearrange("(k p) n -> p k n", p=128)
    qnw_d = c.din("qnw", [64, 1], F32)
    knw_d = c.din("knw", [64, 1], F32)
    cosT = c.din("cosT", [64, SEQ], F32)
    sinT = c.din("sinT", [64, SEQ], F32)
    rmT_d = c.din("rmT", [64, 64], F32)
    ob_d = c.dout("ob", [64, SEQ], BF16)
    ps = c.psum_banks()
    ones = c.sb("ones", [128, 128], BF16)
    onesrow = c.sb("onesrow", [128, 64], F32)
    c.memset("pool", ones[:, :], 1.0, w=["const"])
    c.memset("pool", onesrow[:, :], 1.0, w=["const"])
    qnw = c.sb("qnw_s", [64, 1], F32)
    knw = c.sb("knw_s", [64, 1], F32)
    rmT = c.sb("rmT_s", [64, 64], F32)
    for t, d in ((qnw, qnw_d), (knw, knw_d), (rmT, rmT_d)):
        c.dma("sp", t[:], d, w=["const"])
    wqb = c.sb("wqb", [128, 8, 64], BF16)
    wkb = c.sb("wkb", [128, 8, 64], BF16)
    wvb = c.sb("wvb", [128, 8, 64], BF16)
    for t, d, n in ((wqb, wq, "wq"), (wkb, wk, "wk"), (wvb, wv, "wv")):
        load_cast(c, t[:], d, [128, 8, 64], n)
    QT = c.sb("QT", [128, SEQ], BF16)
    KT = c.sb("KT", [128, SEQ], BF16)
    c.memset("pool", QT[64:128, :], 0.0, w=["QT"])
    c.memset("pool", KT[64:128, :], 0.0, w=["KT"])
    VA = c.sb("VA", [128, SEQ // 128, 65], BF16)
    OB = c.sb("OB", [64, SEQ], BF16)
    c.memset("pool", VA[:, :, 64:65], 1.0, w=["VA"])
    xnb = [c.sb(f"xnb{i}", [128, 8, 514], BF16) for i in range(2)]
    cs = [c.sb(f"cs{i}", [64, 512], F32) for i in range(2)]
    sn = [c.sb(f"sn{i}", [64, 512], F32) for i in range(2)]
    W = []
    for i, (tag, wb, nw, dst) in enumerate((("q", wqb, qnw, QT), ("k", wkb, knw, KT))):
        W.append(dict(tag=tag, wb=wb, nw=nw, dst=dst, dk=dst is QT and "QT" or "KT", pq=ps[3 * i], pss=ps[3 * i + 1], prot=ps[3 * i + 2],
                      sqb=c.sb("sqb" + tag, [64, 512], BF16), rs=c.sb("rs" + tag, [64, 512], F32), qn=c.sb("qn" + tag, [64, 512], F32),
                      t1=c.sb("t1" + tag, [64, 512], F32), t2=c.sb("t2" + tag, [64, 512], F32)))
    for tb in range(SEQ // 512):
        s = tb % 2
        cols = slice(tb * 512, (tb + 1) * 512)
        c.dma("sp", xnb[s][:, :, :], xnp[:, :, tb * 512:tb * 512 + 514], w=[f"xnb{s}"])
        c.dma("pool", cs[s][:, :], cosT[:, cols], w=[f"cs{s}"])
        c.dma("pool", sn[s][:, :], sinT[:, cols], w=[f"sn{s}"])

        def st_proj(w):
            for k in range(8):
                c.mm(w["pq"][0:64, :], w["wb"][:, k, :], xnb[s][:, k, 1:513], k == 0, k == 7, r=[f"xnb{s}", "wts"], w=["pq" + w["tag"]])

        def st_sq(w):
            c.act(w["sqb"][:, :], w["pq"][0:64, :], AF.Square, r=["pq" + w["tag"]], w=["sqb" + w["tag"]])

        def st_ss(w):
            c.mm(w["pss"][0:64, :], ones[0:64, 0:64], w["sqb"][:, :], True, True, r=["sqb" + w["tag"], "const"], w=["pss" + w["tag"]])

        def st_rs(w):
            rstd_from_ss(c, w["rs"], w["pss"], 1.0 / 64, EPS, r=["pss" + w["tag"]], w=["rs" + w["tag"]], rows=slice(0, 64))

        def st_qn(w):
            c.stt("dve", w["qn"][:, :], w["pq"][0:64, :], w["nw"][:, 0:1], w["rs"][:, :], ALU.mult, ALU.mult,
                  r=["pq" + w["tag"], "rs" + w["tag"], "const"], w=["qn" + w["tag"]])

        def st_rot(w):
            c.mm(w["prot"][0:64, :], rmT[:, :], w["qn"][:, :], True, True, r=["qn" + w["tag"], "const"], w=["prot" + w["tag"]])

        def st_mul(w):
            c.tt("dve", w["t1"][:, :], w["qn"][:, :], cs[s][:, :], ALU.mult, r=["qn" + w["tag"], f"cs{s}"], w=["t1" + w["tag"]])
            c.tt("dve", w["t2"][:, :], w["prot"][0:64, :], sn[s][:, :], ALU.mult, r=["prot" + w["tag"], f"sn{s}"], w=["t2" + w["tag"]])

        def st_add(w):
            c.tt("dve", w["dst"][0:64, cols], w["t1"][:, :], w["t2"][:, :], ALU.add, r=["t1" + w["tag"], "t2" + w["tag"]], w=[w["dk"]])

        for stp in (st_proj, st_sq, st_ss, st_rs, st_qn, st_rot, st_mul, st_add):
            for w in W:
                stp(w)
        for ci in range(4):
            pb = ps[6 + ci % 2]
            for k in range(8):
                c.mm(pb[:, 0:64], xnb[s][:, k, 1 + ci * 128:1 + (ci + 1) * 128], wvb[:, k, :], k == 0, k == 7,
                     r=[f"xnb{s}", "wts"], w=[f"pv{ci % 2}"])
            c.cp("act", VA[:, tb * 4 + ci, 0:64], pb[:, 0:64], r=[f"pv{ci % 2}"], w=["VA"])
    c.S.barrier()
    pT = [c.sb(f"pT{i}", [128, 512], BF16) for i in range(3)]
    r64 = c.sb("r64", [128, 512], F32)
    osb = c.sb("osb", [64, 512], F32)
    attention_core(c, QT, KT, VA, MB_DK, 64 ** -0.5, OB, ps, pT, r64, osb, onesrow)
    for i in range(4):
        c.dma("sp", ob_d[:, i * 2048:(i + 1) * 2048], OB[:, i * 2048:(i + 1) * 2048], r=["OB"], final=True)


def emit_MA(c):
    xnp = c.din("xnp", [D_MODEL, SEQ + 2], BF16).rearrange("(k p) t -> p k t", p=128)
    wcq = c.din("wcq", [D_MODEL, 192], F32).rearrange("(k p) n -> p k n", p=128)
    wckv = c.din("wckv", [D_MODEL, 128], F32).rearrange("(k p) n -> p k n", p=128)
    wkr = c.din("wkr", [D_MODEL, 32], F32).rearrange("(k p) n -> p k n", p=128)
    wuq0_d = c.din("wuq0", [128, 96], F32)
    wuq1_d = c.din("wuq1", [64, 96], F32)
    wuk_d = c.din("wuk", [128, 64], F32)
    wuv_d = c.din("wuv", [128, 64], F32)
    qnw_d = c.din("qnw", [128, 2], F32)
    kvnw_d = c.din("kvnw", [128, 1], F32)
    cosT = c.din("cosT", [96, SEQ], F32)
    sinT = c.din("sinT", [96, SEQ], F32)
    rmT_d = c.din("rmT", [96, 96], F32)
    ob_d = c.dout("ob", [64, SEQ], BF16)
    ps = c.psum_banks()
    ones = c.sb("ones", [128, 128], BF16)
    onesrow = c.sb("onesrow", [128, 64], F32)
    c.memset("pool", ones[:, :], 1.0, w=["const"])
    c.memset("pool", onesrow[:, :], 1.0, w=["const"])
    qnw = c.sb("qnw_s", [128, 2], F32)
    kvnw = c.sb("kvnw_s", [128, 1], F32)
    rmT = c.sb("rmT_s", [96, 96], F32)
    for t, d in ((qnw, qnw_d), (kvnw, kvnw_d), (rmT, rmT_d)):
        c.dma("sp", t[:], d, w=["const"])
    wcqb = c.sb("wcqb", [128, 8, 192], BF16)
    wckvb = c.sb("wckvb", [128, 8, 128], BF16)
    wkrp = c.sb("wkrp", [128, 8, 96], BF16)
    wkp = c.sb("wkp", [128, 96], BF16)
    wuq0 = c.sb("wuq0b", [128, 96], BF16)
    wuq1 = c.sb("wuq1b", [64, 96], BF16)
    wuv = c.sb("wuvb", [128, 64], BF16)
    c.memset("pool", wkrp[:, :, :], 0.0, w=["wts"])
    c.memset("pool", wkp[:, :], 0.0, w=["wts"])
    load_cast(c, wcqb[:], wcq, [128, 8, 192], "wcq")
    load_cast(c, wckvb[:], wckv, [128, 8, 128], "wckv")
    load_cast(c, wkrp[:, :, 64:96], wkr, [128, 8, 32], "wkr")
    load_cast(c, wkp[:, 0:64], wuk_d, [128, 64], "wuk")
    load_cast(c, wuq0[:], wuq0_d, [128, 96], "wuq0")
    load_cast(c, wuq1[:], wuq1_d, [64, 96], "wuq1")
    load_cast(c, wuv[:], wuv_d, [128, 64], "wuv")
    QT = c.sb("QT", [128, SEQ], BF16)
    KT = c.sb("KT", [128, SEQ], BF16)
    c.memset("pool", QT[64:128, :], 0.0, w=["QT"])
    c.memset("pool", KT[64:128, :], 0.0, w=["KT"])
    VA = c.sb("VA", [128, SEQ // 128, 65], BF16)
    OB = c.sb("OB", [64, SEQ], BF16)
    c.memset("pool", VA[:, :, 64:65], 1.0, w=["VA"])
    xnb = [c.sb(f"xnb{i}", [128, 8, 514], BF16) for i in range(2)]
    cs = [c.sb(f"cs{i}", [96, 512], F32) for i in range(2)]
    sn = [c.sb(f"sn{i}", [96, 512], F32) for i in range(2)]
    sq0 = c.sb("sq0", [128, 512], BF16)
    sq1 = c.sb("sq1", [64, 512], BF16)
    rs = c.sb("rs", [128, 512], F32)
    rs2 = c.sb("rs2", [128, 512], F32)
    cqn0 = c.sb("cqn0", [128, 512], BF16)
    cqn1 = c.sb("cqn1", [64, 512], BF16)
    ckvn = c.sb("ckvn", [128, 512], BF16)
    qs = c.sb("qs", [96, 512], F32)
    t1 = c.sb("t1", [96, 512], F32)
    t2 = c.sb("t2", [96, 512], F32)
    for tb in range(SEQ // 512):
        s = tb % 2
        cols = slice(tb * 512, (tb + 1) * 512)
        c.dma("sp", xnb[s][:, :, :], xnp[:, :, tb * 512:tb * 512 + 514], w=[f"xnb{s}"])
        c.dma("pool", cs[s][:, :], cosT[:, cols], w=[f"cs{s}"])
        c.dma("pool", sn[s][:, :], sinT[:, cols], w=[f"sn{s}"])
        xk = [f"xnb{s}", "wts"]
        for k in range(8):
            c.mm(ps[0][:, :], wcqb[:, k, 0:128], xnb[s][:, k, 1:513], k == 0, k == 7, r=xk, w=["pcq0"])
        for k in range(8):
            c.mm(ps[1][0:64, :], wcqb[:, k, 128:192], xnb[s][:, k, 1:513], k == 0, k == 7, r=xk, w=["pcq1"])
        for k in range(8):
            c.mm(ps[2][:, :], wckvb[:, k, :], xnb[s][:, k, 1:513], k == 0, k == 7, r=xk, w=["pckv"])
        c.act(sq0[:, :], ps[0][:, :], AF.Square, r=["pcq0"], w=["sq0"])
        c.act(sq1[:, :], ps[1][0:64, :], AF.Square, r=["pcq1"], w=["sq1"])
        c.mm(ps[3][:, :], ones[:, :], sq0[:, :], True, False, r=["sq0", "const"], w=["pss"])
        c.mm(ps[3][:, :], ones[0:64, :], sq1[:, :], False, True, r=["sq1", "const"], w=["pss"])
        rstd_from_ss(c, rs, ps[3], 1.0 / 192, EPS, r=["pss"], w=["rs"])
        c.stt("dve", cqn0[:, :], ps[0][:, :], qnw[:, 0:1], rs[:, :], ALU.mult, ALU.mult, r=["pcq0", "rs", "const"], w=["cqn0"])
        c.stt("dve", cqn1[:, :], ps[1][0:64, :], qnw[0:64, 1:2], rs[0:64, :], ALU.mult, ALU.mult, r=["pcq1", "rs", "const"], w=["cqn1"])
        c.act(sq0[:, :], ps[2][:, :], AF.Square, r=["pckv"], w=["sq0"])
        c.mm(ps[3][:, :], ones[:, :], sq0[:, :], True, True, r=["sq0", "const"], w=["pss"])
        rstd_from_ss(c, rs2, ps[3], 1.0 / 128, EPS, r=["pss"], w=["rs2"])
        c.stt("dve", ckvn[:, :], ps[2][:, :], kvnw[:, 0:1], rs2[:, :], ALU.mult, ALU.mult, r=["pckv", "rs2", "const"], w=["ckvn"])
        for which, dst, dk_ in (("q", QT, "QT"), ("k", KT, "KT")):
            if which == "q":
                c.mm(ps[4][0:96, :], wuq0[:, :], cqn0[:, :], True, False, r=["cqn0", "wts"], w=["pqk"])
                c.mm(ps[4][0:96, :], wuq1[:, :], cqn1[:, :], False, True, r=["cqn1", "wts"], w=["pqk"])
            else:
                c.mm(ps[4][0:96, :], wkp[:, :], ckvn[:, :], True, False, r=["ckvn", "wts"], w=["pqk"])
                for k in range(8):
                    c.mm(ps[4][0:96, :], wkrp[:, k, :], xnb[s][:, k, 1:513], False, k == 7, r=xk, w=["pqk"])
            c.cp("act", qs[:, :], ps[4][0:96, :], r=["pqk"], w=["qs"])
            c.mm(ps[5][0:96, :], rmT[:, :], qs[:, :], True, True, r=["qs", "const"], w=["prot"])
            c.tt("dve", t1[:, :], qs[:, :], cs[s][:, :], ALU.mult, r=["qs", f"cs{s}"], w=["t1"])
            c.tt("dve", t2[:, :], ps[5][0:96, :], sn[s][:, :], ALU.mult, r=["prot", f"sn{s}"], w=["t2"])
            c.tt("dve", dst[0:96, cols], t1[:, :], t2[:, :], ALU.add, r=["t1", "t2"], w=[dk_])
        for ci in range(4):
            pb = ps[6 + ci % 2]
            c.mm(pb[:, 0:64], ckvn[:, ci * 128:(ci + 1) * 128], wuv[:, :], True, True, r=["ckvn", "wts"], w=[f"pv{ci % 2}"])
            c.cp("act", VA[:, tb * 4 + ci, 0:64], pb[:, 0:64], r=[f"pv{ci % 2}"], w=["VA"])
    c.S.barrier()
    pT = [c.sb(f"pT{i}", [128, 512], BF16) for i in range(3)]
    r64 = c.sb("r64", [128, 512], F32)
    osb = c.sb("osb", [64, 512], F32)
    attention_core(c, QT, KT, VA, 128, 96 ** -0.5, OB, ps, pT, r64, osb, onesrow)
    for i in range(4):
        c.dma("sp", ob_d[:, i * 2048:(i + 1) * 2048], OB[:, i * 2048:(i + 1) * 2048], r=["OB"], final=True)


OFF_A, OFF_B, OFF_C, OFF_D = 0, 352, 864, 1640


def rope_tables(rot_dim):
    rows = SEQ // 64
    row = np.repeat(np.arange(rows), 64).astype(np.float32)
    col = np.tile(np.arange(64), rows).astype(np.float32)
    sec = rot_dim // 2
    inv_freq = (np.float32(10000.0) ** (-np.arange(0, sec, 2, dtype=np.float32) / np.float32(sec))).astype(np.float32)
    ang_r = row[:, None] * inv_freq
    ang_c = col[:, None] * inv_freq
    ang = np.concatenate([ang_r, ang_r, ang_c, ang_c], -1).astype(np.float32)
    return np.cos(ang).astype(np.float32), np.sin(ang).astype(np.float32)


def rot_matrix(r):
    Rm = np.zeros((r, r), np.float32)
    q = r // 4
    for a in range(2):
        for e in range(q):
            Rm[a * 2 * q + e, a * 2 * q + q + e] = -1.0
            Rm[a * 2 * q + q + e, a * 2 * q + e] = 1.0
    return Rm


_CONST = {}


def consts():
    if not _CONST:
        cb, sb_ = rope_tables(64)
        ca, sa = rope_tables(32)
        _CONST["cosB"] = np.ascontiguousarray(cb.T)
        _CONST["sinB"] = np.ascontiguousarray(sb_.T)
        cA = np.ones((96, SEQ), np.float32)
        sA = np.zeros((96, SEQ), np.float32)
        cA[64:96] = ca.T
        sA[64:96] = sa.T
        _CONST["cosA"], _CONST["sinA"] = cA, sA
        _CONST["rmTB"] = np.ascontiguousarray(rot_matrix(64).T)
        rA = np.zeros((96, 96), np.float32)
        rA[64:96, 64:96] = rot_matrix(32).T
        _CONST["rmTA"] = rA
    return _CONST


def pad_seq(xnt):
    out = np.zeros((xnt.shape[0], SEQ + 2), xnt.dtype)
    out[:, 1:-1] = xnt
    return out


def maps_MB(inp, l, xnp_b):
    K_ = consts()
    w = inp["w_in"][l]
    maps = []
    for cidx in range(NCORES):
        b, h = cidx // 4, cidx % 4
        g = h // 2
        maps.append({
            "xnp": xnp_b[b],
            "wq": np.ascontiguousarray(w[:, OFF_B + h * 64:OFF_B + (h + 1) * 64]),
            "wk": np.ascontiguousarray(w[:, OFF_B + 256 + g * 64:OFF_B + 256 + (g + 1) * 64]),
            "wv": np.ascontiguousarray(w[:, OFF_B + 384 + g * 64:OFF_B + 384 + (g + 1) * 64]),
            "qnw": np.ascontiguousarray(inp["b_q_norm"][l].reshape(64, 1)),
            "knw": np.ascontiguousarray(inp["b_k_norm"][l].reshape(64, 1)),
            "cosT": K_["cosB"], "sinT": K_["sinB"], "rmT": K_["rmTB"],
        })
    return maps


def maps_MA(inp, l, xnp_b):
    K_ = consts()
    w = inp["w_in"][l]
    wuq = inp["a_w_uq"][l]
    wukv = inp["a_w_ukv"][l]
    qn = inp["a_q_norm"][l]
    qnw = np.zeros((128, 2), np.float32)
    qnw[:, 0] = qn[0:128]
    qnw[0:64, 1] = qn[128:192]
    maps = []
    for cidx in range(NCORES):
        b, h = cidx // 4, cidx % 4
        wq_h = wuq[:, h * 96:(h + 1) * 96]
        maps.append({
            "xnp": xnp_b[b],
            "wcq": np.ascontiguousarray(w[:, OFF_A:OFF_A + 192]),
            "wckv": np.ascontiguousarray(w[:, OFF_A + 192:OFF_A + 320]),
            "wkr": np.ascontiguousarray(w[:, OFF_A + 320:OFF_A + 352]),
            "wuq0": np.ascontiguousarray(wq_h[0:128]), "wuq1": np.ascontiguousarray(wq_h[128:192]),
            "wuk": np.ascontiguousarray(wukv[:, h * 128:h * 128 + 64]),
            "wuv": np.ascontiguousarray(wukv[:, h * 128 + 64:h * 128 + 128]),
            "qnw": qnw, "kvnw": np.ascontiguousarray(inp["a_kv_norm"][l].reshape(128, 1)),
            "cosT": K_["cosA"], "sinT": K_["sinA"], "rmT": K_["rmTA"],
        })
    return maps


NEG = -30000.0
FILLER = 0
MB_DK = 128


def chunk_masks():
    i = np.arange(128)
    m = {}
    m["Lf"] = (i[:, None] > i[None, :]).astype(np.float32)
    m["Rf"] = (i[:, None] <= i[None, :]).astype(np.float32)
    m["Lb"] = (i[:, None] < i[None, :]).astype(np.float32)
    m["Rb"] = (i[:, None] >= i[None, :]).astype(np.float32)
    m["nmf"] = np.where(i[:, None] <= i[None, :], 0.0, NEG).astype(np.float32)
    m["nmb"] = np.where(i[:, None] >= i[None, :], 0.0, NEG).astype(np.float32)
    m["nmfs"] = np.where(i[:, None] < i[None, :], 0.0, NEG).astype(np.float32)
    m["nmbs"] = np.where(i[:, None] > i[None, :], 0.0, NEG).astype(np.float32)
    m["ident"] = np.eye(128, dtype=np.float32)
    return m


def emit_conv_silu(c, ps_main, ps_halo, pre, acc, taps, dst, tagk, rk, wk):
    c.cp("act", pre[:, 1:513], ps_main[:, 0:512], r=[rk[0]], w=[tagk + "pre"])
    c.cp("dve", pre[:, 0:514:513], ps_halo[:, 0:2], r=[rk[1]], w=[tagk + "pre"])
    c.act(acc[:, :], pre[:, 1:513], AF.Identity, r=[tagk + "pre", "const"], w=[tagk + "acc"], scale=taps[:, 1:2], bias=taps[:, 3:4])
    c.stt("dve", acc[:, :], pre[:, 0:512], taps[:, 0:1], acc[:, :], ALU.mult, ALU.add, r=[tagk + "pre", tagk + "acc", "const"], w=[tagk + "acc"])
    c.stt("dve", acc[:, :], pre[:, 2:514], taps[:, 2:3], acc[:, :], ALU.mult, ALU.add, r=[tagk + "pre", tagk + "acc", "const"], w=[tagk + "acc"])
    c.act(dst, acc[:, :], AF.Silu, r=[tagk + "acc"], w=wk)


def emit_MC(c):
    NCH = SEQ // 128
    xnp = c.din("xnp", [D_MODEL, SEQ + 2], BF16).rearrange("(k p) t -> p k t", p=128)
    w1 = c.din("w1", [D_MODEL, 128], F32).rearrange("(k p) n -> p k n", p=128)
    w2 = c.din("w2", [D_MODEL, 128], F32).rearrange("(k p) n -> p k n", p=128)
    wdt = c.din("wdt", [D_MODEL, 2], F32).rearrange("(k p) n -> p k n", p=128)
    taps1_d = c.din("taps1", [128, 4], F32)
    taps2_d = c.din("taps2", [128, 4], F32)
    scal_d = c.din("scal", [128, 6], F32)
    mk_d = {k: c.din("m_" + k, [128, 128], F32) for k in ("Lf", "Rf", "Lb", "Rb", "nmf", "nmb", "ident")}
    ob_d = c.dout("ob", [64, SEQ], BF16)
    ps = c.psum_banks()
    mk = {k: c.sb("mk_" + k, [128, 128], F32) for k in mk_d}
    for k in mk_d:
        c.dma("sp", mk[k][:, :], mk_d[k], w=["const"])
    taps1 = c.sb("taps1_s", [128, 4], F32)
    taps2 = c.sb("taps2_s", [128, 4], F32)
    scal = c.sb("scal_s", [128, 6], F32)
    for t, d in ((taps1, taps1_d), (taps2, taps2_d), (scal, scal_d)):
        c.dma("sp", t[:], d, w=["const"])
    ones32 = c.sb("ones32", [128, 128], F32)
    c.memset("pool", ones32[:, :], 1.0, w=["const"])
    w1b = c.sb("w1b", [128, 8, 128], BF16)
    w2b = c.sb("w2b", [128, 8, 128], BF16)
    wdtb = c.sb("wdtb", [128, 8, 2], BF16)
    load_cast(c, w1b[:], w1, [128, 8, 128], "w1")
    load_cast(c, w2b[:], w2, [128, 8, 128], "w2")
    load_cast(c, wdtb[:], wdt, [128, 8, 2], "wdt")
    F1 = c.sb("F1", [128, SEQ], F32)
    F2 = c.sb("F2", [128, SEQ], F32)
    Y = c.sb("Y", [64, SEQ], F32)
    C0 = c.sb("C0", [128, SEQ], BF16)
    F1b = c.sb("F1b", [128, SEQ], BF16)
    c.memset("pool", C0[0:64, :], 0.0, w=["C0"])
    OB = c.sb("OB", [64, SEQ], BF16)
    RAW = c.sb("RAW", [128, NCH, 2], F32)
    DT = c.sb("DT", [128, NCH, 2], F32)
    AA = c.sb("AA", [128, NCH, 2], F32)
    expa = c.sb("expa", [128, 2], F32)
    xnb = [c.sb(f"xnb{i}", [128, 8, 514], BF16) for i in range(2)]
    pre = c.sb("pre", [128, 514], F32)
    acc = c.sb("acc", [128, 512], F32)
    for tb in range(SEQ // 512):
        s = tb % 2
        cols = slice(tb * 512, (tb + 1) * 512)
        c.dma("sp", xnb[s][:, :, :], xnp[:, :, tb * 512:tb * 512 + 514], w=[f"xnb{s}"])
        xk = [f"xnb{s}", "wts"]
        for ti, (wb, taps, F) in enumerate(((w1b, taps1, F1), (w2b, taps2, F2))):
            for k in range(8):
                c.mm(ps[ti][:, :], wb[:, k, :], xnb[s][:, k, 1:513], k == 0, k == 7, r=xk, w=[f"pm{ti}"])
            for k in range(8):
                c.mm(ps[2 + ti][:, 0:2], wb[:, k, :], xnb[s][:, k, 0:514:513], k == 0, k == 7, r=xk, w=[f"ph{ti}"])
            emit_conv_silu(c, ps[ti], ps[2 + ti], pre, acc, taps, F[:, cols], "c", (f"pm{ti}", f"ph{ti}"), [f"F{ti + 1}"])
            if ti == 1:
                c.cp("pool", C0[64:128, cols], F2[64:128, cols], r=["F2"], w=["C0"])
            else:
                c.cp("pool", F1b[:, cols], F1[:, cols], r=["F1"], w=["F1b"])
        for ci in range(4):
            pb = ps[4 + ci % 2]
            for k in range(8):
                c.mm(pb[:, 0:2], xnb[s][:, k, 1 + ci * 128:1 + (ci + 1) * 128], wdtb[:, k, :], k == 0, k == 7, r=xk, w=[f"pdt{ci % 2}"])
            c.cp("act", RAW[:, tb * 4 + ci, :], pb[:, 0:2], r=[f"pdt{ci % 2}"], w=["RAW"])
    for d in range(2):
        c.act(DT[:, :, d], RAW[:, :, d], AF.Exp, r=["RAW", "const"], w=["DT"], bias=scal[:, d:d + 1])
        c.act(DT[:, :, d], DT[:, :, d], AF.Ln, r=["DT"], w=["DT"], bias=1.0)
    c.act(expa[:, :], scal[:, 2:4], AF.Exp, r=["const"], w=["expa"])
    for d in range(2):
        c.ts("dve", AA[:, :, d], DT[:, :, d], expa[:, d:d + 1], -1.0, ALU.mult, ALU.mult, r=["DT", "expa"], w=["AA"])
    c.S.barrier()
    T = {}
    for d in range(2):
        for n in ("lhsA", "abc", "E", "eacb"):
            T[(n, d)] = c.sb(f"{n}{d}", [128, 128], F32)
        for n in ("MT", "BD", "CD"):
            T[(n, d)] = c.sb(f"{n}{d}", [128, 128], BF16)
        T[("XC", d)] = c.sb(f"XC{d}", [128, 64], BF16)
        T[("STb", d)] = c.sb(f"STb{d}", [128, 64], BF16)
        c.memset("pool", T[("STb", d)][:, :], 0.0, w=[f"STb{d}"])
        T[("sm", d)] = c.sb(f"sm{d}", [128, 2], F32)
        T[("ST", d)] = c.sb(f"ST{d}", [128, 64], F32)
        c.memset("pool", T[("BD", d)][:, :], 0.0, w=[f"BD{d}"])
        c.memset("pool", T[("CD", d)][:, :], 0.0, w=[f"CD{d}"])
        c.memset("pool", T[("ST", d)][:, :], 0.0, w=[f"ST{d}"])
    tmpy = c.sb("tmpy", [64, 128], F32)
    done = set()
    order = []
    for i in range(NCH):
        order.append((0, i))
        order.append((1, NCH - 1 - i))
    for d, ch in order:
        cc = slice(ch * 128, (ch + 1) * 128)
        mL, mR, nm = (mk["Lf"], mk["Rf"], mk["nmf"]) if d == 0 else (mk["Lb"], mk["Rb"], mk["nmb"])
        a_col = AA[:, ch, d:d + 1]
        dt_col = DT[:, ch, d:d + 1]
        t = lambda n: T[(n, d)]
        k_ = lambda n: f"{n}{d}"
        bA, bB, bC, bD = ps[4 * d], ps[4 * d + 1], ps[4 * d + 2], ps[4 * d + 3]
        kA, kB, kC, kD = (f"B{4 * d + i}" for i in range(4))
        sm = t("sm")
        c.ts("dve", t("lhsA")[:, :], mL[:, :], a_col, None, ALU.mult, None, r=["AA", "const"], w=[k_("lhsA")])
        c.ts("dve", t("abc")[:, :], ones32[:, :], a_col, None, ALU.mult, None, r=["AA", "const"], w=[k_("abc")])
        c.mm(bA[:, 0:128], t("lhsA")[:, :], mR[:, :], True, True, r=[k_("lhsA"), "const"], w=[kA, k_("pseg")])
        c.mm(bA[:, 128:256], t("abc")[:, :], mR[:, :], True, True, r=[k_("abc"), "const"], w=[kA, k_("pacb")])
        c.mm(bB[:, 0:1], mL[:, :], a_col, True, True, r=["AA", "const"], w=[kB, k_("psm0")])
        c.mm(bB[:, 1:2], ones32[:, :], a_col, True, True, r=["AA", "const"], w=[kB, k_("psm1")])
        c.mm(bB[:, 128:256], F1b[:, cc], C0[:, cc], True, True, r=["F1b", "C0"], w=[kB, k_("psc")])
        c.tr(bC[:, 0:128], F1[:, cc], mk["ident"][:, :], r=["F1", "const"], w=[kC, k_("ptr")])
        c.act(sm[:, :], bB[:, 0:2], AF.Exp, r=[k_("psm0"), k_("psm1")], w=[kB, k_("sm")])
        c.tt("dve", t("E")[:, :], bA[:, 0:128], nm[:, :], ALU.add, r=[k_("pseg"), "const"], w=[kA, k_("E")])
        c.act(t("eacb")[64:128, :], bA[64:128, 128:256], AF.Exp, r=[k_("pacb")], w=[kA, k_("eacb")])
        c.act(t("E")[:, :], t("E")[:, :], AF.Exp, r=[k_("E")], w=[k_("E")])
        c.tt("dve", t("MT")[:, :], bB[:, 128:256], t("E")[:, :], ALU.mult, r=[k_("psc"), k_("E")], w=[kB, k_("MT")])
        c.ts("dve", t("XC")[:, :], bC[:, 0:64], dt_col, None, ALU.mult, None, r=[k_("ptr"), "DT"], w=[kC, k_("XC")])
        c.ts("dve", t("BD")[:, 64:128], bC[:, 64:128], sm[:, 0:1], None, ALU.mult, None, r=[k_("ptr"), k_("sm")], w=[kC, k_("BD")])
        c.tt("dve", t("CD")[64:128, :], F2[64:128, cc], t("eacb")[64:128, :], ALU.mult, r=["F2", k_("eacb")], w=[k_("CD")])
        c.mm(bD[0:64, 0:128], t("XC")[:, :], t("MT")[:, :], True, False, r=[k_("XC"), k_("MT")], w=[kD, k_("py")])
        c.mm(bD[0:64, 0:128], t("STb")[:, :], t("CD")[:, :], False, True, r=[k_("STb"), k_("CD")], w=[kD, k_("py")])
        c.mm(bD[:, 128:192], t("BD")[:, :], t("XC")[:, :], True, True, r=[k_("BD"), k_("XC")], w=[kD, k_("pst")])
        c.stt("dve", t("ST")[64:128, :], t("ST")[64:128, :], sm[64:128, 1:2], bD[64:128, 128:192], ALU.mult, ALU.add,
              r=[k_("ST"), k_("sm"), k_("pst")], w=[kD, k_("ST")])
        c.cp("act", t("STb")[64:128, :], t("ST")[64:128, :], r=[k_("ST")], w=[k_("STb")])
        if ch not in done:
            done.add(ch)
            c.cp("act", Y[:, cc], bD[0:64, 0:128], r=[k_("py")], w=[kD, "Y"])
        else:
            c.tt("dve", tmpy[:, :], Y[:, cc], bD[0:64, 0:128], ALU.add, r=["Y", k_("py")], w=[kD, "tmpy"])
            c.stt("dve", tmpy[:, :], F1[0:64, cc], scal[0:64, 4:5], tmpy[:, :], ALU.mult, ALU.add, r=["F1", "tmpy", "const"], w=["tmpy"])
            c.tt("pool", OB[:, cc], tmpy[:, :], F2[0:64, cc], ALU.mult, r=["tmpy", "F2"], w=["OB"])
    for i in range(4):
        c.dma("sp", ob_d[:, i * 2048:(i + 1) * 2048], OB[:, i * 2048:(i + 1) * 2048], r=["OB"], final=True)


def maps_MC(inp, l, xnp_b):
    mk = chunk_masks()
    w = inp["w_in"][l]
    cw = inp["c_conv_w"][l]
    cb = inp["c_conv_b"][l]
    maps = []
    for cidx in range(NCORES):
        b, h = cidx // 4, cidx % 4
        g = h // 2
        xs_c = np.arange(h * 64, (h + 1) * 64)
        B_c = 256 + np.arange(g * 64, (g + 1) * 64)
        C_c = 384 + np.arange(g * 64, (g + 1) * 64)
        ch1 = np.concatenate([xs_c, B_c])
        taps1 = np.concatenate([cw[:, ch1].T, cb[ch1][:, None]], 1).astype(np.float32)
        taps2 = np.zeros((128, 4), np.float32)
        taps2[0:64, 1] = 1.0
        taps2[64:128, 0:3] = cw[:, C_c].T
        taps2[64:128, 3] = cb[C_c]
        scal = np.zeros((128, 6), np.float32)
        scal[:, 0] = inp["c_dt_bias"][l][0, h]
        scal[:, 1] = inp["c_dt_bias"][l][1, h]
        scal[:, 2] = inp["c_a_log"][l][0, h]
        scal[:, 3] = inp["c_a_log"][l][1, h]
        scal[:, 4] = inp["c_d_skip"][l][h]
        m = {
            "xnp": xnp_b[b],
            "w1": np.ascontiguousarray(w[:, OFF_C + 256 + ch1]),
            "w2": np.ascontiguousarray(np.concatenate([w[:, OFF_C + h * 64:OFF_C + (h + 1) * 64], w[:, OFF_C + 256 + C_c]], 1)),
            "wdt": np.ascontiguousarray(w[:, [OFF_C + 768 + h, OFF_C + 772 + h]]),
            "taps1": np.ascontiguousarray(taps1), "taps2": taps2, "scal": scal,
        }
        for k in ("Lf", "Rf", "Lb", "Rb", "nmf", "nmb", "ident"):
            m["m_" + k] = mk[k]
        maps.append(m)
    return maps


def emit_MD(c):
    NCH = SEQ // 128
    xnp = c.din("xnp", [D_MODEL, SEQ + 2], BF16).rearrange("(k p) t -> p k t", p=128)
    wqv = c.din("wqv", [D_MODEL, 128], F32).rearrange("(k p) n -> p k n", p=128)
    wkz = c.din("wkz", [D_MODEL, 128], F32).rearrange("(k p) n -> p k n", p=128)
    wab = c.din("wab", [D_MODEL, 4], F32).rearrange("(k p) n -> p k n", p=128)
    tapsqv_d = c.din("tapsqv", [128, 4], F32)
    tapskz_d = c.din("tapskz", [128, 4], F32)
    scal_d = c.din("scal", [128, 6], F32)
    normw_d = c.din("normw", [128, 64], F32)
    mnames = ("Lf", "Rf", "Lb", "Rb", "nmf", "nmb", "ident")
    mk_d = {k: c.din("m_" + k, [128, 128], F32) for k in mnames}
    ob_d = c.dout("ob", [64, SEQ], BF16)
    ps = c.psum_banks()
    mk = {k: c.sb("mk_" + k, [128, 128], F32) for k in mk_d}
    for k in mk_d:
        c.dma("sp", mk[k][:, :], mk_d[k], w=["const"])
    tapsqv = c.sb("tapsqv_s", [128, 4], F32)
    tapskz = c.sb("tapskz_s", [128, 4], F32)
    scal = c.sb("scal_s", [128, 6], F32)
    normw = c.sb("normw_s", [128, 64], F32)
    for t, d in ((tapsqv, tapsqv_d), (tapskz, tapskz_d), (scal, scal_d), (normw, normw_d)):
        c.dma("sp", t[:], d, w=["const"])
    ones32 = c.sb("ones32", [128, 128], F32)
    c.memset("pool", ones32[:, :], 1.0, w=["const"])
    wqvb = c.sb("wqvb", [128, 8, 128], BF16)
    wkzb = c.sb("wkzb", [128, 8, 128], BF16)
    wabb = c.sb("wabb", [128, 8, 4], BF16)
    load_cast(c, wqvb[:], wqv, [128, 8, 128], "wqv")
    load_cast(c, wkzb[:], wkz, [128, 8, 128], "wkz")
    load_cast(c, wabb[:], wab, [128, 8, 4], "wab")
    FQV = c.sb("FQV", [128, SEQ], F32)
    FKZ = c.sb("FKZ", [128, SEQ], F32)
    OACC = c.sb("OACC", [128, NCH, 64], F32)
    ZS = c.sb("ZS", [128, NCH, 64], F32)
    OB = c.sb("OB", [64, SEQ], BF16)
    RAW = c.sb("RAW", [128, NCH, 4], F32)
    BETA = c.sb("BETA", [128, NCH, 2], F32)
    GG = c.sb("GG", [128, NCH, 2], F32)
    expa = c.sb("expa", [128, 2], F32)
    st1 = contextlib.ExitStack()
    xnb = [c.sb(f"xnb{i}", [128, 8, 514], BF16, st=st1) for i in range(2)]
    pre = c.sb("pre", [128, 514], F32, st=st1)
    acc = c.sb("acc", [128, 512], F32, st=st1)
    sq = c.sb("sq", [64, 512], F32, st=st1)
    rs = c.sb("rs", [64, 512], F32, st=st1)
    for tb in range(SEQ // 512):
        s = tb % 2
        cols = slice(tb * 512, (tb + 1) * 512)
        c.dma("sp", xnb[s][:, :, :], xnp[:, :, tb * 512:tb * 512 + 514], w=[f"xnb{s}"])
        xk = [f"xnb{s}", "wts"]
        for ti, (wb, taps, F, qscale) in enumerate(((wqvb, tapsqv, FQV, 0.125), (wkzb, tapskz, FKZ, 1.0))):
            for k in range(8):
                c.mm(ps[ti][:, :], wb[:, k, :], xnb[s][:, k, 1:513], k == 0, k == 7, r=xk, w=[f"pm{ti}"])
            for k in range(8):
                c.mm(ps[2 + ti][:, 0:2], wb[:, k, :], xnb[s][:, k, 0:514:513], k == 0, k == 7, r=xk, w=[f"ph{ti}"])
            fk = f"F{ti}"
            emit_conv_silu(c, ps[ti], ps[2 + ti], pre, acc, taps, F[:, cols], "d", (f"pm{ti}", f"ph{ti}"), [fk])
            c.tt("dve", sq[:, :], F[0:64, cols], F[0:64, cols], ALU.mult, r=[fk], w=["sq"])
            c.mm(ps[6][0:64, :], ones32[0:64, 0:64], sq[:, :], True, True, r=["sq", "const"], w=["pss"])
            c.act(rs[:, :], ps[6][0:64, :], AF.Ln, r=["pss"], w=["rs"], bias=1e-6)
            c.act(rs[:, :], rs[:, :], AF.Exp, r=["rs"], w=["rs"], scale=-0.5)
            c.stt("dve", F[0:64, cols], F[0:64, cols], qscale, rs[:, :], ALU.mult, ALU.mult, r=[fk, "rs"], w=[fk])
        for ci in range(4):
            pb = ps[4 + ci % 2]
            for k in range(8):
                c.mm(pb[:, 0:4], xnb[s][:, k, 1 + ci * 128:1 + (ci + 1) * 128], wabb[:, k, :], k == 0, k == 7, r=xk, w=[f"pab{ci % 2}"])
            c.cp("act", RAW[:, tb * 4 + ci, :], pb[:, 0:4], r=[f"pab{ci % 2}"], w=["RAW"])
    c.act(BETA[:, :, :], RAW[:, :, 0:2], AF.Exp, r=["RAW"], w=["BETA"], scale=-1.0)
    c.ts("dve", BETA[:, :, :], BETA[:, :, :], 1.0, None, ALU.add, None, r=["BETA"], w=["BETA"])
    c.recip(BETA[:, :, :], BETA[:, :, :], r=["BETA"], w=["BETA"])
    for d in range(2):
        c.act(GG[:, :, d], RAW[:, :, 2 + d], AF.Exp, r=["RAW", "const"], w=["GG"], bias=scal[:, d:d + 1])
        c.act(GG[:, :, d], GG[:, :, d], AF.Ln, r=["GG"], w=["GG"], bias=1.0)
    c.act(expa[:, :], scal[:, 2:4], AF.Exp, r=["const"], w=["expa"])
    for d in range(2):
        c.ts("dve", GG[:, :, d], GG[:, :, d], expa[:, d:d + 1], -1.0, ALU.mult, ALU.mult, r=["GG", "expa"], w=["GG"])
    c.S.barrier()
    st1.close()
    G = 8
    slots = []
    for g in range(G):
        t = {n: c.sb(f"{n}_{g}", [128, 128], F32) for n in ("E", "Es", "NT", "NN", "PTk", "Pk", "X0", "X1", "PT")}
        t["WT"] = c.sb(f"WT_{g}", [64, 128], F32)
        for n in ("VN", "KD", "OT"):
            t[n] = c.sb(f"{n}_{g}", [128, 64], F32)
        t["sm"] = c.sb(f"sm_{g}", [128, 4], F32)
        slots.append(t)
    Sst = [c.sb(f"S{d}", [64, 64], F32) for d in range(2)]
    for d in range(2):
        c.memset("pool", Sst[d][:, :], 0.0, w=[f"S{d}"])
    osum = c.sb("osum", [128, 64], F32)
    osq = c.sb("osq", [128, 64], F32)
    oss = c.sb("oss", [128, 1], F32)
    og = c.sb("og", [128, 64], F32)
    ident = mk["ident"]
    done = set()
    stepno = [0]

    def region(g, s):
        bank = (g // 4) * 4 + (s % 4)
        reg = g % 4
        return ps[bank][:, reg * 128:(reg + 1) * 128], f"B{bank}", f"r{bank}_{reg}"

    def STEP(pe_fn, cons_fn):
        s = stepno[0]
        stepno[0] += 1
        for g in range(G):
            R_, bk, rk = region(g, s)
            pe_fn(g, R_, [bk, rk])
        for g in range(G):
            R_, bk, rk = region(g, s)
            cons_fn(g, R_, rk, bk)

    for grp in range(NCH // 4):
        cds = [(0, 4 * grp + j) for j in range(4)] + [(1, NCH - 1 - 4 * grp - j) for j in range(4)]
        info = []
        for g, (d, ch) in enumerate(cds):
            mL, mR, nm, ms = (mk["Lf"], mk["Rf"], mk["nmf"], mk["Lb"]) if d == 0 else (mk["Lb"], mk["Rb"], mk["nmb"], mk["Lf"])
            info.append(dict(d=d, ch=ch, cc=slice(ch * 128, (ch + 1) * 128), mL=mL, mR=mR, nm=nm, ms=ms,
                             g_col=GG[:, ch, d:d + 1], b_col=BETA[:, ch, d:d + 1], t=slots[g],
                             k=lambda n, g=g: f"{n}_{g}"))
        for I in info:
            c.ts("dve", I["t"]["E"][:, :], I["mL"][:, :], I["g_col"], None, ALU.mult, None, r=["GG", "const"], w=[I["k"]("E")])

        def pe(g, R_, w):
            I = info[g]
            c.mm(R_, I["t"]["E"][:, :], I["mR"][:, :], True, True, r=[I["k"]("E"), "const"], w=w)

        def cons(g, R_, rk, bk):
            I = info[g]
            c.tt("dve", I["t"]["E"][:, :], R_, I["nm"][:, :], ALU.add, r=[rk, "const"], w=[bk, I["k"]("E")])
            c.act(I["t"]["E"][:, :], I["t"]["E"][:, :], AF.Exp, r=[I["k"]("E")], w=[I["k"]("E")])
            c.tt("dve", I["t"]["Es"][:, :], I["t"]["E"][:, :], I["ms"][:, :], ALU.mult, r=[I["k"]("E"), "const"], w=[I["k"]("Es")])
        STEP(pe, cons)

        def pe(g, R_, w):
            I = info[g]
            c.mm(R_[:, 0:1], I["mR"][:, :], I["g_col"], True, True, r=["GG", "const"], w=w)
            c.mm(R_[:, 1:2], I["mL"][:, :], I["g_col"], True, True, r=["GG", "const"], w=w)
            c.mm(R_[:, 2:3], ones32[:, :], I["g_col"], True, True, r=["GG", "const"], w=w)

        def cons(g, R_, rk, bk):
            I = info[g]
            c.act(I["t"]["sm"][:, 0:3], R_[:, 0:3], AF.Exp, r=[rk], w=[bk, I["k"]("sm")])
        STEP(pe, cons)

        def pe(g, R_, w):
            I = info[g]
            c.mm(R_, FKZ[0:64, I["cc"]], FKZ[0:64, I["cc"]], True, True, r=["F1"], w=w)

        def cons(g, R_, rk, bk):
            I = info[g]
            c.stt("dve", I["t"]["NT"][:, :], R_, I["b_col"], I["t"]["Es"][:, :], ALU.mult, ALU.mult,
                  r=[rk, "BETA", I["k"]("Es")], w=[bk, I["k"]("NT")])
        STEP(pe, cons)

        def pe(g, R_, w):
            I = info[g]
            c.mm(R_, FKZ[0:64, I["cc"]], FQV[0:64, I["cc"]], True, True, r=["F0", "F1"], w=w)

        def cons(g, R_, rk, bk):
            I = info[g]
            c.tt("dve", I["t"]["PT"][:, :], R_, I["t"]["E"][:, :], ALU.mult, r=[rk, I["k"]("E")], w=[bk, I["k"]("PT")])
        STEP(pe, cons)

        def pe(g, R_, w):
            I = info[g]
            c.tr(R_, I["t"]["NT"][:, :], ident[:, :], r=[I["k"]("NT"), "const"], w=w)

        def cons(g, R_, rk, bk):
            I = info[g]
            c.cp("act", I["t"]["NN"][:, :], R_, r=[rk], w=[bk, I["k"]("NN")])
        STEP(pe, cons)

        def pe(g, R_, w):
            I = info[g]
            c.tr(R_, FQV[:, I["cc"]], ident[:, :], r=["F0", "const"], w=w)

        def cons(g, R_, rk, bk):
            I = info[g]
            c.cp("act", I["t"]["X0"][:, 0:64], R_[:, 64:128], r=[rk], w=[bk, I["k"]("X0")])
        STEP(pe, cons)

        def pe(g, R_, w):
            I = info[g]
            c.tr(R_, FKZ[:, I["cc"]], ident[:, :], r=["F1", "const"], w=w)

        def cons(g, R_, rk, bk):
            I = info[g]
            sm = I["t"]["sm"]
            c.ts("dve", I["t"]["X0"][:, 64:128], R_[:, 0:64], sm[:, 0:1], None, ALU.mult, None, r=[rk, I["k"]("sm")], w=[bk, I["k"]("X0")])
            c.ts("dve", I["t"]["KD"][:, :], R_[:, 0:64], sm[:, 1:2], None, ALU.mult, None, r=[rk, I["k"]("sm")], w=[bk, I["k"]("KD")])
            if I["ch"] not in done:
                c.cp("act", ZS[:, I["ch"], :], R_[:, 64:128], r=[rk], w=[bk, "ZS"])
        STEP(pe, cons)

        def pe(g, R_, w):
            I = info[g]
            c.mm(R_, I["t"]["NT"][:, :], I["t"]["X0"][:, :], True, True, r=[I["k"]("NT"), I["k"]("X0")], w=w)

        def cons(g, R_, rk, bk):
            I = info[g]
            c.tt("dve", I["t"]["X1"][:, :], I["t"]["X0"][:, :], R_, ALU.subtract, r=[I["k"]("X0"), rk], w=[bk, I["k"]("X1")])
        STEP(pe, cons)

        cur, oth = "X1", "X0"
        prevT, prevN = "NT", "NN"
        for lv in range(6):
            newT, newN = ("PTk", "Pk") if lv % 2 == 0 else ("NT", "NN")

            def pe(g, R_, w, prevT=prevT, prevN=prevN):
                I = info[g]
                c.mm(R_, I["t"][prevN][:, :], I["t"][prevT][:, :], True, True, r=[I["k"](prevT), I["k"](prevN)], w=w)

            def cons(g, R_, rk, bk, newT=newT):
                I = info[g]
                c.cp("act", I["t"][newT][:, :], R_, r=[rk], w=[bk, I["k"](newT)])
            STEP(pe, cons)
            if lv < 5:
                def pe(g, R_, w, prevT=prevT, prevN=prevN):
                    I = info[g]
                    c.mm(R_, I["t"][prevT][:, :], I["t"][prevN][:, :], True, True, r=[I["k"](prevT), I["k"](prevN)], w=w)

                def cons(g, R_, rk, bk, newN=newN):
                    I = info[g]
                    c.cp("dve", I["t"][newN][:, :], R_, r=[rk], w=[bk, I["k"](newN)])
                STEP(pe, cons)

            def pe(g, R_, w, newT=newT, cur=cur):
                I = info[g]
                c.mm(R_, I["t"][newT][:, :], I["t"][cur][:, :], True, True, r=[I["k"](newT), I["k"](cur)], w=w)

            def cons(g, R_, rk, bk, cur=cur, oth=oth):
                I = info[g]
                c.tt("dve", I["t"][oth][:, :], I["t"][cur][:, :], R_, ALU.add, r=[I["k"](cur), rk], w=[bk, I["k"](oth)])
            STEP(pe, cons)
            cur, oth = oth, cur
            prevT, prevN = newT, newN
        UWn = oth
        for I in info:
            c.ts("dve", I["t"][UWn][:, :], I["t"][cur][:, :], I["b_col"], None, ALU.mult, None, r=[I["k"](cur), "BETA"], w=[I["k"](UWn)])

        def pe(g, R_, w):
            I = info[g]
            c.tr(R_[0:64, :], I["t"][UWn][:, 64:128], ident[:, :], r=[I["k"](UWn), "const"], w=w)

        def cons(g, R_, rk, bk):
            I = info[g]
            c.cp("act", I["t"]["WT"][:, :], R_[0:64, :], r=[rk], w=[bk, I["k"]("WT")])
        STEP(pe, cons)

        for j in range(4):
            for d in range(2):
                g = d * 4 + j
                I = info[g]
                t, k_ = I["t"], I["k"]
                ch, cc = I["ch"], I["cc"]
                bA, bB, bC = ps[4 * d], ps[4 * d + 1], ps[4 * d + 2]
                kA, kB, kC = f"B{4 * d}", f"B{4 * d + 1}", f"B{4 * d + 2}"
                S_ = Sst[d]
                sm = t["sm"]
                c.mm(bA[:, 0:64], t["WT"][:, :], S_[:, :], True, True, r=[k_("WT"), f"S{d}"], w=[kA, f"pa{d}"])
                c.mm(bA[:, 128:192], FQV[0:64, cc], S_[:, :], True, True, r=["F0", f"S{d}"], w=[kA, f"po{d}"])
                c.tt("dve", t["VN"][:, :], t[UWn][:, 0:64], bA[:, 0:64], ALU.subtract, r=[k_(UWn), f"pa{d}"], w=[kA, k_("VN")])
                c.act(t["OT"][:, :], bA[:, 128:192], AF.Identity, r=[f"po{d}", k_("sm")], w=[kA, k_("OT")], scale=sm[:, 0:1])
                c.mm(bB[:, 0:64], t["PT"][:, :], t["VN"][:, :], True, True, r=[k_("PT"), k_("VN")], w=[kB, f"po2{d}"])
                c.mm(bB[0:64, 128:192], t["KD"][:, :], t["VN"][:, :], True, True, r=[k_("KD"), k_("VN")], w=[kB, f"pS{d}"])
                c.stt("dve", S_[:, :], S_[:, :], sm[0:64, 2:3], bB[0:64, 128:192], ALU.mult, ALU.add,
                      r=[f"S{d}", k_("sm"), f"pS{d}"], w=[kB, f"S{d}"])
                if ch not in done:
                    done.add(ch)
                    c.tt("dve", OACC[:, ch, :], t["OT"][:, :], bB[:, 0:64], ALU.add, r=[k_("OT"), f"po2{d}"], w=[kB, "OACC"])
                else:
                    c.tt("dve", osum[:, :], t["OT"][:, :], bB[:, 0:64], ALU.add, r=[k_("OT"), f"po2{d}"], w=[kB, "osum"])
                    c.tt("dve", osum[:, :], osum[:, :], OACC[:, ch, :], ALU.add, r=["osum", "OACC"], w=["osum"])
                    c.tt("pool", osq[:, :], osum[:, :], osum[:, :], ALU.mult, r=["osum"], w=["osq"])
                    c.S.add("dve", lambda e: e.reduce_sum(out=oss[:, 0:1], in_=osq[:, :], axis=mybir.AxisListType.X), ["osq"], ["oss"])
                    c.act(oss[:, :], oss[:, :], AF.Ln, r=["oss"], w=["oss"], scale=1.0 / 64, bias=EPS)
                    c.act(oss[:, :], oss[:, :], AF.Exp, r=["oss"], w=["oss"], scale=-0.5)
                    c.stt("dve", og[:, :], osum[:, :], oss[:, 0:1], normw[:, :], ALU.mult, ALU.mult, r=["osum", "oss", "const"], w=["og"])
                    c.tt("pool", og[:, :], og[:, :], ZS[:, ch, :], ALU.mult, r=["og", "ZS"], w=["og"])
                    c.tr(bC[0:64, 0:128], og[:, :], ident[:, :], r=["og", "const"], w=[kC, "ptrO"])
                    c.cp("act", OB[:, cc], bC[0:64, 0:128], r=["ptrO"], w=[kC, "OB"])
    for i in range(4):
        c.dma("sp", ob_d[:, i * 2048:(i + 1) * 2048], OB[:, i * 2048:(i + 1) * 2048], r=["OB"], final=True)


def maps_MD(inp, l, xnp_b):
    mk = chunk_masks()
    w = inp["w_in"][l]
    cw = inp["d_conv_w"][l]
    maps = []
    for cidx in range(NCORES):
        b, h = cidx // 4, cidx % 4
        hc = np.arange(h * 64, (h + 1) * 64)
        tqv = np.zeros((128, 4), np.float32)
        tqv[0:64, 0:3] = cw[:, hc].T
        tqv[64:128, 0:3] = cw[:, 512 + hc].T
        tkz = np.zeros((128, 4), np.float32)
        tkz[0:64, 0:3] = cw[:, 256 + hc].T
        tkz[64:128, 1] = 1.0
        scal = np.zeros((128, 6), np.float32)
        scal[:, 0] = inp["d_dt_bias"][l][0, h]
        scal[:, 1] = inp["d_dt_bias"][l][1, h]
        scal[:, 2] = inp["d_a_log"][l][0, h]
        scal[:, 3] = inp["d_a_log"][l][1, h]
        m = {
            "xnp": xnp_b[b],
            "wqv": np.ascontiguousarray(np.concatenate([w[:, OFF_D + hc], w[:, OFF_D + 512 + hc]], 1)),
            "wkz": np.ascontiguousarray(np.concatenate([w[:, OFF_D + 256 + hc], w[:, OFF_D + 768 + hc]], 1)),
            "wab": np.ascontiguousarray(w[:, [OFF_D + 1024 + h, OFF_D + 1028 + h, OFF_D + 1032 + h, OFF_D + 1036 + h]]),
            "tapsqv": tqv, "tapskz": tkz, "scal": scal,
            "normw": np.ascontiguousarray(np.broadcast_to(inp["d_out_norm"][l][None, :], (128, 64))).astype(np.float32),
        }
        for k in ("Lf", "Rf", "Lb", "Rb", "nmf", "nmb", "ident"):
            m["m_" + k] = mk[k]
        maps.append(m)
    return maps


def _standalone(emit, **kw):
    c = Ctx()
    emit(c, **kw)
    return c.finish()


def build_P(**kw):
    return _standalone(emit_P, **kw)


def build_Ta(**kw):
    return _standalone(emit_Ta, **kw)


def build_Tb(**kw):
    return _standalone(emit_Tb, **kw)


def build_MA():
    return _standalone(emit_MA)


def build_MB():
    return _standalone(emit_MB)


def build_MC():
    return _standalone(emit_MC)


def build_MD():
    return _standalone(emit_MD)


_NC_CACHE = {}


def _nc(name):
    if name not in _NC_CACHE:
        _NC_CACHE[name] = {"P": build_P, "Ta": build_Ta, "Tb": build_Tb, "MA": build_MA, "MB": build_MB,
                           "MC": build_MC, "MD": build_MD}[name]()
    return _NC_CACHE[name]


_PERM = np.array([m * 256 + h * 64 + d for h in range(4) for m in range(4) for d in range(64)])


def kernel_unfused(**inputs):
    inp = {k: np.asarray(v) for k, v in inputs.items()}
    x = inp["x"].astype(np.float32)
    NB = x.shape[0]
    NQ = SEQ // NTOK
    XT = [np.ascontiguousarray(x[b].T) for b in range(NB)]
    cq = lambda a, q: np.ascontiguousarray(a[:, q * NTOK:(q + 1) * NTOK])
    cores = [(c_ // NQ, c_ % NQ) for c_ in range(NCORES)]
    res = run(_nc("P"), [{"xt": cq(XT[b], q), "w": pk(inp["pre_mix_norm"][0])} for b, q in cores])
    XNT = [np.concatenate([res[b * NQ + q]["xnt"] for q in range(NQ)], axis=1) for b in range(NB)]
    fcw_all = inp["f_conv_w"]
    for l in range(DEPTH):
        xnp_b = [pad_seq(XNT[b]) for b in range(NB)]
        outs = []
        for name in ("MA", "MB", "MC", "MD"):
            mfn = {"MA": maps_MA, "MB": maps_MB, "MC": maps_MC, "MD": maps_MD}[name]
            outs.append(run(_nc(name), mfn(inp, l, xnp_b)))
        OT = [np.concatenate([outs[m][b * 4 + h]["ob"] for h in range(4) for m in range(4)], axis=0) for b in range(NB)]
        onw_full = np.concatenate([inp["a_out_norm"][l], inp["b_out_norm"][l], inp["c_out_norm"][l], np.ones(256, np.float32)])
        wout_p = np.ascontiguousarray(inp["w_out"][l][_PERM])
        res = run(_nc("Ta"), [{"ot": cq(OT[b], q), "xt": cq(XT[b], q), "wout": wout_p, "onw": pk(onw_full[_PERM]),
                               "postw": pk(inp["post_mix_norm"][l]), "prew": pk(inp["pre_ffn_norm"][l])} for b, q in cores])
        XMT = [np.concatenate([res[b * NQ + q]["xmt"] for q in range(NQ)], axis=1) for b in range(NB)]
        HTP = [pad_seq(np.concatenate([res[b * NQ + q]["ht"] for q in range(NQ)], axis=1)) for b in range(NB)]
        fcw_l = np.ascontiguousarray(fcw_all[l].T.reshape(44, 128, 3).transpose(1, 0, 2))
        fcb_l = np.ascontiguousarray(inp["f_conv_b"][l].reshape(44, 128).T)
        nextw = inp["pre_mix_norm"][min(l + 1, DEPTH - 1)]
        res = run(_nc("Tb"), [{"hth": np.ascontiguousarray(HTP[b][:, q * NTOK:q * NTOK + NTOK + 2]), "xmt": cq(XMT[b], q),
                               "w1": inp["f_w_in"][l], "w2": inp["f_w_out"][l], "fcw": fcw_l, "fcb": fcb_l,
                               "postw": pk(inp["post_ffn_norm"][l]), "nextw": pk(nextw)} for b, q in cores])
        XT = [np.concatenate([res[b * NQ + q]["xo"] for q in range(NQ)], axis=1) for b in range(NB)]
        XNT = [np.concatenate([res[b * NQ + q]["xnt"] for q in range(NQ)], axis=1) for b in range(NB)]
    return np.stack([np.ascontiguousarray(XT[b].T) for b in range(NB)]).astype(np.float32)


_MIX = (("MA", None), ("MB", None), ("MC", None), ("MD", None))


def build_fused():
    emits = {"MA": emit_MA, "MB": emit_MB, "MC": emit_MC, "MD": emit_MD}
    c = Ctx()
    c.psum_banks()
    x0 = c.din("xt0", [D_MODEL, SEQ], F32)
    xout = c.dout("xout", [D_MODEL, SEQ], F32)
    XNP = c.scratch("XNP", [D_MODEL, SEQ + 2], BF16)
    HTP = c.scratch("HTP", [D_MODEL, SEQ + 2], BF16)
    OT = c.scratch("OT", [D_MODEL, SEQ], BF16)
    XMT = c.scratch("XMT", [D_MODEL, SEQ], F32)
    X1 = c.scratch("X1", [D_MODEL, SEQ], F32)
    XND = c.scratch("XND", [D_MODEL, SEQ], BF16)
    c.begin_phase("Z_", {})
    z = c.sb("z", [128, 8, 1], BF16)
    c.memset("pool", z[:, :, :], 0.0, w=["z"])
    for T in (XNP, HTP):
        for col in (0, SEQ + 1):
            c.dma("sp", T.rearrange("(k p) t -> p k t", p=128)[:, :, col:col + 1], z[:, :, :], r=["z"], slow=True)
    c.end_phase()
    c.begin_phase("P_", {"xt": x0, "xnt": XNP[:, 1:SEQ + 1]})
    emit_P(c, ncols=SEQ)
    c.end_phase()
    xin = x0
    for l in range(DEPTH):
        for m, nm in enumerate(("MA", "MB", "MC", "MD")):
            for h in range(4):
                c.begin_phase(f"L{l}{nm}{h}_", {"xnp": XNP, "ob": OT[h * 256 + m * 64:h * 256 + (m + 1) * 64, :]})
                emits[nm](c)
                c.end_phase()
        c.begin_phase(f"L{l}Ta_", {"ot": OT, "xt": xin, "xmt": XMT, "ht": HTP[:, 1:SEQ + 1]})
        emit_Ta(c, ncols=SEQ)
        c.end_phase()
        xo = xout if l == DEPTH - 1 else X1
        c.begin_phase(f"L{l}Tb_", {"hth": HTP, "xmt": XMT, "xo": xo, "xnt": XNP[:, 1:SEQ + 1] if l < DEPTH - 1 else XND})
        emit_Tb(c, ncols=SEQ)
        c.end_phase()
        xin = xo
    return c.finish()


def fused_maps(inp):
    x = inp["x"].astype(np.float32)
    mfns = {"MA": maps_MA, "MB": maps_MB, "MC": maps_MC, "MD": maps_MD}
    maps = []
    per_layer = []
    for l in range(DEPTH):
        per_layer.append({nm: fn(inp, l, [None, None]) for nm, fn in mfns.items()})
    for b in range(x.shape[0]):
        m = {"xt0": np.ascontiguousarray(x[b].T), "P_w": pk(inp["pre_mix_norm"][0])}
        for l in range(DEPTH):
            for nm in mfns:
                for h in range(4):
                    for k, v in per_layer[l][nm][b * 4 + h].items():
                        if k != "xnp":
                            m[f"L{l}{nm}{h}_{k}"] = v
            onw_full = np.concatenate([inp["a_out_norm"][l], inp["b_out_norm"][l], inp["c_out_norm"][l], np.ones(256, np.float32)])
            m[f"L{l}Ta_wout"] = np.ascontiguousarray(inp["w_out"][l][_PERM])
            m[f"L{l}Ta_onw"] = pk(onw_full[_PERM])
            m[f"L{l}Ta_postw"] = pk(inp["post_mix_norm"][l])
            m[f"L{l}Ta_prew"] = pk(inp["pre_ffn_norm"][l])
            m[f"L{l}Tb_w1"] = inp["f_w_in"][l]
            m[f"L{l}Tb_w2"] = inp["f_w_out"][l]
            m[f"L{l}Tb_fcw"] = np.ascontiguousarray(inp["f_conv_w"][l].T.reshape(44, 128, 3).transpose(1, 0, 2))
            m[f"L{l}Tb_fcb"] = np.ascontiguousarray(inp["f_conv_b"][l].reshape(44, 128).T)
            m[f"L{l}Tb_postw"] = pk(inp["post_ffn_norm"][l])
            m[f"L{l}Tb_nextw"] = pk(inp["pre_mix_norm"][min(l + 1, DEPTH - 1)])
        maps.append(m)
    return maps


def kernel(**inputs):
    inp = {k: np.asarray(v) for k, v in inputs.items()}
    if "F" not in _NC_CACHE:
        _NC_CACHE["F"] = build_fused()
    res = run(_NC_CACHE["F"], fused_maps(inp))
    return np.stack([np.ascontiguousarray(res[b]["xout"].T) for b in range(len(res))]).astype(np.float32)
```

```python
import contextlib
import numpy as np
import ml_dtypes
import concourse.bass as bass
import concourse.mybir as mybir
from concourse.bass_utils import run_bass_kernel_spmd

F32 = mybir.dt.float32
BF16 = mybir.dt.bfloat16
ALU = mybir.AluOpType
AF = mybir.ActivationFunctionType

D_MODEL = 1024
SEQ = 8192
DEPTH = 2
EPS = 1e-6
D_FF = 2816
NCORES = 8
NTOK = 2048

ENGS = ("pe", "act", "dve", "pool", "sp")
N_DMA_SEMS = 12


class Op:
    __slots__ = ("eng", "fn", "deps", "signal", "semkey", "semval", "is_dma", "dma_slot")

    def __init__(self, eng, fn, is_dma=False):
        self.eng = eng
        self.fn = fn
        self.deps = []
        self.signal = False
        self.semkey = None
        self.semval = None
        self.is_dma = is_dma
        self.dma_slot = None


class Sched:
    def __init__(self, nc):
        self.nc = nc
        self.ops = {e: [] for e in ENGS}
        self.writers = {}
        self.readers = {}
        self.dma_count = {e: 0 for e in ENGS}
        self.dma_last = {}
        self.last_op = {}

    def add(self, eng, fn, reads=(), writes=(), is_dma=False):
        op = Op(eng, fn, is_dma)
        if is_dma:
            n = self.dma_count[eng]
            self.dma_count[eng] = n + 1
            op.dma_slot = n % N_DMA_SEMS
            prev = self.dma_last.get((eng, op.dma_slot))
            if prev is not None:
                op.deps.append(prev)
            self.dma_last[(eng, op.dma_slot)] = op
        deps = op.deps
        for k in reads:
            deps.extend(self.writers.get(k, {}).values())
        for k in writes:
            deps.extend(self.writers.get(k, {}).values())
            deps.extend(self.readers.get(k, {}).values())
        tk = (eng, op.dma_slot) if is_dma else eng
        for k in reads:
            self.readers.setdefault(k, {})[tk] = op
        for k in writes:
            self.writers.setdefault(k, {})[tk] = op
            self.readers[k] = {}
        if eng == "pe" and not is_dma:
            op.deps = [d for d in deps if not (d.eng == "pe" and not d.is_dma)]
        self.ops[eng].append(op)
        self.last_op[tk] = op
        return op

    def barrier(self):
        lasts = list(self.last_op.values())
        for e in ENGS:
            if e == "sp" or self.ops[e]:
                op = Op(e, None)
                op.deps = [d for d in lasts]
                self.ops[e].append(op)
                self.last_op[e] = op
        self.writers = {}
        self.readers = {}

    def finalize(self, final_waits=()):
        nc = self.nc
        for e in ENGS:
            for op in self.ops[e]:
                for d in op.deps:
                    d.signal = True
        for op in final_waits:
            op.signal = True
        with contextlib.ExitStack() as st:
            esem = {e: st.enter_context(nc.semaphore(f"s_{e}")) for e in ENGS}
            dsem = {}
            for e in ENGS:
                for i in range(min(N_DMA_SEMS, self.dma_count[e])):
                    dsem[(e, i)] = st.enter_context(nc.semaphore(f"d_{e}{i}"))
            for e in ENGS:
                c = 0
                dc = {}
                for op in self.ops[e]:
                    if op.is_dma:
                        k = (e, op.dma_slot)
                        dc[k] = dc.get(k, 0) + 16
                        op.semkey = ("d", k)
                        op.semval = dc[k]
                    elif op.signal and op.fn is not None:
                        c += 1
                        op.semkey = ("e", e)
                        op.semval = c
                    elif op.signal:
                        c += 1
                        op.semkey = ("e", e)
                        op.semval = c
            block = st.enter_context(nc.Block())

            def semof(key):
                return esem[key[1]] if key[0] == "e" else dsem[key[1]]

            def run(e, eng):
                waited = {}
                for op in self.ops[e]:
                    need = {}
                    for d in op.deps:
                        if d.semkey is None:
                            continue
                        if waited.get(d.semkey, 0) >= d.semval:
                            continue
                        if need.get(d.semkey, 0) < d.semval:
                            need[d.semkey] = d.semval
                    for k, v in need.items():
                        eng.wait_ge(semof(k), v)
                        waited[k] = v
                    if op.fn is None:
                        if op.signal:
                            eng.sem_inc(semof(op.semkey), 1)
                        continue
                    ins = op.fn(eng)
                    if op.is_dma:
                        ins.then_inc(semof(op.semkey), 16)
                    elif op.signal:
                        ins.then_inc(semof(op.semkey), 1)
                if e == "sp":
                    for op in final_waits:
                        if waited.get(op.semkey, 0) < op.semval:
                            eng.wait_ge(semof(op.semkey), op.semval)
                            waited[op.semkey] = op.semval

            @block.sync
            def _(eng):
                run("sp", eng)

            @block.scalar
            def _(eng):
                run("act", eng)

            @block.vector
            def _(eng):
                run("dve", eng)

            @block.gpsimd
            def _(eng):
                run("pool", eng)

            @block.tensor
            def _(eng):
                run("pe", eng)


class Ctx:
    def __init__(self):
        self.nc = bass.Bass("TRN2", target_bir_lowering=False)
        self.S = Sched(self.nc)
        self.st = contextlib.ExitStack()
        self.finals = []
        self.ps_banks = None
        self._n = 0
        self.io = {}
        self.prefix = ""
        self.phase_st = None

    def begin_phase(self, prefix, io):
        self.prefix = prefix
        self.io = dict(io)
        self.phase_st = contextlib.ExitStack()

    def end_phase(self):
        self.S.barrier()
        self.phase_st.close()
        self.phase_st = None
        self.io = {}

    def din(self, name, shape, dt):
        if name in self.io:
            ap = self.io[name]
            assert list(ap.shape) == list(shape), (name, ap.shape, shape)
            return ap
        return self.nc.dram_tensor(self.prefix + name, list(shape), dt, kind="ExternalInput").ap()

    def dout(self, name, shape, dt):
        if name in self.io:
            ap = self.io[name]
            assert list(ap.shape) == list(shape), (name, ap.shape, shape)
            return ap
        return self.nc.dram_tensor(self.prefix + name, list(shape), dt, kind="ExternalOutput").ap()

    def scratch(self, name, shape, dt):
        return self.nc.dram_tensor(name, list(shape), dt, kind="Internal").ap()

    def sb(self, name, shape, dt, st=None):
        return (st or self.phase_st or self.st).enter_context(self.nc.sbuf_tensor(self.prefix + name, list(shape), dt))

    def psum_banks(self):
        if self.ps_banks is None:
            self.ps_banks = [self.st.enter_context(self.nc.psum_tensor(f"psb{i}", [128, 512], F32)) for i in range(8)]
        return self.ps_banks

    def dma(self, q, out, in_, r=(), w=(), final=False, slow=False):
        if slow:
            op = self.S.add(q, lambda e: e.dma_start(out=out, in_=in_, allow_slow_non_contiguous=True), r, w, is_dma=True)
        else:
            op = self.S.add(q, lambda e: e.dma_start(out=out, in_=in_), r, w, is_dma=True)
        if final:
            self.finals.append(op)
        return op

    def mm(self, out, lhsT, rhs, start, stop, r=(), w=()):
        return self.S.add("pe", lambda e: e.matmul(out, lhsT=lhsT, rhs=rhs, start=start, stop=stop), r, w)

    def tr(self, out, in_, ident, r=(), w=()):
        return self.S.add("pe", lambda e: e.transpose(out, in_, ident), r, w)

    def act(self, out, in_, func, r=(), w=(), scale=1.0, bias=0.0):
        return self.S.add("act", lambda e: e.activation(out=out, in_=in_, func=func, bias=bias, scale=scale), r, w)

    def ts(self, eng, out, in0, s1, s2, op0, op1, r=(), w=()):
        if s2 is None:
            return self.S.add(eng, lambda e: e.tensor_single_scalar(out=out, in_=in0, scalar=s1, op=op0), r, w)
        return self.S.add(eng, lambda e: e.tensor_scalar(out=out, in0=in0, scalar1=s1, scalar2=s2, op0=op0, op1=op1), r, w)

    def stt(self, eng, out, in0, scalar, in1, op0, op1, r=(), w=()):
        return self.S.add(eng, lambda e: e.scalar_tensor_tensor(out=out, in0=in0, scalar=scalar, in1=in1, op0=op0, op1=op1), r, w)

    def tt(self, eng, out, in0, in1, op, r=(), w=()):
        return self.S.add(eng, lambda e: e.tensor_tensor(out=out, in0=in0, in1=in1, op=op), r, w)

    def cp(self, eng, out, in_, r=(), w=()):
        if eng == "act":
            return self.S.add(eng, lambda e: e.copy(out=out, in_=in_), r, w)
        return self.S.add(eng, lambda e: e.tensor_copy(out=out, in_=in_), r, w)

    def memset(self, eng, ap, val, w=()):
        return self.S.add(eng, lambda e: e.memset(ap, val), (), w)

    def recip(self, out, in_, r=(), w=()):
        return self.S.add("dve", lambda e: e.reciprocal(out=out, in_=in_), r, w)

    def finish(self):
        self.S.finalize(final_waits=self.finals)
        self.st.close()
        return self.nc

    def uid(self, p):
        self._n += 1
        return f"{p}{self._n}"


def rstd_from_ss(c, out_sb, ss_ps, inv_d, eps, r, w, rows=slice(0, 128)):
    c.act(out_sb[rows], ss_ps[rows], AF.Ln, r=r, w=w, scale=inv_d, bias=eps)
    c.act(out_sb[rows], out_sb[rows], AF.Exp, r=w, w=w, scale=-0.5)


def pk(v):
    v = np.asarray(v, np.float32)
    return np.ascontiguousarray(v.reshape(-1, 128).T)


def bf(a):
    return np.asarray(a).astype(ml_dtypes.bfloat16)


def run(nc, in_maps):
    res = run_bass_kernel_spmd(nc, in_maps, core_ids=list(range(len(in_maps))))
    return res.results


def emit_norm_block(c, tag, src, nch, n, wt, ones_bf, ssps, sq, rstd, dst, inv_d, slot, pre_keys, eng_sq=("act", "pool")):
    ksq, krs = f"{tag}sq{slot}", f"{tag}rstd{slot}"
    for k in range(nch):
        e = eng_sq[k % len(eng_sq)]
        if e == "act":
            c.act(sq[:, k, :n], src[:, k, :n], AF.Square, r=pre_keys, w=[ksq])
        else:
            c.tt(e, sq[:, k, :n], src[:, k, :n], src[:, k, :n], ALU.mult, r=pre_keys, w=[ksq])
    for k in range(nch):
        c.mm(ssps[:, :n], ones_bf[:, :], sq[:, k, :n], k == 0, k == nch - 1, r=[ksq, "const"], w=[f"{tag}ssps{slot}"])
    rstd_from_ss(c, rstd[:, :n], ssps[:, :n], inv_d, EPS, r=[f"{tag}ssps{slot}"], w=[krs])
    return krs


def emit_P(c, ncols=NTOK):
    xt = c.din("xt", [D_MODEL, ncols], F32).rearrange("(k p) t -> p k t", p=128)
    w = c.din("w", [128, 8], F32)
    xnt = c.dout("xnt", [D_MODEL, ncols], BF16).rearrange("(k p) t -> p k t", p=128)
    ps = c.psum_banks()
    wt = c.sb("wt", [128, 8], F32)
    ones = c.sb("ones", [128, 128], BF16)
    c.memset("pool", ones[:, :], 1.0, w=["const"])
    c.dma("sp", wt[:, :], w, w=["const"])
    NB = 2
    xs = [c.sb(f"xs{i}", [128, 8, 512], F32) for i in range(NB)]
    sq = [c.sb(f"sq{i}", [128, 8, 512], BF16) for i in range(NB)]
    rs = [c.sb(f"rs{i}", [128, 512], F32) for i in range(NB)]
    xo = [c.sb(f"xo{i}", [128, 8, 512], BF16) for i in range(NB)]
    def load(b):
        c.dma("sp", xs[b % NB][:, :, :], xt[:, :, b * 512:(b + 1) * 512], w=[f"xs{b % NB}"])

    load(0)
    for b in range(ncols // 512):
        s = b % NB
        cols = slice(b * 512, (b + 1) * 512)
        if b + 1 < ncols // 512:
            load(b + 1)
        krs = emit_norm_block(c, "p", xs[s], 8, 512, wt, ones, ps[s], sq[s], rs[s], None, 1.0 / D_MODEL, s, [f"xs{s}"], eng_sq=("act",))
        for k in range(8):
            c.stt("dve", xo[s][:, k, :], xs[s][:, k, :], wt[:, k:k + 1], rs[s][:, :], ALU.mult, ALU.mult,
                  r=[f"xs{s}", krs, "const"], w=[f"xo{s}"])
        c.dma("sp", xnt[:, :, cols], xo[s][:, :, :], r=[f"xo{s}"], final=True)


def emit_Ta(c, ncols=NTOK):
    ot = c.din("ot", [D_MODEL, ncols], BF16).rearrange("(k p) t -> p k t", p=128)
    xt = c.din("xt", [D_MODEL, ncols], F32).rearrange("(k p) t -> p k t", p=128)
    wout = c.din("wout", [D_MODEL, D_MODEL], F32).rearrange("(k p) n -> p k n", p=128)
    onw_d = c.din("onw", [128, 8], F32)
    postw_d = c.din("postw", [128, 8], F32)
    prew_d = c.din("prew", [128, 8], F32)
    xmt = c.dout("xmt", [D_MODEL, ncols], F32).rearrange("(k p) t -> p k t", p=128)
    ht = c.dout("ht", [D_MODEL, ncols], BF16).rearrange("(k p) t -> p k t", p=128)
    ps = c.psum_banks()
    onw = c.sb("onw_s", [128, 8], F32)
    postw = c.sb("postw_s", [128, 8], F32)
    prew = c.sb("prew_s", [128, 8], F32)
    ones = c.sb("ones", [128, 128], BF16)
    bd2 = c.sb("bd2", [128, 128], BF16)
    top = c.sb("top", [128, 128], BF16)
    c.memset("pool", ones[:, :], 1.0, w=["const"])
    c.memset("pool", bd2[:, :], 0.0, w=["const"])
    c.memset("pool", bd2[0:64, 0:64], 1.0, w=["const"])
    c.memset("pool", bd2[64:128, 64:128], 1.0, w=["const"])
    c.memset("pool", top[:, :], 0.0, w=["const"])
    c.memset("pool", top[0:64, :], 1.0, w=["const"])
    for t, d in ((onw, onw_d), (postw, postw_d), (prew, prew_d)):
        c.dma("sp", t[:, :], d, w=["const"])
    wo = c.sb("wo", [128, 8, D_MODEL], BF16)
    stg = [c.sb(f"stg{i}", [128, 2, D_MODEL], F32) for i in range(2)]
    for i in range(4):
        s = i % 2
        c.dma("pool", stg[s][:, :, :], wout[:, 2 * i:2 * i + 2, :], w=[f"stg{s}"])
        c.cp("dve" if i % 2 else "act", wo[:, 2 * i:2 * i + 2, :], stg[s][:, :, :], r=[f"stg{s}"], w=["wo"])
    B = []
    for p in range(2):
        B.append(dict(
            ots=c.sb(f"ots{p}", [128, 8, 512], BF16), xs=c.sb(f"xs{p}", [128, 8, 512], F32), sq=c.sb(f"sq{p}", [128, 8, 512], BF16),
            on=c.sb(f"on{p}", [128, 8, 512], BF16), ys=c.sb(f"ys{p}", [128, 8, 512], F32), hs=c.sb(f"hs{p}", [128, 8, 512], BF16),
            rsAB=c.sb(f"rsAB{p}", [128, 512], F32), rsC=c.sb(f"rsC{p}", [128, 512], F32), rs2=c.sb(f"rs2{p}", [128, 512], F32),
            rs3=c.sb(f"rs3{p}", [128, 512], F32)))
    def load_ta(b):
        q = b % 2
        c.dma("sp", B[q]["ots"][:, :, :], ot[:, :, b * 512:(b + 1) * 512], w=[f"ots{q}"])
        c.dma("sp", B[q]["xs"][:, :, :], xt[:, :, b * 512:(b + 1) * 512], w=[f"xs{q}"])

    for b in range(ncols // 512):
        p = b % 2
        T = B[p]
        ots, xs, sq, on, ys, hs = T["ots"], T["xs"], T["sq"], T["on"], T["ys"], T["hs"]
        rsAB, rsC, rs2, rs3 = T["rsAB"], T["rsC"], T["rs2"], T["rs3"]
        K_ = lambda n: f"{n}{p}"
        pA, pB, pO = ps[4 * p], ps[4 * p + 1], (ps[4 * p + 2], ps[4 * p + 3])
        kA, kB, kO = f"bk{p}a", f"bk{p}b", (f"bk{p}c", f"bk{p}d")
        cols = slice(b * 512, (b + 1) * 512)
        if b == 0:
            load_ta(0)
        if b + 1 < ncols // 512:
            load_ta(b + 1)
        for k in range(8):
            c.act(sq[:, k, :], ots[:, k, :], AF.Square, r=[K_("ots")], w=[K_("sq")])
        for i, k in enumerate((0, 2, 4, 6)):
            c.mm(pA[:, :], bd2[:, :], sq[:, k, :], i == 0, i == 3, r=[K_("sq"), "const"], w=[kA])
        for i, k in enumerate((1, 3, 5, 7)):
            c.mm(pB[:, :], top[:, :], sq[:, k, :], i == 0, i == 3, r=[K_("sq"), "const"], w=[kB])
        rstd_from_ss(c, rsAB, pA, 1.0 / 256, EPS, r=[kA], w=[kA, K_("rsAB")])
        rstd_from_ss(c, rsC, pB, 1.0 / 256, EPS, r=[kB], w=[kB, K_("rsC")])
        for k in range(8):
            if k % 2 == 0:
                c.stt("dve", on[:, k, :], ots[:, k, :], onw[:, k:k + 1], rsAB[:, :], ALU.mult, ALU.mult,
                      r=[K_("ots"), K_("rsAB"), "const"], w=[K_("on")])
            else:
                c.stt("dve", on[0:64, k, :], ots[0:64, k, :], onw[0:64, k:k + 1], rsC[0:64, :], ALU.mult, ALU.mult,
                      r=[K_("ots"), K_("rsC"), "const"], w=[K_("on")])
                c.cp("dve", on[64:128, k, :], ots[64:128, k, :], r=[K_("ots")], w=[K_("on")])
        for m in range(8):
            pb, kb = pO[m % 2], kO[m % 2]
            for k in range(8):
                c.mm(pb[:, :], wo[:, k, m * 128:(m + 1) * 128], on[:, k, :], k == 0, k == 7, r=[K_("on"), "wo"], w=[kb])
            c.cp("act", ys[:, m, :], pb[:, :], r=[kb], w=[kb, K_("ys")])
            c.act(sq[:, m, :], pb[:, :], AF.Square, r=[kb], w=[kb, K_("sq")])
        for k in range(8):
            c.mm(pA[:, :], ones[:, :], sq[:, k, :], k == 0, k == 7, r=[K_("sq"), "const"], w=[kA])
        rstd_from_ss(c, rs2, pA, 1.0 / D_MODEL, EPS, r=[kA], w=[kA, K_("rs2")])
        for k in range(8):
            c.stt("dve", ys[:, k, :], ys[:, k, :], postw[:, k:k + 1], rs2[:, :], ALU.mult, ALU.mult,
                  r=[K_("ys"), K_("rs2"), "const"], w=[K_("ys")])
            c.tt("dve", xs[:, k, :], xs[:, k, :], ys[:, k, :], ALU.add, r=[K_("ys"), K_("xs")], w=[K_("xs")])
        c.dma("sp", xmt[:, :, cols], xs[:, :, :], r=[K_("xs")], final=True)
        for k in range(8):
            c.act(sq[:, k, :], xs[:, k, :], AF.Square, r=[K_("xs")], w=[K_("sq")])
        for k in range(8):
            c.mm(pB[:, :], ones[:, :], sq[:, k, :], k == 0, k == 7, r=[K_("sq"), "const"], w=[kB])
        rstd_from_ss(c, rs3, pB, 1.0 / D_MODEL, EPS, r=[kB], w=[kB, K_("rs3")])
        for k in range(8):
            c.stt("dve", hs[:, k, :], xs[:, k, :], prew[:, k:k + 1], rs3[:, :], ALU.mult, ALU.mult,
                  r=[K_("xs"), K_("rs3"), "const"], w=[K_("hs")])
        c.dma("sp", ht[:, :, cols], hs[:, :, :], r=[K_("hs")], final=True)


def emit_Tb(c, ncols=NTOK, TB=256):
    hth = c.din("hth", [D_MODEL, ncols + 2], BF16).rearrange("(k p) t -> p k t", p=128)
    xmt = c.din("xmt", [D_MODEL, ncols], F32).rearrange("(k p) t -> p k t", p=128)
    w1 = c.din("w1", [D_MODEL, 2 * D_FF], F32).rearrange("(k p) n -> p k n", p=128)
    w2 = c.din("w2", [D_FF, D_MODEL], F32).rearrange("(k p) n -> p k n", p=128)
    fcw_d = c.din("fcw", [128, 44, 3], F32)
    fcb_d = c.din("fcb", [128, 44], F32)
    postw_d = c.din("postw", [128, 8], F32)
    nextw_d = c.din("nextw", [128, 8], F32)
    xo = c.dout("xo", [D_MODEL, ncols], F32).rearrange("(k p) t -> p k t", p=128)
    xnt = c.dout("xnt", [D_MODEL, ncols], BF16).rearrange("(k p) t -> p k t", p=128)
    ps = c.psum_banks()
    fcw = c.sb("fcw_s", [128, 44, 3], F32)
    fcb = c.sb("fcb_s", [128, 44], F32)
    postw = c.sb("postw_s", [128, 8], F32)
    nextw = c.sb("nextw_s", [128, 8], F32)
    ones = c.sb("ones", [128, 128], BF16)
    c.memset("pool", ones[:, :], 1.0, w=["const"])
    for t, d in ((fcw, fcw_d), (fcb, fcb_d), (postw, postw_d), (nextw, nextw_d)):
        c.dma("sp", t[:], d, w=["const"])
    w1b = c.sb("w1b", [128, 8, 2 * D_FF], BF16)
    w2b = c.sb("w2b", [128, 22, D_MODEL], BF16)
    stg = [c.sb(f"stg{i}", [128, 1408], F32) for i in range(2)]
    n = 0
    for k in range(8):
        for c0 in range(0, 2 * D_FF, 1408):
            c1 = c0 + 1408
            s = n % 2
            c.dma("sp" if n % 2 else "pool", stg[s][:, :], w1[:, k, c0:c1], w=[f"stg{s}"])
            c.cp(("act", "dve")[n % 2], w1b[:, k, c0:c1], stg[s][:, :], r=[f"stg{s}"], w=[f"w1b{k}"])
            n += 1
    for j in range(22):
        s = n % 2
        c.dma("sp" if n % 2 else "pool", stg[s][:, :D_MODEL], w2[:, j, :], w=[f"stg{s}"])
        c.cp(("act", "dve")[n % 2], w2b[:, j, :], stg[s][:, :D_MODEL], r=[f"stg{s}"], w=["w2b"])
        n += 1
    w1keys = [f"w1b{k}" for k in range(8)]
    hb = [c.sb(f"hb{i}", [128, 8, TB + 2], BF16) for i in range(2)]
    xm = [c.sb(f"xm{i}", [128, 8, TB], F32) for i in range(2)]
    acts = c.sb("acts", [128, 22, TB], BF16)
    ag = [c.sb(f"ag{i}", [128, TB], F32) for i in range(2)]
    au = [c.sb(f"au{i}", [128, TB], F32) for i in range(2)]
    sg = [c.sb(f"sg{i}", [128, TB], F32) for i in range(2)]
    ys = c.sb("ys", [128, 8, TB], F32)
    sq = c.sb("sq", [128, 8, TB], BF16)
    rs2 = c.sb("rs2", [128, TB], F32)
    rs3 = c.sb("rs3", [128, TB], F32)
    xn = c.sb("xn", [128, 8, TB], BF16)
    nblk = ncols // TB
    pending_tail = []
    for b in range(nblk):
        s = b % 2
        c.dma("sp", hb[s][:, :, :], hth[:, :, b * TB:b * TB + TB + 2], w=[f"hb{s}"])
        c.dma("sp", xm[s][:, :, :], xmt[:, :, b * TB:(b + 1) * TB], w=[f"xm{s}"])
        for j in range(22):
            q = j % 2
            if j == 4 and pending_tail:
                pending_tail.pop(0)()
            for which, col0, acc, pb, pk_ in (("g", j * 128, ag[q], ps[2 * q], f"psg{q}"),
                                             ("u", D_FF + j * 128, au[q], ps[2 * q + 1], f"psu{q}")):
                jj = j if which == "g" else 22 + j
                for k in range(8):
                    c.mm(pb[:, :TB + 2], w1b[:, k, col0:col0 + 128], hb[s][:, k, :], k == 0, k == 7,
                         r=[f"hb{s}", f"w1b{k}"], w=[pk_])
                ak = f"a{which}{q}"
                c.act(acc[:, :], pb[:, 1:TB + 1], AF.Identity, r=[pk_, "const"], w=[ak],
                      scale=fcw[:, jj, 1:2], bias=fcb[:, jj:jj + 1])
                c.stt("dve", acc[:, :], pb[:, 0:TB], fcw[:, jj, 0:1], acc[:, :], ALU.mult, ALU.add,
                      r=[pk_, ak, "const"], w=[ak])
                c.stt("dve", acc[:, :], pb[:, 2:TB + 2], fcw[:, jj, 2:3], acc[:, :], ALU.mult, ALU.add,
                      r=[pk_, ak, "const"], w=[ak])
            c.act(sg[q][:, :], ag[q][:, :], AF.Silu, r=[f"ag{q}"], w=[f"sg{q}"])
            c.tt("dve", acts[:, j, :], sg[q][:, :], au[q][:, :], ALU.mult, r=[f"sg{q}", f"au{q}"], w=["acts"])
        for m in range(8):
            pb = ps[4 + m % 2]
            for j in range(22):
                c.mm(pb[:, :TB], w2b[:, j, m * 128:(m + 1) * 128], acts[:, j, :], j == 0, j == 21,
                     r=["acts", "w2b"], w=[f"pso{m % 2}"])
            c.cp("act", ys[:, m, :], pb[:, :TB], r=[f"pso{m % 2}"], w=["ys"])
            c.act(sq[:, m, :], pb[:, :TB], AF.Square, r=[f"pso{m % 2}"], w=["sq"])
        def tail(b=b, s=s):
            for k in range(8):
                c.mm(ps[6][:, :TB], ones[:, :], sq[:, k, :], k == 0, k == 7, r=["sq", "const"], w=["ps6"])
            rstd_from_ss(c, rs2, ps[6][:, :TB], 1.0 / D_MODEL, EPS, r=["ps6"], w=["rs2"])
            for k in range(8):
                c.stt("dve", ys[:, k, :], ys[:, k, :], postw[:, k:k + 1], rs2[:, :], ALU.mult, ALU.mult,
                      r=["ys", "rs2", "const"], w=["ys"])
                c.tt("dve", ys[:, k, :], ys[:, k, :], xm[s][:, k, :], ALU.add, r=["ys", f"xm{s}"], w=["ys"])
            c.dma("sp", xo[:, :, b * TB:(b + 1) * TB], ys[:, :, :], r=["ys"], final=True)
            for k in range(8):
                c.act(sq[:, k, :], ys[:, k, :], AF.Square, r=["ys"], w=["sq"])
            for k in range(8):
                c.mm(ps[7][:, :TB], ones[:, :], sq[:, k, :], k == 0, k == 7, r=["sq", "const"], w=["ps7"])
            rstd_from_ss(c, rs3, ps[7][:, :TB], 1.0 / D_MODEL, EPS, r=["ps7"], w=["rs3"])
            for k in range(8):
                c.stt("dve", xn[:, k, :], ys[:, k, :], nextw[:, k:k + 1], rs3[:, :], ALU.mult, ALU.mult,
                      r=["ys", "rs3", "const"], w=["xn"])
            c.dma("sp", xnt[:, :, b * TB:(b + 1) * TB], xn[:, :, :], r=["xn"], final=True)
        pending_tail.append(tail)
    while pending_tail:
        pending_tail.pop(0)()


def attention_core(c, QT, KT, VA, dk, scale, OB, ps, pT, r64, osb, onesrow):
    NQ, NK = SEQ // 512, SEQ // 128
    its = [(qb, kc) for qb in range(NQ) for kc in range(NK)]

    def s_mm(i):
        qb, kc = its[i]
        c.mm(ps[i % 3][:, :], KT[0:dk, kc * 128:(kc + 1) * 128], QT[0:dk, qb * 512:(qb + 1) * 512], True, True,
             r=["QT", "KT"], w=[f"sT{i % 3}"])

    pending = []

    def fin1(qb):
        ob = ps[4 + qb % 2]
        c.cp("act", r64[64:65, :], ob[64:65, :], r=[f"oacc{qb % 2}"], w=["r64"])
        c.recip(r64[64:65, :], r64[64:65, :], r=["r64"], w=["r64"])
        c.cp("act", osb[:, :], ob[0:64, :], r=[f"oacc{qb % 2}"], w=["osb"])

    def fin2(qb):
        c.mm(ps[6][0:64, :], onesrow[64:65, 0:64], r64[64:65, :], True, True, r=["r64", "const"], w=["bc"])
        c.tt("dve", OB[:, qb * 512:(qb + 1) * 512], osb[:, :], ps[6][0:64, :], ALU.mult, r=["osb", "bc"], w=["OB"])

    s_mm(0)
    s_mm(1)
    for i, (qb, kc) in enumerate(its):
        c.act(pT[i % 3][:, :], ps[i % 3][:, :], AF.Exp, r=[f"sT{i % 3}"], w=[f"pT{i % 3}"], scale=scale)
        if i + 2 < len(its):
            s_mm(i + 2)
        c.mm(ps[4 + qb % 2][0:65, :], VA[:, kc, :], pT[i % 3][:, :], kc == 0, kc == NK - 1,
             r=[f"pT{i % 3}", "VA"], w=[f"oacc{qb % 2}"])
        if FILLER:
            c.mm(ps[7][:, 0:FILLER], KT[0:dk, 0:128], QT[0:dk, 0:FILLER], True, True, r=["QT", "KT"], w=["filler"])
        for p in [p for p in pending if p[0] == i]:
            p[1]()
        pending = [p for p in pending if p[0] != i]
        if kc == NK - 1:
            fin1(qb)
            if i + 6 < len(its):
                pending.append((i + 6, lambda qb=qb: fin2(qb)))
            else:
                fin2(qb)


def load_cast(c, dst, src_ap, shape, tag, q="sp", eng="pool", wkey="wts"):
    stg = c.sb(c.uid(tag + "_stg"), shape, F32)
    c.dma(q, stg[:], src_ap, w=[tag + "stg"])
    c.cp(eng, dst, stg[:], r=[tag + "stg"], w=[wkey])


def emit_MB(c):
    xnp = c.din("xnp", [D_MODEL, SEQ + 2], BF16).rearrange("(k p) t -> p k t", p=128)
    wq = c.din("wq", [D_MODEL, 64], F32).rearrange("(k p) n -> p k n", p=128)
    wk = c.din("wk", [D_MODEL, 64], F32).rearrange("(k p) n -> p k n", p=128)
    wv = c.din("wv", [D_MODEL, 64], F32).rearrange("(k p) n -> p k n", p=128)
    qnw_d = c.din("qnw", [64, 1], F32)
    knw_d = c.din("knw", [64, 1], F32)
    cosT = c.din("cosT", [64, SEQ], F32)
    sinT = c.din("sinT", [64, SEQ], F32)
    rmT_d = c.din("rmT", [64, 64], F32)
    ob_d = c.dout("ob", [64, SEQ], BF16)
    ps = c.psum_banks()
    ones = c.sb("ones", [128, 128], BF16)
    onesrow = c.sb("onesrow", [128, 64], F32)
    c.memset("pool", ones[:, :], 1.0, w=["const"])
    c.memset("pool", onesrow[:, :], 1.0, w=["const"])
    qnw = c.sb("qnw_s", [64, 1], F32)
    knw = c.sb("knw_s", [64, 1], F32)
    rmT = c.sb("rmT_s", [64, 64], F32)
    for t, d in ((qnw, qnw_d), (knw, knw_d), (rmT, rmT_d)):
        c.dma("sp", t[:], d, w=["const"])
    wqb = c.sb("wqb", [128, 8, 64], BF16)
    wkb = c.sb("wkb", [128, 8, 64], BF16)
    wvb = c.sb("wvb", [128, 8, 64], BF16)
    for t, d, n in ((wqb, wq, "wq"), (wkb, wk, "wk"), (wvb, wv, "wv")):
        load_cast(c, t[:], d, [128, 8, 64], n)
    QT = c.sb("QT", [128, SEQ], BF16)
    KT = c.sb("KT", [128, SEQ], BF16)
    c.memset("pool", QT[64:128, :], 0.0, w=["QT"])
    c.memset("pool", KT[64:128, :], 0.0, w=["KT"])
    VA = c.sb("VA", [128, SEQ // 128, 65], BF16)
    OB = c.sb("OB", [64, SEQ], BF16)
    c.memset("pool", VA[:, :, 64:65], 1.0, w=["VA"])
    xnb = [c.sb(f"xnb{i}", [128, 8, 514], BF16) for i in range(2)]
    cs = [c.sb(f"cs{i}", [64, 512], F32) for i in range(2)]
    sn = [c.sb(f"sn{i}", [64, 512], F32) for i in range(2)]
    W = []
    for i, (tag, wb, nw, dst) in enumerate((("q", wqb, qnw, QT), ("k", wkb, knw, KT))):
        W.append(dict(tag=tag, wb=wb, nw=nw, dst=dst, dk=dst is QT and "QT" or "KT", pq=ps[3 * i], pss=ps[3 * i + 1], prot=ps[3 * i + 2],
                      sqb=c.sb("sqb" + tag, [64, 512], BF16), rs=c.sb("rs" + tag, [64, 512], F32), qn=c.sb("qn" + tag, [64, 512], F32),
                      t1=c.sb("t1" + tag, [64, 512], F32), t2=c.sb("t2" + tag, [64, 512], F32)))
    for tb in range(SEQ // 512):
        s = tb % 2
        cols = slice(tb * 512, (tb + 1) * 512)
        c.dma("sp", xnb[s][:, :, :], xnp[:, :, tb * 512:tb * 512 + 514], w=[f"xnb{s}"])
        c.dma("pool", cs[s][:, :], cosT[:, cols], w=[f"cs{s}"])
        c.dma("pool", sn[s][:, :], sinT[:, cols], w=[f"sn{s}"])

        def st_proj(w):
            for k in range(8):
                c.mm(w["pq"][0:64, :], w["wb"][:, k, :], xnb[s][:, k, 1:513], k == 0, k == 7, r=[f"xnb{s}", "wts"], w=["pq" + w["tag"]])

        def st_sq(w):
            c.act(w["sqb"][:, :], w["pq"][0:64, :], AF.Square, r=["pq" + w["tag"]], w=["sqb" + w["tag"]])

        def st_ss(w):
            c.mm(w["pss"][0:64, :], ones[0:64, 0:64], w["sqb"][:, :], True, True, r=["sqb" + w["tag"], "const"], w=["pss" + w["tag"]])

        def st_rs(w):
            rstd_from_ss(c, w["rs"], w["pss"], 1.0 / 64, EPS, r=["pss" + w["tag"]], w=["rs" + w["tag"]], rows=slice(0, 64))

        def st_qn(w):
            c.stt("dve", w["qn"][:, :], w["pq"][0:64, :], w["nw"][:, 0:1], w["rs"][:, :], ALU.mult, ALU.mult,
                  r=["pq" + w["tag"], "rs" + w["tag"], "const"], w=["qn" + w["tag"]])

        def st_rot(w):
            c.mm(w["prot"][0:64, :], rmT[:, :], w["qn"][:, :], True, True, r=["qn" + w["tag"], "const"], w=["prot" + w["tag"]])

        def st_mul(w):
            c.tt("dve", w["t1"][:, :], w["qn"][:, :], cs[s][:, :], ALU.mult, r=["qn" + w["tag"], f"cs{s}"], w=["t1" + w["tag"]])
            c.tt("dve", w["t2"][:, :], w["prot"][0:64, :], sn[s][:, :], ALU.mult, r=["prot" + w["tag"], f"sn{s}"], w=["t2" + w["tag"]])

        def st_add(w):
            c.tt("dve", w["dst"][0:64, cols], w["t1"][:, :], w["t2"][:, :], ALU.add, r=["t1" + w["tag"], "t2" + w["tag"]], w=[w["dk"]])

        for stp in (st_proj, st_sq, st_ss, st_rs, st_qn, st_rot, st_mul, st_add):
            for w in W:
                stp(w)
        for ci in range(4):
            pb = ps[6 + ci % 2]
            for k in range(8):
                c.mm(pb[:, 0:64], xnb[s][:, k, 1 + ci * 128:1 + (ci + 1) * 128], wvb[:, k, :], k == 0, k == 7,
                     r=[f"xnb{s}", "wts"], w=[f"pv{ci % 2}"])
            c.cp("act", VA[:, tb * 4 + ci, 0:64], pb[:, 0:64], r=[f"pv{ci % 2}"], w=["VA"])
    c.S.barrier()
    pT = [c.sb(f"pT{i}", [128, 512], BF16) for i in range(3)]
    r64 = c.sb("r64", [128, 512], F32)
    osb = c.sb("osb", [64, 512], F32)
    attention_core(c, QT, KT, VA, MB_DK, 64 ** -0.5, OB, ps, pT, r64, osb, onesrow)
    for i in range(4):
        c.dma("sp", ob_d[:, i * 2048:(i + 1) * 2048], OB[:, i * 2048:(i + 1) * 2048], r=["OB"], final=True)


def emit_MA(c):
    xnp = c.din("xnp", [D_MODEL, SEQ + 2], BF16).rearrange("(k p) t -> p k t", p=128)
    wcq = c.din("wcq", [D_MODEL, 192], F32).rearrange("(k p) n -> p k n", p=128)
    wckv = c.din("wckv", [D_MODEL, 128], F32).rearrange("(k p) n -> p k n", p=128)
    wkr = c.din("wkr", [D_MODEL, 32], F32).rearrange("(k p) n -> p k n", p=128)
    wuq0_d = c.din("wuq0", [128, 96], F32)
    wuq1_d = c.din("wuq1", [64, 96], F32)
    wuk_d = c.din("wuk", [128, 64], F32)
    wuv_d = c.din("wuv", [128, 64], F32)
    qnw_d = c.din("qnw", [128, 2], F32)
    kvnw_d = c.din("kvnw", [128, 1], F32)
    cosT = c.din("cosT", [96, SEQ], F32)
    sinT = c.din("sinT", [96, SEQ], F32)
    rmT_d = c.din("rmT", [96, 96], F32)
    ob_d = c.dout("ob", [64, SEQ], BF16)
    ps = c.psum_banks()
    ones = c.sb("ones", [128, 128], BF16)
    onesrow = c.sb("onesrow", [128, 64], F32)
    c.memset("pool", ones[:, :], 1.0, w=["const"])
    c.memset("pool", onesrow[:, :], 1.0, w=["const"])
    qnw = c.sb("qnw_s", [128, 2], F32)
    kvnw = c.sb("kvnw_s", [128, 1], F32)
    rmT = c.sb("rmT_s", [96, 96], F32)
    for t, d in ((qnw, qnw_d), (kvnw, kvnw_d), (rmT, rmT_d)):
        c.dma("sp", t[:], d, w=["const"])
    wcqb = c.sb("wcqb", [128, 8, 192], BF16)
    wckvb = c.sb("wckvb", [128, 8, 128], BF16)
    wkrp = c.sb("wkrp", [128, 8, 96], BF16)
    wkp = c.sb("wkp", [128, 96], BF16)
    wuq0 = c.sb("wuq0b", [128, 96], BF16)
    wuq1 = c.sb("wuq1b", [64, 96], BF16)
    wuv = c.sb("wuvb", [128, 64], BF16)
    c.memset("pool", wkrp[:, :, :], 0.0, w=["wts"])
    c.memset("pool", wkp[:, :], 0.0, w=["wts"])
    load_cast(c, wcqb[:], wcq, [128, 8, 192], "wcq")
    load_cast(c, wckvb[:], wckv, [128, 8, 128], "wckv")
    load_cast(c, wkrp[:, :, 64:96], wkr, [128, 8, 32], "wkr")
    load_cast(c, wkp[:, 0:64], wuk_d, [128, 64], "wuk")
    load_cast(c, wuq0[:], wuq0_d, [128, 96], "wuq0")
    load_cast(c, wuq1[:], wuq1_d, [64, 96], "wuq1")
    load_cast(c, wuv[:], wuv_d, [128, 64], "wuv")
    QT = c.sb("QT", [128, SEQ], BF16)
    KT = c.sb("KT", [128, SEQ], BF16)
    c.memset("pool", QT[64:128, :], 0.0, w=["QT"])
    c.memset("pool", KT[64:128, :], 0.0, w=["KT"])
    VA = c.sb("VA", [128, SEQ // 128, 65], BF16)
    OB = c.sb("OB", [64, SEQ], BF16)
    c.memset("pool", VA[:, :, 64:65], 1.0, w=["VA"])
    xnb = [c.sb(f"xnb{i}", [128, 8, 514], BF16) for i in range(2)]
    cs = [c.sb(f"cs{i}", [96, 512], F32) for i in range(2)]
    sn = [c.sb(f"sn{i}", [96, 512], F32) for i in range(2)]
    sq0 = c.sb("sq0", [128, 512], BF16)
    sq1 = c.sb("sq1", [64, 512], BF16)
    rs = c.sb("rs", [128, 512], F32)
    rs2 = c.sb("rs2", [128, 512], F32)
    cqn0 = c.sb("cqn0", [128, 512], BF16)
    cqn1 = c.sb("cqn1", [64, 512], BF16)
    ckvn = c.sb("ckvn", [128, 512], BF16)
    qs = c.sb("qs", [96, 512], F32)
    t1 = c.sb("t1", [96, 512], F32)
    t2 = c.sb("t2", [96, 512], F32)
    for tb in range(SEQ // 512):
        s = tb % 2
        cols = slice(tb * 512, (tb + 1) * 512)
        c.dma("sp", xnb[s][:, :, :], xnp[:, :, tb * 512:tb * 512 + 514], w=[f"xnb{s}"])
        c.dma("pool", cs[s][:, :], cosT[:, cols], w=[f"cs{s}"])
        c.dma("pool", sn[s][:, :], sinT[:, cols], w=[f"sn{s}"])
        xk = [f"xnb{s}", "wts"]
        for k in range(8):
            c.mm(ps[0][:, :], wcqb[:, k, 0:128], xnb[s][:, k, 1:513], k == 0, k == 7, r=xk, w=["pcq0"])
        for k in range(8):
            c.mm(ps[1][0:64, :], wcqb[:, k, 128:192], xnb[s][:, k, 1:513], k == 0, k == 7, r=xk, w=["pcq1"])
        for k in range(8):
            c.mm(ps[2][:, :], wckvb[:, k, :], xnb[s][:, k, 1:513], k == 0, k == 7, r=xk, w=["pckv"])
        c.act(sq0[:, :], ps[0][:, :], AF.Square, r=["pcq0"], w=["sq0"])
        c.act(sq1[:, :], ps[1][0:64, :], AF.Square, r=["pcq1"], w=["sq1"])
        c.mm(ps[3][:, :], ones[:, :], sq0[:, :], True, False, r=["sq0", "const"], w=["pss"])
        c.mm(ps[3][:, :], ones[0:64, :], sq1[:, :], False, True, r=["sq1", "const"], w=["pss"])
        rstd_from_ss(c, rs, ps[3], 1.0 / 192, EPS, r=["pss"], w=["rs"])
        c.stt("dve", cqn0[:, :], ps[0][:, :], qnw[:, 0:1], rs[:, :], ALU.mult, ALU.mult, r=["pcq0", "rs", "const"], w=["cqn0"])
        c.stt("dve", cqn1[:, :], ps[1][0:64, :], qnw[0:64, 1:2], rs[0:64, :], ALU.mult, ALU.mult, r=["pcq1", "rs", "const"], w=["cqn1"])
        c.act(sq0[:, :], ps[2][:, :], AF.Square, r=["pckv"], w=["sq0"])
        c.mm(ps[3][:, :], ones[:, :], sq0[:, :], True, True, r=["sq0", "const"], w=["pss"])
        rstd_from_ss(c, rs2, ps[3], 1.0 / 128, EPS, r=["pss"], w=["rs2"])
        c.stt("dve", ckvn[:, :], ps[2][:, :], kvnw[:, 0:1], rs2[:, :], ALU.mult, ALU.mult, r=["pckv", "rs2", "const"], w=["ckvn"])
        for which, dst, dk_ in (("q", QT, "QT"), ("k", KT, "KT")):
            if which == "q":
                c.mm(ps[4][0:96, :], wuq0[:, :], cqn0[:, :], True, False, r=["cqn0", "wts"], w=["pqk"])
                c.mm(ps[4][0:96, :], wuq1[:, :], cqn1[:, :], False, True, r=["cqn1", "wts"], w=["pqk"])
            else:
                c.mm(ps[4][0:96, :], wkp[:, :], ckvn[:, :], True, False, r=["ckvn", "wts"], w=["pqk"])
                for k in range(8):
                    c.mm(ps[4][0:96, :], wkrp[:, k, :], xnb[s][:, k, 1:513], False, k == 7, r=xk, w=["pqk"])
            c.cp("act", qs[:, :], ps[4][0:96, :], r=["pqk"], w=["qs"])
            c.mm(ps[5][0:96, :], rmT[:, :], qs[:, :], True, True, r=["qs", "const"], w=["prot"])
            c.tt("dve", t1[:, :], qs[:, :], cs[s][:, :], ALU.mult, r=["qs", f"cs{s}"], w=["t1"])
            c.tt("dve", t2[:, :], ps[5][0:96, :], sn[s][:, :], ALU.mult, r=["prot", f"sn{s}"], w=["t2"])
            c.tt("dve", dst[0:96, cols], t1[:, :], t2[:, :], ALU.add, r=["t1", "t2"], w=[dk_])
        for ci in range(4):
            pb = ps[6 + ci % 2]
            c.mm(pb[:, 0:64], ckvn[:, ci * 128:(ci + 1) * 128], wuv[:, :], True, True, r=["ckvn", "wts"], w=[f"pv{ci % 2}"])
            c.cp("act", VA[:, tb * 4 + ci, 0:64], pb[:, 0:64], r=[f"pv{ci % 2}"], w=["VA"])
    c.S.barrier()
    pT = [c.sb(f"pT{i}", [128, 512], BF16) for i in range(3)]
    r64 = c.sb("r64", [128, 512], F32)
    osb = c.sb("osb", [64, 512], F32)
    attention_core(c, QT, KT, VA, 128, 96 ** -0.5, OB, ps, pT, r64, osb, onesrow)
    for i in range(4):
        c.dma("sp", ob_d[:, i * 2048:(i + 1) * 2048], OB[:, i * 2048:(i + 1) * 2048], r=["OB"], final=True)


OFF_A, OFF_B, OFF_C, OFF_D = 0, 352, 864, 1640


def rope_tables(rot_dim):
    rows = SEQ // 64
    row = np.repeat(np.arange(rows), 64).astype(np.float32)
    col = np.tile(np.arange(64), rows).astype(np.float32)
    sec = rot_dim // 2
    inv_freq = (np.float32(10000.0) ** (-np.arange(0, sec, 2, dtype=np.float32) / np.float32(sec))).astype(np.float32)
    ang_r = row[:, None] * inv_freq
    ang_c = col[:, None] * inv_freq
    ang = np.concatenate([ang_r, ang_r, ang_c, ang_c], -1).astype(np.float32)
    return np.cos(ang).astype(np.float32), np.sin(ang).astype(np.float32)


def rot_matrix(r):
    Rm = np.zeros((r, r), np.float32)
    q = r // 4
    for a in range(2):
        for e in range(q):
            Rm[a * 2 * q + e, a * 2 * q + q + e] = -1.0
            Rm[a * 2 * q + q + e, a * 2 * q + e] = 1.0
    return Rm


_CONST = {}


def consts():
    if not _CONST:
        cb, sb_ = rope_tables(64)
        ca, sa = rope_tables(32)
        _CONST["cosB"] = np.ascontiguousarray(cb.T)
        _CONST["sinB"] = np.ascontiguousarray(sb_.T)
        cA = np.ones((96, SEQ), np.float32)
        sA = np.zeros((96, SEQ), np.float32)
        cA[64:96] = ca.T
        sA[64:96] = sa.T
        _CONST["cosA"], _CONST["sinA"] = cA, sA
        _CONST["rmTB"] = np.ascontiguousarray(rot_matrix(64).T)
        rA = np.zeros((96, 96), np.float32)
        rA[64:96, 64:96] = rot_matrix(32).T
        _CONST["rmTA"] = rA
    return _CONST


def pad_seq(xnt):
    out = np.zeros((xnt.shape[0], SEQ + 2), xnt.dtype)
    out[:, 1:-1] = xnt
    return out


def maps_MB(inp, l, xnp_b):
    K_ = consts()
    w = inp["w_in"][l]
    maps = []
    for cidx in range(NCORES):
        b, h = cidx // 4, cidx % 4
        g = h // 2
        maps.append({
            "xnp": xnp_b[b],
            "wq": np.ascontiguousarray(w[:, OFF_B + h * 64:OFF_B + (h + 1) * 64]),
            "wk": np.ascontiguousarray(w[:, OFF_B + 256 + g * 64:OFF_B + 256 + (g + 1) * 64]),
            "wv": np.ascontiguousarray(w[:, OFF_B + 384 + g * 64:OFF_B + 384 + (g + 1) * 64]),
            "qnw": np.ascontiguousarray(inp["b_q_norm"][l].reshape(64, 1)),
            "knw": np.ascontiguousarray(inp["b_k_norm"][l].reshape(64, 1)),
            "cosT": K_["cosB"], "sinT": K_["sinB"], "rmT": K_["rmTB"],
        })
    return maps


def maps_MA(inp, l, xnp_b):
    K_ = consts()
    w = inp["w_in"][l]
    wuq = inp["a_w_uq"][l]
    wukv = inp["a_w_ukv"][l]
    qn = inp["a_q_norm"][l]
    qnw = np.zeros((128, 2), np.float32)
    qnw[:, 0] = qn[0:128]
    qnw[0:64, 1] = qn[128:192]
    maps = []
    for cidx in range(NCORES):
        b, h = cidx // 4, cidx % 4
        wq_h = wuq[:, h * 96:(h + 1) * 96]
        maps.append({
            "xnp": xnp_b[b],
            "wcq": np.ascontiguousarray(w[:, OFF_A:OFF_A + 192]),
            "wckv": np.ascontiguousarray(w[:, OFF_A + 192:OFF_A + 320]),
            "wkr": np.ascontiguousarray(w[:, OFF_A + 320:OFF_A + 352]),
            "wuq0": np.ascontiguousarray(wq_h[0:128]), "wuq1": np.ascontiguousarray(wq_h[128:192]),
            "wuk": np.ascontiguousarray(wukv[:, h * 128:h * 128 + 64]),
            "wuv": np.ascontiguousarray(wukv[:, h * 128 + 64:h * 128 + 128]),
            "qnw": qnw, "kvnw": np.ascontiguousarray(inp["a_kv_norm"][l].reshape(128, 1)),
            "cosT": K_["cosA"], "sinT": K_["sinA"], "rmT": K_["rmTA"],
        })
    return maps


NEG = -30000.0
FILLER = 0
MB_DK = 128


def chunk_masks():
    i = np.arange(128)
    m = {}
    m["Lf"] = (i[:, None] > i[None, :]).astype(np.float32)
    m["Rf"] = (i[:, None] <= i[None, :]).astype(np.float32)
    m["Lb"] = (i[:, None] < i[None, :]).astype(np.float32)
    m["Rb"] = (i[:, None] >= i[None, :]).astype(np.float32)
    m["nmf"] = np.where(i[:, None] <= i[None, :], 0.0, NEG).astype(np.float32)
    m["nmb"] = np.where(i[:, None] >= i[None, :], 0.0, NEG).astype(np.float32)
    m["nmfs"] = np.where(i[:, None] < i[None, :], 0.0, NEG).astype(np.float32)
    m["nmbs"] = np.where(i[:, None] > i[None, :], 0.0, NEG).astype(np.float32)
    m["ident"] = np.eye(128, dtype=np.float32)
    return m


def emit_conv_silu(c, ps_main, ps_halo, pre, acc, taps, dst, tagk, rk, wk):
    c.cp("act", pre[:, 1:513], ps_main[:, 0:512], r=[rk[0]], w=[tagk + "pre"])
    c.cp("dve", pre[:, 0:514:513], ps_halo[:, 0:2], r=[rk[1]], w=[tagk + "pre"])
    c.act(acc[:, :], pre[:, 1:513], AF.Identity, r=[tagk + "pre", "const"], w=[tagk + "acc"], scale=taps[:, 1:2], bias=taps[:, 3:4])
    c.stt("dve", acc[:, :], pre[:, 0:512], taps[:, 0:1], acc[:, :], ALU.mult, ALU.add, r=[tagk + "pre", tagk + "acc", "const"], w=[tagk + "acc"])
    c.stt("dve", acc[:, :], pre[:, 2:514], taps[:, 2:3], acc[:, :], ALU.mult, ALU.add, r=[tagk + "pre", tagk + "acc", "const"], w=[tagk + "acc"])
    c.act(dst, acc[:, :], AF.Silu, r=[tagk + "acc"], w=wk)


def emit_MC(c):
    NCH = SEQ // 128
    xnp = c.din("xnp", [D_MODEL, SEQ + 2], BF16).rearrange("(k p) t -> p k t", p=128)
    w1 = c.din("w1", [D_MODEL, 128], F32).rearrange("(k p) n -> p k n", p=128)
    w2 = c.din("w2", [D_MODEL, 128], F32).rearrange("(k p) n -> p k n", p=128)
    wdt = c.din("wdt", [D_MODEL, 2], F32).rearrange("(k p) n -> p k n", p=128)
    taps1_d = c.din("taps1", [128, 4], F32)
    taps2_d = c.din("taps2", [128, 4], F32)
    scal_d = c.din("scal", [128, 6], F32)
    mk_d = {k: c.din("m_" + k, [128, 128], F32) for k in ("Lf", "Rf", "Lb", "Rb", "nmf", "nmb", "ident")}
    ob_d = c.dout("ob", [64, SEQ], BF16)
    ps = c.psum_banks()
    mk = {k: c.sb("mk_" + k, [128, 128], F32) for k in mk_d}
    for k in mk_d:
        c.dma("sp", mk[k][:, :], mk_d[k], w=["const"])
    taps1 = c.sb("taps1_s", [128, 4], F32)
    taps2 = c.sb("taps2_s", [128, 4], F32)
    scal = c.sb("scal_s", [128, 6], F32)
    for t, d in ((taps1, taps1_d), (taps2, taps2_d), (scal, scal_d)):
        c.dma("sp", t[:], d, w=["const"])
    ones32 = c.sb("ones32", [128, 128], F32)
    c.memset("pool", ones32[:, :], 1.0, w=["const"])
    w1b = c.sb("w1b", [128, 8, 128], BF16)
    w2b = c.sb("w2b", [128, 8, 128], BF16)
    wdtb = c.sb("wdtb", [128, 8, 2], BF16)
    load_cast(c, w1b[:], w1, [128, 8, 128], "w1")
    load_cast(c, w2b[:], w2, [128, 8, 128], "w2")
    load_cast(c, wdtb[:], wdt, [128, 8, 2], "wdt")
    F1 = c.sb("F1", [128, SEQ], F32)
    F2 = c.sb("F2", [128, SEQ], F32)
    Y = c.sb("Y", [64, SEQ], F32)
    C0 = c.sb("C0", [128, SEQ], BF16)
    F1b = c.sb("F1b", [128, SEQ], BF16)
    c.memset("pool", C0[0:64, :], 0.0, w=["C0"])
    OB = c.sb("OB", [64, SEQ], BF16)
    RAW = c.sb("RAW", [128, NCH, 2], F32)
    DT = c.sb("DT", [128, NCH, 2], F32)
    AA = c.sb("AA", [128, NCH, 2], F32)
    expa = c.sb("expa", [128, 2], F32)
    xnb = [c.sb(f"xnb{i}", [128, 8, 514], BF16) for i in range(2)]
    pre = c.sb("pre", [128, 514], F32)
    acc = c.sb("acc", [128, 512], F32)
    for tb in range(SEQ // 512):
        s = tb % 2
        cols = slice(tb * 512, (tb + 1) * 512)
        c.dma("sp", xnb[s][:, :, :], xnp[:, :, tb * 512:tb * 512 + 514], w=[f"xnb{s}"])
        xk = [f"xnb{s}", "wts"]
        for ti, (wb, taps, F) in enumerate(((w1b, taps1, F1), (w2b, taps2, F2))):
            for k in range(8):
                c.mm(ps[ti][:, :], wb[:, k, :], xnb[s][:, k, 1:513], k == 0, k == 7, r=xk, w=[f"pm{ti}"])
            for k in range(8):
                c.mm(ps[2 + ti][:, 0:2], wb[:, k, :], xnb[s][:, k, 0:514:513], k == 0, k == 7, r=xk, w=[f"ph{ti}"])
            emit_conv_silu(c, ps[ti], ps[2 + ti], pre, acc, taps, F[:, cols], "c", (f"pm{ti}", f"ph{ti}"), [f"F{ti + 1}"])
            if ti == 1:
                c.cp("pool", C0[64:128, cols], F2[64:128, cols], r=["F2"], w=["C0"])
            else:
                c.cp("pool", F1b[:, cols], F1[:, cols], r=["F1"], w=["F1b"])
        for ci in range(4):
            pb = ps[4 + ci % 2]
            for k in range(8):
                c.mm(pb[:, 0:2], xnb[s][:, k, 1 + ci * 128:1 + (ci + 1) * 128], wdtb[:, k, :], k == 0, k == 7, r=xk, w=[f"pdt{ci % 2}"])
            c.cp("act", RAW[:, tb * 4 + ci, :], pb[:, 0:2], r=[f"pdt{ci % 2}"], w=["RAW"])
    for d in range(2):
        c.act(DT[:, :, d], RAW[:, :, d], AF.Exp, r=["RAW", "const"], w=["DT"], bias=scal[:, d:d + 1])
        c.act(DT[:, :, d], DT[:, :, d], AF.Ln, r=["DT"], w=["DT"], bias=1.0)
    c.act(expa[:, :], scal[:, 2:4], AF.Exp, r=["const"], w=["expa"])
    for d in range(2):
        c.ts("dve", AA[:, :, d], DT[:, :, d], expa[:, d:d + 1], -1.0, ALU.mult, ALU.mult, r=["DT", "expa"], w=["AA"])
    c.S.barrier()
    T = {}
    for d in range(2):
        for n in ("lhsA", "abc", "E", "eacb"):
            T[(n, d)] = c.sb(f"{n}{d}", [128, 128], F32)
        for n in ("MT", "BD", "CD"):
            T[(n, d)] = c.sb(f"{n}{d}", [128, 128], BF16)
        T[("XC", d)] = c.sb(f"XC{d}", [128, 64], BF16)
        T[("STb", d)] = c.sb(f"STb{d}", [128, 64], BF16)
        c.memset("pool", T[("STb", d)][:, :], 0.0, w=[f"STb{d}"])
        T[("sm", d)] = c.sb(f"sm{d}", [128, 2], F32)
        T[("ST", d)] = c.sb(f"ST{d}", [128, 64], F32)
        c.memset("pool", T[("BD", d)][:, :], 0.0, w=[f"BD{d}"])
        c.memset("pool", T[("CD", d)][:, :], 0.0, w=[f"CD{d}"])
        c.memset("pool", T[("ST", d)][:, :], 0.0, w=[f"ST{d}"])
    tmpy = c.sb("tmpy", [64, 128], F32)
    done = set()
    order = []
    for i in range(NCH):
        order.append((0, i))
        order.append((1, NCH - 1 - i))
    for d, ch in order:
        cc = slice(ch * 128, (ch + 1) * 128)
        mL, mR, nm = (mk["Lf"], mk["Rf"], mk["nmf"]) if d == 0 else (mk["Lb"], mk["Rb"], mk["nmb"])
        a_col = AA[:, ch, d:d + 1]
        dt_col = DT[:, ch, d:d + 1]
        t = lambda n: T[(n, d)]
        k_ = lambda n: f"{n}{d}"
        bA, bB, bC, bD = ps[4 * d], ps[4 * d + 1], ps[4 * d + 2], ps[4 * d + 3]
        kA, kB, kC, kD = (f"B{4 * d + i}" for i in range(4))
        sm = t("sm")
        c.ts("dve", t("lhsA")[:, :], mL[:, :], a_col, None, ALU.mult, None, r=["AA", "const"], w=[k_("lhsA")])
        c.ts("dve", t("abc")[:, :], ones32[:, :], a_col, None, ALU.mult, None, r=["AA", "const"], w=[k_("abc")])
        c.mm(bA[:, 0:128], t("lhsA")[:, :], mR[:, :], True, True, r=[k_("lhsA"), "const"], w=[kA, k_("pseg")])
        c.mm(bA[:, 128:256], t("abc")[:, :], mR[:, :], True, True, r=[k_("abc"), "const"], w=[kA, k_("pacb")])
        c.mm(bB[:, 0:1], mL[:, :], a_col, True, True, r=["AA", "const"], w=[kB, k_("psm0")])
        c.mm(bB[:, 1:2], ones32[:, :], a_col, True, True, r=["AA", "const"], w=[kB, k_("psm1")])
        c.mm(bB[:, 128:256], F1b[:, cc], C0[:, cc], True, True, r=["F1b", "C0"], w=[kB, k_("psc")])
        c.tr(bC[:, 0:128], F1[:, cc], mk["ident"][:, :], r=["F1", "const"], w=[kC, k_("ptr")])
        c.act(sm[:, :], bB[:, 0:2], AF.Exp, r=[k_("psm0"), k_("psm1")], w=[kB, k_("sm")])
        c.tt("dve", t("E")[:, :], bA[:, 0:128], nm[:, :], ALU.add, r=[k_("pseg"), "const"], w=[kA, k_("E")])
        c.act(t("eacb")[64:128, :], bA[64:128, 128:256], AF.Exp, r=[k_("pacb")], w=[kA, k_("eacb")])
        c.act(t("E")[:, :], t("E")[:, :], AF.Exp, r=[k_("E")], w=[k_("E")])
        c.tt("dve", t("MT")[:, :], bB[:, 128:256], t("E")[:, :], ALU.mult, r=[k_("psc"), k_("E")], w=[kB, k_("MT")])
        c.ts("dve", t("XC")[:, :], bC[:, 0:64], dt_col, None, ALU.mult, None, r=[k_("ptr"), "DT"], w=[kC, k_("XC")])
        c.ts("dve", t("BD")[:, 64:128], bC[:, 64:128], sm[:, 0:1], None, ALU.mult, None, r=[k_("ptr"), k_("sm")], w=[kC, k_("BD")])
        c.tt("dve", t("CD")[64:128, :], F2[64:128, cc], t("eacb")[64:128, :], ALU.mult, r=["F2", k_("eacb")], w=[k_("CD")])
        c.mm(bD[0:64, 0:128], t("XC")[:, :], t("MT")[:, :], True, False, r=[k_("XC"), k_("MT")], w=[kD, k_("py")])
        c.mm(bD[0:64, 0:128], t("STb")[:, :], t("CD")[:, :], False, True, r=[k_("STb"), k_("CD")], w=[kD, k_("py")])
        c.mm(bD[:, 128:192], t("BD")[:, :], t("XC")[:, :], True, True, r=[k_("BD"), k_("XC")], w=[kD, k_("pst")])
        c.stt("dve", t("ST")[64:128, :], t("ST")[64:128, :], sm[64:128, 1:2], bD[64:128, 128:192], ALU.mult, ALU.add,
              r=[k_("ST"), k_("sm"), k_("pst")], w=[kD, k_("ST")])
        c.cp("act", t("STb")[64:128, :], t("ST")[64:128, :], r=[k_("ST")], w=[k_("STb")])
        if ch not in done:
            done.add(ch)
            c.cp("act", Y[:, cc], bD[0:64, 0:128], r=[k_("py")], w=[kD, "Y"])
        else:
            c.tt("dve", tmpy[:, :], Y[:, cc], bD[0:64, 0:128], ALU.add, r=["Y", k_("py")], w=[kD, "tmpy"])
            c.stt("dve", tmpy[:, :], F1[0:64, cc], scal[0:64, 4:5], tmpy[:, :], ALU.mult, ALU.add, r=["F1", "tmpy", "const"], w=["tmpy"])
            c.tt("pool", OB[:, cc], tmpy[:, :], F2[0:64, cc], ALU.mult, r=["tmpy", "F2"], w=["OB"])
    for i in range(4):
        c.dma("sp", ob_d[:, i * 2048:(i + 1) * 2048], OB[:, i * 2048:(i + 1) * 2048], r=["OB"], final=True)


def maps_MC(inp, l, xnp_b):
    mk = chunk_masks()
    w = inp["w_in"][l]
    cw = inp["c_conv_w"][l]
    cb = inp["c_conv_b"][l]
    maps = []
    for cidx in range(NCORES):
        b, h = cidx // 4, cidx % 4
        g = h // 2
        xs_c = np.arange(h * 64, (h + 1) * 64)
        B_c = 256 + np.arange(g * 64, (g + 1) * 64)
        C_c = 384 + np.arange(g * 64, (g + 1) * 64)
        ch1 = np.concatenate([xs_c, B_c])
        taps1 = np.concatenate([cw[:, ch1].T, cb[ch1][:, None]], 1).astype(np.float32)
        taps2 = np.zeros((128, 4), np.float32)
        taps2[0:64, 1] = 1.0
        taps2[64:128, 0:3] = cw[:, C_c].T
        taps2[64:128, 3] = cb[C_c]
        scal = np.zeros((128, 6), np.float32)
        scal[:, 0] = inp["c_dt_bias"][l][0, h]
        scal[:, 1] = inp["c_dt_bias"][l][1, h]
        scal[:, 2] = inp["c_a_log"][l][0, h]
        scal[:, 3] = inp["c_a_log"][l][1, h]
        scal[:, 4] = inp["c_d_skip"][l][h]
        m = {
            "xnp": xnp_b[b],
            "w1": np.ascontiguousarray(w[:, OFF_C + 256 + ch1]),
            "w2": np.ascontiguousarray(np.concatenate([w[:, OFF_C + h * 64:OFF_C + (h + 1) * 64], w[:, OFF_C + 256 + C_c]], 1)),
            "wdt": np.ascontiguousarray(w[:, [OFF_C + 768 + h, OFF_C + 772 + h]]),
            "taps1": np.ascontiguousarray(taps1), "taps2": taps2, "scal": scal,
        }
        for k in ("Lf", "Rf", "Lb", "Rb", "nmf", "nmb", "ident"):
            m["m_" + k] = mk[k]
        maps.append(m)
    return maps


def emit_MD(c):
    NCH = SEQ // 128
    xnp = c.din("xnp", [D_MODEL, SEQ + 2], BF16).rearrange("(k p) t -> p k t", p=128)
    wqv = c.din("wqv", [D_MODEL, 128], F32).rearrange("(k p) n -> p k n", p=128)
    wkz = c.din("wkz", [D_MODEL, 128], F32).rearrange("(k p) n -> p k n", p=128)
    wab = c.din("wab", [D_MODEL, 4], F32).rearrange("(k p) n -> p k n", p=128)
    tapsqv_d = c.din("tapsqv", [128, 4], F32)
    tapskz_d = c.din("tapskz", [128, 4], F32)
    scal_d = c.din("scal", [128, 6], F32)
    normw_d = c.din("normw", [128, 64], F32)
    mnames = ("Lf", "Rf", "Lb", "Rb", "nmf", "nmb", "ident")
    mk_d = {k: c.din("m_" + k, [128, 128], F32) for k in mnames}
    ob_d = c.dout("ob", [64, SEQ], BF16)
    ps = c.psum_banks()
    mk = {k: c.sb("mk_" + k, [128, 128], F32) for k in mk_d}
    for k in mk_d:
        c.dma("sp", mk[k][:, :], mk_d[k], w=["const"])
    tapsqv = c.sb("tapsqv_s", [128, 4], F32)
    tapskz = c.sb("tapskz_s", [128, 4], F32)
    scal = c.sb("scal_s", [128, 6], F32)
    normw = c.sb("normw_s", [128, 64], F32)
    for t, d in ((tapsqv, tapsqv_d), (tapskz, tapskz_d), (scal, scal_d), (normw, normw_d)):
        c.dma("sp", t[:], d, w=["const"])
    ones32 = c.sb("ones32", [128, 128], F32)
    c.memset("pool", ones32[:, :], 1.0, w=["const"])
    wqvb = c.sb("wqvb", [128, 8, 128], BF16)
    wkzb = c.sb("wkzb", [128, 8, 128], BF16)
    wabb = c.sb("wabb", [128, 8, 4], BF16)
    load_cast(c, wqvb[:], wqv, [128, 8, 128], "wqv")
    load_cast(c, wkzb[:], wkz, [128, 8, 128], "wkz")
    load_cast(c, wabb[:], wab, [128, 8, 4], "wab")
    FQV = c.sb("FQV", [128, SEQ], F32)
    FKZ = c.sb("FKZ", [128, SEQ], F32)
    OACC = c.sb("OACC", [128, NCH, 64], F32)
    ZS = c.sb("ZS", [128, NCH, 64], F32)
    OB = c.sb("OB", [64, SEQ], BF16)
    RAW = c.sb("RAW", [128, NCH, 4], F32)
    BETA = c.sb("BETA", [128, NCH, 2], F32)
    GG = c.sb("GG", [128, NCH, 2], F32)
    expa = c.sb("expa", [128, 2], F32)
    st1 = contextlib.ExitStack()
    xnb = [c.sb(f"xnb{i}", [128, 8, 514], BF16, st=st1) for i in range(2)]
    pre = c.sb("pre", [128, 514], F32, st=st1)
    acc = c.sb("acc", [128, 512], F32, st=st1)
    sq = c.sb("sq", [64, 512], F32, st=st1)
    rs = c.sb("rs", [64, 512], F32, st=st1)
    for tb in range(SEQ // 512):
        s = tb % 2
        cols = slice(tb * 512, (tb + 1) * 512)
        c.dma("sp", xnb[s][:, :, :], xnp[:, :, tb * 512:tb * 512 + 514], w=[f"xnb{s}"])
        xk = [f"xnb{s}", "wts"]
        for ti, (wb, taps, F, qscale) in enumerate(((wqvb, tapsqv, FQV, 0.125), (wkzb, tapskz, FKZ, 1.0))):
            for k in range(8):
                c.mm(ps[ti][:, :], wb[:, k, :], xnb[s][:, k, 1:513], k == 0, k == 7, r=xk, w=[f"pm{ti}"])
            for k in range(8):
                c.mm(ps[2 + ti][:, 0:2], wb[:, k, :], xnb[s][:, k, 0:514:513], k == 0, k == 7, r=xk, w=[f"ph{ti}"])
            fk = f"F{ti}"
            emit_conv_silu(c, ps[ti], ps[2 + ti], pre, acc, taps, F[:, cols], "d", (f"pm{ti}", f"ph{ti}"), [fk])
            c.tt("dve", sq[:, :], F[0:64, cols], F[0:64, cols], ALU.mult, r=[fk], w=["sq"])
            c.mm(ps[6][0:64, :], ones32[0:64, 0:64], sq[:, :], True, True, r=["sq", "const"], w=["pss"])
            c.act(rs[:, :], ps[6][0:64, :], AF.Ln, r=["pss"], w=["rs"], bias=1e-6)
            c.act(rs[:, :], rs[:, :], AF.Exp, r=["rs"], w=["rs"], scale=-0.5)
            c.stt("dve", F[0:64, cols], F[0:64, cols], qscale, rs[:, :], ALU.mult, ALU.mult, r=[fk, "rs"], w=[fk])
        for ci in range(4):
            pb = ps[4 + ci % 2]
            for k in range(8):
                c.mm(pb[:, 0:4], xnb[s][:, k, 1 + ci * 128:1 + (ci + 1) * 128], wabb[:, k, :], k == 0, k == 7, r=xk, w=[f"pab{ci % 2}"])
            c.cp("act", RAW[:, tb * 4 + ci, :], pb[:, 0:4], r=[f"pab{ci % 2}"], w=["RAW"])
    c.act(BETA[:, :, :], RAW[:, :, 0:2], AF.Exp, r=["RAW"], w=["BETA"], scale=-1.0)
    c.ts("dve", BETA[:, :, :], BETA[:, :, :], 1.0, None, ALU.add, None, r=["BETA"], w=["BETA"])
    c.recip(BETA[:, :, :], BETA[:, :, :], r=["BETA"], w=["BETA"])
    for d in range(2):
        c.act(GG[:, :, d], RAW[:, :, 2 + d], AF.Exp, r=["RAW", "const"], w=["GG"], bias=scal[:, d:d + 1])
        c.act(GG[:, :, d], GG[:, :, d], AF.Ln, r=["GG"], w=["GG"], bias=1.0)
    c.act(expa[:, :], scal[:, 2:4], AF.Exp, r=["const"], w=["expa"])
    for d in range(2):
        c.ts("dve", GG[:, :, d], GG[:, :, d], expa[:, d:d + 1], -1.0, ALU.mult, ALU.mult, r=["GG", "expa"], w=["GG"])
    c.S.barrier()
    st1.close()
    G = 8
    slots = []
    for g in range(G):
        t = {n: c.sb(f"{n}_{g}", [128, 128], F32) for n in ("E", "Es", "NT", "NN", "PTk", "Pk", "X0", "X1", "PT")}
        t["WT"] = c.sb(f"WT_{g}", [64, 128], F32)
        for n in ("VN", "KD", "OT"):
            t[n] = c.sb(f"{n}_{g}", [128, 64], F32)
        t["sm"] = c.sb(f"sm_{g}", [128, 4], F32)
        slots.append(t)
    Sst = [c.sb(f"S{d}", [64, 64], F32) for d in range(2)]
    for d in range(2):
        c.memset("pool", Sst[d][:, :], 0.0, w=[f"S{d}"])
    osum = c.sb("osum", [128, 64], F32)
    osq = c.sb("osq", [128, 64], F32)
    oss = c.sb("oss", [128, 1], F32)
    og = c.sb("og", [128, 64], F32)
    ident = mk["ident"]
    done = set()
    stepno = [0]

    def region(g, s):
        bank = (g // 4) * 4 + (s % 4)
        reg = g % 4
        return ps[bank][:, reg * 128:(reg + 1) * 128], f"B{bank}", f"r{bank}_{reg}"

    def STEP(pe_fn, cons_fn):
        s = stepno[0]
        stepno[0] += 1
        for g in range(G):
            R_, bk, rk = region(g, s)
            pe_fn(g, R_, [bk, rk])
        for g in range(G):
            R_, bk, rk = region(g, s)
            cons_fn(g, R_, rk, bk)

    for grp in range(NCH // 4):
        cds = [(0, 4 * grp + j) for j in range(4)] + [(1, NCH - 1 - 4 * grp - j) for j in range(4)]
        info = []
        for g, (d, ch) in enumerate(cds):
            mL, mR, nm, ms = (mk["Lf"], mk["Rf"], mk["nmf"], mk["Lb"]) if d == 0 else (mk["Lb"], mk["Rb"], mk["nmb"], mk["Lf"])
            info.append(dict(d=d, ch=ch, cc=slice(ch * 128, (ch + 1) * 128), mL=mL, mR=mR, nm=nm, ms=ms,
                             g_col=GG[:, ch, d:d + 1], b_col=BETA[:, ch, d:d + 1], t=slots[g],
                             k=lambda n, g=g: f"{n}_{g}"))
        for I in info:
            c.ts("dve", I["t"]["E"][:, :], I["mL"][:, :], I["g_col"], None, ALU.mult, None, r=["GG", "const"], w=[I["k"]("E")])

        def pe(g, R_, w):
            I = info[g]
            c.mm(R_, I["t"]["E"][:, :], I["mR"][:, :], True, True, r=[I["k"]("E"), "const"], w=w)

        def cons(g, R_, rk, bk):
            I = info[g]
            c.tt("dve", I["t"]["E"][:, :], R_, I["nm"][:, :], ALU.add, r=[rk, "const"], w=[bk, I["k"]("E")])
            c.act(I["t"]["E"][:, :], I["t"]["E"][:, :], AF.Exp, r=[I["k"]("E")], w=[I["k"]("E")])
            c.tt("dve", I["t"]["Es"][:, :], I["t"]["E"][:, :], I["ms"][:, :], ALU.mult, r=[I["k"]("E"), "const"], w=[I["k"]("Es")])
        STEP(pe, cons)

        def pe(g, R_, w):
            I = info[g]
            c.mm(R_[:, 0:1], I["mR"][:, :], I["g_col"], True, True, r=["GG", "const"], w=w)
            c.mm(R_[:, 1:2], I["mL"][:, :], I["g_col"], True, True, r=["GG", "const"], w=w)
            c.mm(R_[:, 2:3], ones32[:, :], I["g_col"], True, True, r=["GG", "const"], w=w)

        def cons(g, R_, rk, bk):
            I = info[g]
            c.act(I["t"]["sm"][:, 0:3], R_[:, 0:3], AF.Exp, r=[rk], w=[bk, I["k"]("sm")])
        STEP(pe, cons)

        def pe(g, R_, w):
            I = info[g]
            c.mm(R_, FKZ[0:64, I["cc"]], FKZ[0:64, I["cc"]], True, True, r=["F1"], w=w)

        def cons(g, R_, rk, bk):
            I = info[g]
            c.stt("dve", I["t"]["NT"][:, :], R_, I["b_col"], I["t"]["Es"][:, :], ALU.mult, ALU.mult,
                  r=[rk, "BETA", I["k"]("Es")], w=[bk, I["k"]("NT")])
        STEP(pe, cons)

        def pe(g, R_, w):
            I = info[g]
            c.mm(R_, FKZ[0:64, I["cc"]], FQV[0:64, I["cc"]], True, True, r=["F0", "F1"], w=w)

        def cons(g, R_, rk, bk):
            I = info[g]
            c.tt("dve", I["t"]["PT"][:, :], R_, I["t"]["E"][:, :], ALU.mult, r=[rk, I["k"]("E")], w=[bk, I["k"]("PT")])
        STEP(pe, cons)

        def pe(g, R_, w):
            I = info[g]
            c.tr(R_, I["t"]["NT"][:, :], ident[:, :], r=[I["k"]("NT"), "const"], w=w)

        def cons(g, R_, rk, bk):
            I = info[g]
            c.cp("act", I["t"]["NN"][:, :], R_, r=[rk], w=[bk, I["k"]("NN")])
        STEP(pe, cons)

        def pe(g, R_, w):
            I = info[g]
            c.tr(R_, FQV[:, I["cc"]], ident[:, :], r=["F0", "const"], w=w)

        def cons(g, R_, rk, bk):
            I = info[g]
            c.cp("act", I["t"]["X0"][:, 0:64], R_[:, 64:128], r=[rk], w=[bk, I["k"]("X0")])
        STEP(pe, cons)

        def pe(g, R_, w):
            I = info[g]
            c.tr(R_, FKZ[:, I["cc"]], ident[:, :], r=["F1", "const"], w=w)

        def cons(g, R_, rk, bk):
            I = info[g]
            sm = I["t"]["sm"]
            c.ts("dve", I["t"]["X0"][:, 64:128], R_[:, 0:64], sm[:, 0:1], None, ALU.mult, None, r=[rk, I["k"]("sm")], w=[bk, I["k"]("X0")])
            c.ts("dve", I["t"]["KD"][:, :], R_[:, 0:64], sm[:, 1:2], None, ALU.mult, None, r=[rk, I["k"]("sm")], w=[bk, I["k"]("KD")])
            if I["ch"] not in done:
                c.cp("act", ZS[:, I["ch"], :], R_[:, 64:128], r=[rk], w=[bk, "ZS"])
        STEP(pe, cons)

        def pe(g, R_, w):
            I = info[g]
            c.mm(R_, I["t"]["NT"][:, :], I["t"]["X0"][:, :], True, True, r=[I["k"]("NT"), I["k"]("X0")], w=w)

        def cons(g, R_, rk, bk):
            I = info[g]
            c.tt("dve", I["t"]["X1"][:, :], I["t"]["X0"][:, :], R_, ALU.subtract, r=[I["k"]("X0"), rk], w=[bk, I["k"]("X1")])
        STEP(pe, cons)

        cur, oth = "X1", "X0"
        prevT, prevN = "NT", "NN"
        for lv in range(6):
            newT, newN = ("PTk", "Pk") if lv % 2 == 0 else ("NT", "NN")

            def pe(g, R_, w, prevT=prevT, prevN=prevN):
                I = info[g]
                c.mm(R_, I["t"][prevN][:, :], I["t"][prevT][:, :], True, True, r=[I["k"](prevT), I["k"](prevN)], w=w)

            def cons(g, R_, rk, bk, newT=newT):
                I = info[g]
                c.cp("act", I["t"][newT][:, :], R_, r=[rk], w=[bk, I["k"](newT)])
            STEP(pe, cons)
            if lv < 5:
                def pe(g, R_, w, prevT=prevT, prevN=prevN):
                    I = info[g]
                    c.mm(R_, I["t"][prevT][:, :], I["t"][prevN][:, :], True, True, r=[I["k"](prevT), I["k"](prevN)], w=w)

                def cons(g, R_, rk, bk, newN=newN):
                    I = info[g]
                    c.cp("dve", I["t"][newN][:, :], R_, r=[rk], w=[bk, I["k"](newN)])
                STEP(pe, cons)

            def pe(g, R_, w, newT=newT, cur=cur):
                I = info[g]
                c.mm(R_, I["t"][newT][:, :], I["t"][cur][:, :], True, True, r=[I["k"](newT), I["k"](cur)], w=w)

            def cons(g, R_, rk, bk, cur=cur, oth=oth):
                I = info[g]
                c.tt("dve", I["t"][oth][:, :], I["t"][cur][:, :], R_, ALU.add, r=[I["k"](cur), rk], w=[bk, I["k"](oth)])
            STEP(pe, cons)
            cur, oth = oth, cur
            prevT, prevN = newT, newN
        UWn = oth
        for I in info:
            c.ts("dve", I["t"][UWn][:, :], I["t"][cur][:, :], I["b_col"], None, ALU.mult, None, r=[I["k"](cur), "BETA"], w=[I["k"](UWn)])

        def pe(g, R_, w):
            I = info[g]
            c.tr(R_[0:64, :], I["t"][UWn][:, 64:128], ident[:, :], r=[I["k"](UWn), "const"], w=w)

        def cons(g, R_, rk, bk):
            I = info[g]
            c.cp("act", I["t"]["WT"][:, :], R_[0:64, :], r=[rk], w=[bk, I["k"]("WT")])
        STEP(pe, cons)

        for j in range(4):
            for d in range(2):
                g = d * 4 + j
                I = info[g]
                t, k_ = I["t"], I["k"]
                ch, cc = I["ch"], I["cc"]
                bA, bB, bC = ps[4 * d], ps[4 * d + 1], ps[4 * d + 2]
                kA, kB, kC = f"B{4 * d}", f"B{4 * d + 1}", f"B{4 * d + 2}"
                S_ = Sst[d]
                sm = t["sm"]
                c.mm(bA[:, 0:64], t["WT"][:, :], S_[:, :], True, True, r=[k_("WT"), f"S{d}"], w=[kA, f"pa{d}"])
                c.mm(bA[:, 128:192], FQV[0:64, cc], S_[:, :], True, True, r=["F0", f"S{d}"], w=[kA, f"po{d}"])
                c.tt("dve", t["VN"][:, :], t[UWn][:, 0:64], bA[:, 0:64], ALU.subtract, r=[k_(UWn), f"pa{d}"], w=[kA, k_("VN")])
                c.act(t["OT"][:, :], bA[:, 128:192], AF.Identity, r=[f"po{d}", k_("sm")], w=[kA, k_("OT")], scale=sm[:, 0:1])
                c.mm(bB[:, 0:64], t["PT"][:, :], t["VN"][:, :], True, True, r=[k_("PT"), k_("VN")], w=[kB, f"po2{d}"])
                c.mm(bB[0:64, 128:192], t["KD"][:, :], t["VN"][:, :], True, True, r=[k_("KD"), k_("VN")], w=[kB, f"pS{d}"])
                c.stt("dve", S_[:, :], S_[:, :], sm[0:64, 2:3], bB[0:64, 128:192], ALU.mult, ALU.add,
                      r=[f"S{d}", k_("sm"), f"pS{d}"], w=[kB, f"S{d}"])
                if ch not in done:
                    done.add(ch)
                    c.tt("dve", OACC[:, ch, :], t["OT"][:, :], bB[:, 0:64], ALU.add, r=[k_("OT"), f"po2{d}"], w=[kB, "OACC"])
                else:
                    c.tt("dve", osum[:, :], t["OT"][:, :], bB[:, 0:64], ALU.add, r=[k_("OT"), f"po2{d}"], w=[kB, "osum"])
                    c.tt("dve", osum[:, :], osum[:, :], OACC[:, ch, :], ALU.add, r=["osum", "OACC"], w=["osum"])
                    c.tt("pool", osq[:, :], osum[:, :], osum[:, :], ALU.mult, r=["osum"], w=["osq"])
                    c.S.add("dve", lambda e: e.reduce_sum(out=oss[:, 0:1], in_=osq[:, :], axis=mybir.AxisListType.X), ["osq"], ["oss"])
                    c.act(oss[:, :], oss[:, :], AF.Ln, r=["oss"], w=["oss"], scale=1.0 / 64, bias=EPS)
                    c.act(oss[:, :], oss[:, :], AF.Exp, r=["oss"], w=["oss"], scale=-0.5)
                    c.stt("dve", og[:, :], osum[:, :], oss[:, 0:1], normw[:, :], ALU.mult, ALU.mult, r=["osum", "oss", "const"], w=["og"])
                    c.tt("pool", og[:, :], og[:, :], ZS[:, ch, :], ALU.mult, r=["og", "ZS"], w=["og"])
                    c.tr(bC[0:64, 0:128], og[:, :], ident[:, :], r=["og", "const"], w=[kC, "ptrO"])
                    c.cp("act", OB[:, cc], bC[0:64, 0:128], r=["ptrO"], w=[kC, "OB"])
    for i in range(4):
        c.dma("sp", ob_d[:, i * 2048:(i + 1) * 2048], OB[:, i * 2048:(i + 1) * 2048], r=["OB"], final=True)


def maps_MD(inp, l, xnp_b):
    mk = chunk_masks()
    w = inp["w_in"][l]
    cw = inp["d_conv_w"][l]
    maps = []
    for cidx in range(NCORES):
        b, h = cidx // 4, cidx % 4
        hc = np.arange(h * 64, (h + 1) * 64)
        tqv = np.zeros((128, 4), np.float32)
        tqv[0:64, 0:3] = cw[:, hc].T
        tqv[64:128, 0:3] = cw[:, 512 + hc].T
        tkz = np.zeros((128, 4), np.float32)
        tkz[0:64, 0:3] = cw[:, 256 + hc].T
        tkz[64:128, 1] = 1.0
        scal = np.zeros((128, 6), np.float32)
        scal[:, 0] = inp["d_dt_bias"][l][0, h]
        scal[:, 1] = inp["d_dt_bias"][l][1, h]
        scal[:, 2] = inp["d_a_log"][l][0, h]
        scal[:, 3] = inp["d_a_log"][l][1, h]
        m = {
            "xnp": xnp_b[b],
            "wqv": np.ascontiguousarray(np.concatenate([w[:, OFF_D + hc], w[:, OFF_D + 512 + hc]], 1)),
            "wkz": np.ascontiguousarray(np.concatenate([w[:, OFF_D + 256 + hc], w[:, OFF_D + 768 + hc]], 1)),
            "wab": np.ascontiguousarray(w[:, [OFF_D + 1024 + h, OFF_D + 1028 + h, OFF_D + 1032 + h, OFF_D + 1036 + h]]),
            "tapsqv": tqv, "tapskz": tkz, "scal": scal,
            "normw": np.ascontiguousarray(np.broadcast_to(inp["d_out_norm"][l][None, :], (128, 64))).astype(np.float32),
        }
        for k in ("Lf", "Rf", "Lb", "Rb", "nmf", "nmb", "ident"):
            m["m_" + k] = mk[k]
        maps.append(m)
    return maps


def _standalone(emit, **kw):
    c = Ctx()
    emit(c, **kw)
    return c.finish()


def build_P(**kw):
    return _standalone(emit_P, **kw)


def build_Ta(**kw):
    return _standalone(emit_Ta, **kw)


def build_Tb(**kw):
    return _standalone(emit_Tb, **kw)


def build_MA():
    return _standalone(emit_MA)


def build_MB():
    return _standalone(emit_MB)


def build_MC():
    return _standalone(emit_MC)


def build_MD():
    return _standalone(emit_MD)


_NC_CACHE = {}


def _nc(name):
    if name not in _NC_CACHE:
        _NC_CACHE[name] = {"P": build_P, "Ta": build_Ta, "Tb": build_Tb, "MA": build_MA, "MB": build_MB,
                           "MC": build_MC, "MD": build_MD}[name]()
    return _NC_CACHE[name]


_PERM = np.array([m * 256 + h * 64 + d for h in range(4) for m in range(4) for d in range(64)])


def kernel_unfused(**inputs):
    inp = {k: np.asarray(v) for k, v in inputs.items()}
    x = inp["x"].astype(np.float32)
    NB = x.shape[0]
    NQ = SEQ // NTOK
    XT = [np.ascontiguousarray(x[b].T) for b in range(NB)]
    cq = lambda a, q: np.ascontiguousarray(a[:, q * NTOK:(q + 1) * NTOK])
    cores = [(c_ // NQ, c_ % NQ) for c_ in range(NCORES)]
    res = run(_nc("P"), [{"xt": cq(XT[b], q), "w": pk(inp["pre_mix_norm"][0])} for b, q in cores])
    XNT = [np.concatenate([res[b * NQ + q]["xnt"] for q in range(NQ)], axis=1) for b in range(NB)]
    fcw_all = inp["f_conv_w"]
    for l in range(DEPTH):
        xnp_b = [pad_seq(XNT[b]) for b in range(NB)]
        outs = []
        for name in ("MA", "MB", "MC", "MD"):
            mfn = {"MA": maps_MA, "MB": maps_MB, "MC": maps_MC, "MD": maps_MD}[name]
            outs.append(run(_nc(name), mfn(inp, l, xnp_b)))
        OT = [np.concatenate([outs[m][b * 4 + h]["ob"] for h in range(4) for m in range(4)], axis=0) for b in range(NB)]
        onw_full = np.concatenate([inp["a_out_norm"][l], inp["b_out_norm"][l], inp["c_out_norm"][l], np.ones(256, np.float32)])
        wout_p = np.ascontiguousarray(inp["w_out"][l][_PERM])
        res = run(_nc("Ta"), [{"ot": cq(OT[b], q), "xt": cq(XT[b], q), "wout": wout_p, "onw": pk(onw_full[_PERM]),
                               "postw": pk(inp["post_mix_norm"][l]), "prew": pk(inp["pre_ffn_norm"][l])} for b, q in cores])
        XMT = [np.concatenate([res[b * NQ + q]["xmt"] for q in range(NQ)], axis=1) for b in range(NB)]
        HTP = [pad_seq(np.concatenate([res[b * NQ + q]["ht"] for q in range(NQ)], axis=1)) for b in range(NB)]
        fcw_l = np.ascontiguousarray(fcw_all[l].T.reshape(44, 128, 3).transpose(1, 0, 2))
        fcb_l = np.ascontiguousarray(inp["f_conv_b"][l].reshape(44, 128).T)
        nextw = inp["pre_mix_norm"][min(l + 1, DEPTH - 1)]
        res = run(_nc("Tb"), [{"hth": np.ascontiguousarray(HTP[b][:, q * NTOK:q * NTOK + NTOK + 2]), "xmt": cq(XMT[b], q),
                               "w1": inp["f_w_in"][l], "w2": inp["f_w_out"][l], "fcw": fcw_l, "fcb": fcb_l,
                               "postw": pk(inp["post_ffn_norm"][l]), "nextw": pk(nextw)} for b, q in cores])
        XT = [np.concatenate([res[b * NQ + q]["xo"] for q in range(NQ)], axis=1) for b in range(NB)]
        XNT = [np.concatenate([res[b * NQ + q]["xnt"] for q in range(NQ)], axis=1) for b in range(NB)]
    return np.stack([np.ascontiguousarray(XT[b].T) for b in range(NB)]).astype(np.float32)


_MIX = (("MA", None), ("MB", None), ("MC", None), ("MD", None))


def build_fused():
    emits = {"MA": emit_MA, "MB": emit_MB, "MC": emit_MC, "MD": emit_MD}
    c = Ctx()
    c.psum_banks()
    x0 = c.din("xt0", [D_MODEL, SEQ], F32)
    xout = c.dout("xout", [D_MODEL, SEQ], F32)
    XNP = c.scratch("XNP", [D_MODEL, SEQ + 2], BF16)
    HTP = c.scratch("HTP", [D_MODEL, SEQ + 2], BF16)
    OT = c.scratch("OT", [D_MODEL, SEQ], BF16)
    XMT = c.scratch("XMT", [D_MODEL, SEQ], F32)
    X1 = c.scratch("X1", [D_MODEL, SEQ], F32)
    XND = c.scratch("XND", [D_MODEL, SEQ], BF16)
    c.begin_phase("Z_", {})
    z = c.sb("z", [128, 8, 1], BF16)
    c.memset("pool", z[:, :, :], 0.0, w=["z"])
    for T in (XNP, HTP):
        for col in (0, SEQ + 1):
            c.dma("sp", T.rearrange("(k p) t -> p k t", p=128)[:, :, col:col + 1], z[:, :, :], r=["z"], slow=True)
    c.end_phase()
    c.begin_phase("P_", {"xt": x0, "xnt": XNP[:, 1:SEQ + 1]})
    emit_P(c, ncols=SEQ)
    c.end_phase()
    xin = x0
    for l in range(DEPTH):
        for m, nm in enumerate(("MA", "MB", "MC", "MD")):
            for h in range(4):
                c.begin_phase(f"L{l}{nm}{h}_", {"xnp": XNP, "ob": OT[h * 256 + m * 64:h * 256 + (m + 1) * 64, :]})
                emits[nm](c)
                c.end_phase()
        c.begin_phase(f"L{l}Ta_", {"ot": OT, "xt": xin, "xmt": XMT, "ht": HTP[:, 1:SEQ + 1]})
        emit_Ta(c, ncols=SEQ)
        c.end_phase()
        xo = xout if l == DEPTH - 1 else X1
        c.begin_phase(f"L{l}Tb_", {"hth": HTP, "xmt": XMT, "xo": xo, "xnt": XNP[:, 1:SEQ + 1] if l < DEPTH - 1 else XND})
        emit_Tb(c, ncols=SEQ)
        c.end_phase()
        xin = xo
    return c.finish()


def fused_maps(inp):
    x = inp["x"].astype(np.float32)
    mfns = {"MA": maps_MA, "MB": maps_MB, "MC": maps_MC, "MD": maps_MD}
    maps = []
    per_layer = []
    for l in range(DEPTH):
        per_layer.append({nm: fn(inp, l, [None, None]) for nm, fn in mfns.items()})
    for b in range(x.shape[0]):
        m = {"xt0": np.ascontiguousarray(x[b].T), "P_w": pk(inp["pre_mix_norm"][0])}
        for l in range(DEPTH):
            for nm in mfns:
                for h in range(4):
                    for k, v in per_layer[l][nm][b * 4 + h].items():
                        if k != "xnp":
                            m[f"L{l}{nm}{h}_{k}"] = v
            onw_full = np.concatenate([inp["a_out_norm"][l], inp["b_out_norm"][l], inp["c_out_norm"][l], np.ones(256, np.float32)])
            m[f"L{l}Ta_wout"] = np.ascontiguousarray(inp["w_out"][l][_PERM])
            m[f"L{l}Ta_onw"] = pk(onw_full[_PERM])
            m[f"L{l}Ta_postw"] = pk(inp["post_mix_norm"][l])
            m[f"L{l}Ta_prew"] = pk(inp["pre_ffn_norm"][l])
            m[f"L{l}Tb_w1"] = inp["f_w_in"][l]
            m[f"L{l}Tb_w2"] = inp["f_w_out"][l]
            m[f"L{l}Tb_fcw"] = np.ascontiguousarray(inp["f_conv_w"][l].T.reshape(44, 128, 3).transpose(1, 0, 2))
            m[f"L{l}Tb_fcb"] = np.ascontiguousarray(inp["f_conv_b"][l].reshape(44, 128).T)
            m[f"L{l}Tb_postw"] = pk(inp["post_ffn_norm"][l])
            m[f"L{l}Tb_nextw"] = pk(inp["pre_mix_norm"][min(l + 1, DEPTH - 1)])
        maps.append(m)
    return maps


def kernel(**inputs):
    inp = {k: np.asarray(v) for k, v in inputs.items()}
    if "F" not in _NC_CACHE:
        _NC_CACHE["F"] = build_fused()
    res = run(_NC_CACHE["F"], fused_maps(inp))
    return np.stack([np.ascontiguousarray(res[b]["xout"].T) for b in range(len(res))]).astype(np.float32)
```
